# Optimizing a Trainium2 kernel written in Bass

```python
import math, functools
import jax, jax.numpy as jnp
from jax import lax
import numpy as np

D_MODEL = 1024
BATCH = 8
SEQ = 2048
DEPTH = 1
DEC_BATCH = 128
DEC_SEQ = 8
PAST_LEN = 16384
PAGE_SIZE = 128

D_INNER = 2 * D_MODEL
D_POOL = D_INNER // 2
D_MLSTM = D_INNER - D_POOL
POOL_WINDOWS = (2, 4, 8, 16)
N_POOL_GROUPS = len(POOL_WINDOWS)
POOL_GROUP = D_POOL // N_POOL_GROUPS
POOL_BUF = max(POOL_WINDOWS) - 1
N_HEADS = 4
HEAD_DIM = D_MLSTM // N_HEADS
CHUNK = 64
N_META = 16
EPS = 1e-6
D_PROJ = 2 * D_POOL + 5 * D_MLSTM + 2 * N_HEADS

kernel_name = "hymba_pool_mlstm_decoder_step"


def rms_norm(x, w):
    xf = x.astype(jnp.float32)
    y = xf * lax.rsqrt(jnp.mean(xf * xf, axis=-1, keepdims=True) + EPS) * w.astype(jnp.float32)
    return y.astype(x.dtype)


def causal_multiscale_pool(u, prev, pos0):
    B, T, C = u.shape
    P = POOL_BUF
    ext = jnp.concatenate([prev.astype(jnp.float32), u.astype(jnp.float32)], axis=1)
    cs0 = jnp.concatenate([jnp.zeros((B, 1, C), jnp.float32), jnp.cumsum(ext, axis=1)], axis=1)
    end = cs0[:, P + 1:]
    pos = pos0 + jnp.arange(T)
    cur = ext[:, P:]
    outs = []
    for g, w in enumerate(POOL_WINDOWS):
        sl = slice(g * POOL_GROUP, (g + 1) * POOL_GROUP)
        start = cs0[:, P + 1 - w: P + 1 - w + T, sl]
        cnt = jnp.minimum(w, pos + 1).astype(jnp.float32)
        outs.append((end[..., sl] - start) / cnt[None, :, None] - cur[..., sl])
    return jnp.concatenate(outs, axis=-1), ext[:, -P:]


def mlstm_chunk(state, inp):
    C, n, m = state
    q, k, v, ig, lf = inp
    L = q.shape[2]
    b = jnp.cumsum(lf, axis=-1)
    causal = jnp.tril(jnp.ones((L, L), dtype=bool))
    D = jnp.where(causal, b[..., :, None] - b[..., None, :] + ig[..., None, :], -jnp.inf)
    a = b + m[..., None]
    m_t = jnp.maximum(a, jnp.max(D, axis=-1))
    W = jnp.exp(D - m_t[..., None])
    inter = jnp.exp(a - m_t)
    s = jnp.einsum('bhtd,bhsd->bhts', q, k) * W
    num = inter[..., None] * jnp.einsum('bhtd,bhde->bhte', q, C) + jnp.einsum('bhts,bhse->bhte', s, v)
    qn = inter * jnp.einsum('bhtd,bhd->bht', q, n) + jnp.sum(s, axis=-1)
    h = num / jnp.maximum(jnp.abs(qn), jnp.exp(-m_t))[..., None]
    m_new = m_t[..., -1]
    w_end = jnp.exp(b[..., -1:] - b + ig - m_new[..., None])
    decay = jnp.exp(b[..., -1] + m - m_new)
    C_new = decay[..., None, None] * C + jnp.einsum('bhs,bhsd,bhse->bhde', w_end, k, v)
    n_new = decay[..., None] * n + jnp.einsum('bhs,bhsd->bhd', w_end, k)
    return (C_new, n_new, m_new), h


def mlstm_prompt(q, k, v, ig, lf):
    B, H, T, _ = q.shape
    state = (jnp.zeros((B, H, HEAD_DIM, HEAD_DIM), jnp.float32),
             jnp.zeros((B, H, HEAD_DIM), jnp.float32),
             jnp.zeros((B, H), jnp.float32))
    state, h_meta = mlstm_chunk(state, tuple(a[:, :, :N_META] for a in (q, k, v, ig, lf)))
    nc = (T - N_META) // CHUNK

    def to_chunks(a):
        a = a[:, :, N_META:]
        return jnp.moveaxis(a.reshape(B, H, nc, CHUNK, *a.shape[3:]), 2, 0)

    state, h_c = lax.scan(mlstm_chunk, state, tuple(to_chunks(a) for a in (q, k, v, ig, lf)))
    h_c = jnp.moveaxis(h_c, 0, 2).reshape(B, H, T - N_META, HEAD_DIM)
    return state, jnp.concatenate([h_meta, h_c], axis=2)


def mlstm_sample(state, q, k, v, ig, lf):
    state = tuple(s.astype(jnp.float32) for s in state)
    return mlstm_chunk(state, (q, k, v, ig, lf))


def mixer_layer(h, pool_prev, pos0, mlstm_run, norm_w, w_in, b_if, w_pool, pool_scale, mhln_w, w_out):
    B, T, _ = h.shape
    f32 = jnp.float32
    xn = rms_norm(h, norm_w)
    proj = xn @ w_in
    cuts = np.cumsum([D_POOL, D_POOL, D_MLSTM, D_MLSTM, D_MLSTM, D_MLSTM, D_MLSTM, N_HEADS])
    u_a, z_a, q, k, v, o, z_b, i_pre, f_pre = jnp.split(proj, cuts, axis=-1)
    pooled, pool_rows = causal_multiscale_pool(u_a, pool_prev, pos0)
    mixed = jnp.einsum('btgc,gcd->btgd', pooled.reshape(B, T, N_POOL_GROUPS, POOL_GROUP),
                       w_pool.astype(f32)).reshape(B, T, D_POOL)
    y_a = mixed * pool_scale.astype(f32) * jax.nn.silu(z_a.astype(f32))
    def heads(a):
        return a.astype(f32).reshape(B, T, N_HEADS, HEAD_DIM).transpose(0, 2, 1, 3)
    qh, kh, vh = heads(q), heads(k) * (HEAD_DIM ** -0.5), heads(v)
    gb = b_if.astype(f32)
    ig = (i_pre.astype(f32) + gb[:N_HEADS]).transpose(0, 2, 1)
    lf = jax.nn.log_sigmoid(f_pre.astype(f32) + gb[N_HEADS:]).transpose(0, 2, 1)
    state, ht = mlstm_run(qh, kh, vh, ig, lf)
    mu = jnp.mean(ht, axis=-1, keepdims=True)
    var = jnp.mean(jnp.square(ht - mu), axis=-1, keepdims=True)
    hn = (ht - mu) * lax.rsqrt(var + EPS) * mhln_w.astype(f32)[None, :, None, :]
    hn = hn.transpose(0, 2, 1, 3).reshape(B, T, D_MLSTM)
    y_b = hn * jax.nn.sigmoid(o.astype(f32)) * jax.nn.silu(z_b.astype(f32))
    y = jnp.concatenate([y_a, y_b], axis=-1).astype(h.dtype) @ w_out
    return h + y, pool_rows, state


def setup_inputs(seed: int = 0) -> dict:
    key = jax.random.key(seed)
    ks = jax.random.split(key, 18)
    nrm = jax.random.normal
    G = POOL_GROUP
    b_i = 0.1 * nrm(ks[10], (DEPTH, N_HEADS))
    b_f = jnp.linspace(3.0, 6.0, N_HEADS)[None, :] + 0.1 * nrm(ks[11], (DEPTH, N_HEADS))
    return {
        "x_prompt": nrm(ks[0], (BATCH, SEQ, D_MODEL), jnp.float32),
        "x_sample": nrm(ks[1], (DEC_BATCH, DEC_SEQ, D_MODEL), jnp.float32),
        "state_pool": nrm(ks[2], (DEPTH, DEC_BATCH, POOL_BUF, D_POOL), jnp.float32),
        "state_C": 0.1 * nrm(ks[3], (DEPTH, DEC_BATCH, N_HEADS, HEAD_DIM, HEAD_DIM), jnp.float32),
        "state_n": 0.1 * nrm(ks[4], (DEPTH, DEC_BATCH, N_HEADS, HEAD_DIM), jnp.float32),
        "state_m": nrm(ks[5], (DEPTH, DEC_BATCH, N_HEADS), jnp.float32),
        "meta_tokens": nrm(ks[6], (N_META, D_MODEL), jnp.float32),
        "norm1_w": 1.0 + 0.1 * nrm(ks[7], (DEPTH, D_MODEL), jnp.float32),
        "w_in": nrm(ks[8], (DEPTH, D_MODEL, D_PROJ), jnp.float32) * D_MODEL ** -0.5,
        "b_if": jnp.concatenate([b_i, b_f], axis=-1).astype(jnp.float32),
        "w_pool": nrm(ks[12], (DEPTH, N_POOL_GROUPS, G, G), jnp.float32) * G ** -0.5,
        "pool_scale": 1.0 + 0.1 * nrm(ks[13], (DEPTH, D_POOL), jnp.float32),
        "mhln_w": 1.0 + 0.1 * nrm(ks[14], (DEPTH, N_HEADS, HEAD_DIM), jnp.float32),
        "w_out": nrm(ks[15], (DEPTH, D_INNER, D_MODEL), jnp.float32) * D_INNER ** -0.5,
        "normf_w": 1.0 + 0.1 * nrm(ks[16], (D_MODEL,), jnp.float32),
    }


def reference(x_prompt, x_sample, state_pool, state_C, state_n, state_m, meta_tokens, norm1_w, w_in,
              b_if, w_pool, pool_scale, mhln_w, w_out, normf_w):
    Bp = x_prompt.shape[0]
    meta = jnp.broadcast_to(meta_tokens.astype(x_prompt.dtype)[None], (Bp, N_META, D_MODEL))
    hp = jnp.concatenate([meta, x_prompt], axis=1)
    hs = x_sample
    pool_p, C_p, n_p, m_p = [], [], [], []
    pool_s, C_s, n_s, m_s = [], [], [], []
    for l in range(DEPTH):
        w = (norm1_w[l], w_in[l], b_if[l], w_pool[l], pool_scale[l], mhln_w[l], w_out[l])
        hp, pr, (cp, np_, mp) = mixer_layer(hp, jnp.zeros((Bp, POOL_BUF, D_POOL), hp.dtype), 0,
                                            mlstm_prompt, *w)
        run_s = functools.partial(mlstm_sample, (state_C[l], state_n[l], state_m[l]))
        hs, ps, (cs, ns, ms) = mixer_layer(hs, state_pool[l], PAST_LEN, run_s, *w)
        pool_p.append(pr); C_p.append(cp); n_p.append(np_); m_p.append(mp)
        pool_s.append(ps); C_s.append(cs); n_s.append(ns); m_s.append(ms)
    y_prompt = rms_norm(hp, normf_w)[:, N_META:]
    y_sample = rms_norm(hs, normf_w)
    return (y_prompt, y_sample,
            jnp.stack(pool_p), jnp.stack(C_p), jnp.stack(n_p), jnp.stack(m_p),
            jnp.stack(pool_s), jnp.stack(C_s), jnp.stack(n_s), jnp.stack(m_s))
```

```python
import contextlib
import numpy as np
import concourse.bass as bass
import concourse.mybir as mybir
from concourse.bass_utils import run_bass_kernel_spmd

F32 = mybir.dt.float32
BF16 = mybir.dt.bfloat16
AF = mybir.ActivationFunctionType
ALU = mybir.AluOpType

D = 1024
TP = 2064
NS = 128
TT = TP + NS
SOFF = TP
NSEQ = 16
EPS = 1e-6
DPROJ = 7176
TBLK = [(0, 512), (512, 512), (1024, 512), (1536, 512), (2048, 144)]
CHUNKS = [(0, 16)] + [(16 + 128 * c, 128) for c in range(16)] + [(SOFF, 128)]
NCH = len(CHUNKS)
NTILES = [(128 * j, min(128, TT - 128 * j)) for j in range(18)]
UW = 2096 + 23 * NSEQ
UP0 = 32
US0 = 2096

C_ID = 0
C_CM = 128
C_BD = 256
C_SEL = 384
C_BDS = 896
C_RC = 912
C_N = 976


class Buf:
    __slots__ = ("name", "w", "r", "psum")

    def __init__(self, name):
        self.name = name
        self.w = []
        self.r = []
        self.psum = name.startswith("ps") and name[2:].isdigit()


class _Rec:
    def __getattr__(self, name):
        return lambda *a, **k: (name, a, k)


_REC = _Rec()


class FW:
    ENG = ("pe", "act", "dve", "pool", "sp")

    def __init__(self, nc, sems):
        self.nc = nc
        self.free_sems = list(sems)
        self.sem = {}
        self.cnt = {}
        for e in self.ENG:
            self.sem[e] = self.free_sems.pop()
            self.cnt[e] = 0
        self.known = {e: {} for e in self.ENG}
        self.prog = {e: [] for e in self.ENG}

    def _needs(self, e, reads, writes):
        need = {}

        def add(ev):
            k, v = ev
            if k == "pe" and e == "pe":
                return
            if isinstance(k, tuple):
                v = self.cnt[k]
            if need.get(k, 0) < v:
                need[k] = v

        for b in reads:
            for ev in b.w:
                add(ev)
            if b.psum:
                for ev in b.r:
                    if ev[0] != e:
                        add(ev)
        for b in writes:
            for ev in b.w:
                add(ev)
            for ev in b.r:
                add(ev)
        out = []
        kn = self.known[e]
        for k, v in need.items():
            if kn.get(k, 0) < v:
                kn[k] = v
                out.append((k, v))
        return out

    def _commit(self, ev, reads, writes):
        for b in reads:
            b.r.append(ev)
            if len(b.r) > 16:
                d = {}
                for k, v in b.r:
                    if d.get(k, 0) < v:
                        d[k] = v
                b.r = list(d.items())
        for b in writes:
            b.w = [ev]
            b.r = []

    deferred = None

    def op(self, e, fn, reads=(), writes=(), inc=True):
        rec = fn(_REC) if callable(fn) else fn
        if self.deferred is not None:
            self.deferred.append(("op", e, rec, tuple(reads), tuple(writes), inc))
            return
        waits = self._needs(e, reads, writes)
        if inc:
            self.cnt[e] += 1
            ev = (e, self.cnt[e])
        else:
            ev = (e, self.cnt[e] + 1)
        self.prog[e].append((waits, rec, (e, 1) if inc else None))
        self._commit(ev, reads, writes)

    def replay(self, item):
        if item[0] == "op":
            _, e, rec, reads, writes, inc = item
            self.op(e, rec, reads, writes, inc)
        else:
            _, q, rec, key, reads, writes = item
            self.dma(q, rec, key, reads, writes)

    def dma(self, q, fn, key, reads=(), writes=()):
        rec = fn(_REC) if callable(fn) else fn
        if self.deferred is not None:
            self.deferred.append(("dma", q, rec, key, tuple(reads), tuple(writes)))
            return
        key = ("d", key)
        if key not in self.sem:
            self.sem[key] = self.free_sems.pop()
            self.cnt[key] = 0
        waits = self._needs(q, reads, writes)
        self.cnt[key] += 16
        ev = (key, self.cnt[key])
        self.prog[q].append((waits, rec, (key, 16)))
        self._commit(ev, reads, writes)

    def all_events(self):
        evs = []
        for k, v in self.cnt.items():
            if v > 0 and k != "sp":
                evs.append((k, v))
        return evs

    def finish(self, q="sp"):
        waits = []
        for k, v in self.cnt.items():
            if v > 0 and k != q and self.known[q].get(k, 0) < v:
                waits.append((k, v))
        self.prog[q].append((waits, None, None))

    def emit(self, block):
        def run(e):
            def body(engine):
                for waits, fn, inc in self.prog[e]:
                    for k, v in waits:
                        engine.wait_ge(self.sem[k], v)
                    if fn is None:
                        continue
                    ins = getattr(engine, fn[0])(*fn[1], **fn[2])
                    if inc is not None:
                        ins.then_inc(self.sem[inc[0]], inc[1])
            return body
        block.tensor(run("pe"))
        block.scalar(run("act"))
        block.vector(run("dve"))
        block.gpsimd(run("pool"))
        block.sync(run("sp"))


def make_consts():
    c = np.zeros((128, C_N), np.float32)
    p = np.arange(128)
    c[:, C_ID:C_ID + 128] = np.eye(128, dtype=np.float32)
    c[:, C_CM:C_CM + 128] = (p[:, None] <= p[None, :]).astype(np.float32)
    c[:, C_BD:C_BD + 128] = ((p[:, None] <= p[None, :]) & ((p[:, None] // 8) == (p[None, :] // 8))).astype(np.float32)
    for h in range(4):
        c[h, C_SEL + 128 * h:C_SEL + 128 * (h + 1)] = 1.0
    c[:, C_BDS:C_BDS + 16] = ((p[:, None] // 8) == np.arange(16)[None, :]).astype(np.float32)
    for g, w in enumerate((2, 4, 8, 16)):
        pos = np.arange(16)
        c[:, C_RC + 16 * g:C_RC + 16 * (g + 1)] = (1.0 / np.minimum(w, pos + 1)).astype(np.float32)[None, :]
    return c


def build_nc():
    nc = bass.Bass("TRN2", target_bir_lowering=False)

    def din(name, shape):
        return nc.dram_tensor(name, list(shape), F32, kind="ExternalInput").ap()

    def dout(name, shape):
        return nc.dram_tensor(name, list(shape), F32, kind="ExternalOutput").ap()

    xp = din("xp", [2048, D])
    xs = din("xs", [NS, D])
    spool = din("spool", [NSEQ, 15, D])
    sC = din("sC", [NSEQ, 4, 256, 256])
    sn = din("sn", [NSEQ * 4, 256])
    sm = din("sm", [NSEQ, 4])
    meta = din("meta", [16, D])
    norm1_w = din("norm1_w", [D])
    w_in = din("w_in", [D, DPROJ])
    b_if = din("b_if", [8])
    w_pool = din("w_pool", [4, 256, 256])
    pool_scale = din("pool_scale", [D])
    mhln_w = din("mhln_w", [D])
    w_out = din("w_out", [2048, D])
    normf_w = din("normf_w", [D])
    consts = din("consts", [128, C_N])

    y_prompt = dout("y_prompt", [2048, D])
    y_sample = dout("y_sample", [NS, D])
    pool_prompt = dout("pool_prompt", [15, D])
    C_prompt = dout("C_prompt", [4, 256, 256])
    n_prompt = dout("n_prompt", [4, 256])
    m_prompt = dout("m_prompt", [4, 1])
    pool_sample = dout("pool_sample", [NSEQ, 15, D])
    C_sample = dout("C_sample", [NSEQ, 4, 256, 256])
    n_sample = dout("n_sample", [NSEQ * 4, 256])
    m_sample = dout("m_sample", [NSEQ, 4])

    with contextlib.ExitStack() as st:
        E = st.enter_context

        def sb(name, shape, dt=F32):
            return E(nc.sbuf_tensor(name, list(shape), dt))

        xnT = sb("xnT", [128, 8, TT], BF16)
        yT = sb("yT", [128, 16, TT], BF16)
        NW = 4
        wslot = [sb(f"wslot{i}", [128, 8, 256], BF16) for i in range(NW)]
        cst = sb("cst", [128, C_N])
        ident = sb("ident", [128, 128], BF16)
        tokQ = sb("tokQ", [128, NCH, 12])
        dec_bc = sb("dec_bc", [128, 4, 33])
        n1w = sb("n1w", [128, 8])
        ps5 = sb("ps5", [128, 8])
        mh4 = sb("mh4", [128, 8])
        epsT = sb("epsT", [128, 1])
        wg = sb("wg", [128, 8, 8], BF16)
        wp = sb("wp", [128, 4, 2, 256], BF16)
        nT_f = sb("nT_f", [128, 2, 64])
        nT_b = sb("nT_b", [128, 2, 64], BF16)
        nnT = sb("nnT", [128, 2, 64])
        sn_tok = sb("sn_tok", [64, 256])
        stt = [sb(f"stt{i}", [128, 16]) for i in range(8)]
        junk = sb("junk", [128, 1024], BF16)
        Dall = sb("Dall", [4, 33])
        m0T = sb("m0T", [4, 16])
        bif = sb("bif", [4, 2])
        nbf = sb("nbf", [4, 1])
        msm = sb("msm", [4, 17])

        ARENA = 18600
        arena = sb("arena", [128, ARENA])

        class Carve:
            def __init__(self):
                self.off = 0

            def get(self, shape, dt=F32):
                n = int(np.prod(shape[1:]))
                words = n if dt == F32 else (n + 1) // 2
                a = arena[0:shape[0], self.off:self.off + words]
                self.off += words
                assert self.off <= ARENA, self.off
                if dt != F32:
                    a = a.bitcast(dt)
                if len(shape) == 3:
                    a = a.rearrange("p (a b) -> p a b", b=shape[2])
                elif len(shape) == 4:
                    a = a.rearrange("p (a b c) -> p a b c", b=shape[2], c=shape[3])
                return a

        ps = E(nc.psum_tensor("ps", [128, 8, 512], F32))
        sems = [E(nc.semaphore(f"s{i}")) for i in range(80)]
        block = E(nc.Block())
        fw = FW(nc, sems)

        B = {}

        def bf(name):
            if name not in B:
                B[name] = Buf(name)
            return B[name]

        PB = [bf(f"ps{i}") for i in range(8)]

        def phase_bufs(names):
            evs = fw.all_events()
            out = []
            for n in names:
                b = Buf(n)
                b.w = list(evs)
                B[n] = b
                out.append(b)
            return out

        def psb(bank):
            return ps[:, bank, :].bitcast(BF16)

        fw.dma("sp", lambda e: e.dma_start(out=cst[:], in_=consts[:, :]), "const", writes=[bf("cst")])
        NCD = dict(allow_slow_non_contiguous=True)
        fw.dma("sp", lambda e: e.dma_start(out=n1w[:], in_=norm1_w.rearrange("(k p) -> p k", p=128), **NCD), "const", writes=[bf("n1w")])
        fw.dma("sp", lambda e: e.dma_start(out=ps5[:], in_=pool_scale.rearrange("(k p) -> p k", p=128), **NCD), "const", writes=[bf("ps5")])
        fw.dma("sp", lambda e: e.dma_start(out=mh4[:], in_=mhln_w.rearrange("(k p) -> p k", p=128), **NCD), "const", writes=[bf("mh4")])
        fw.dma("sp", lambda e: e.dma_start(out=m0T[:], in_=sm.rearrange("j h -> h j"), **NCD), "const", writes=[bf("m0T")])
        fw.dma("sp", lambda e: e.dma_start(out=bif[:], in_=b_if.rearrange("(t h) -> h t", h=4), **NCD), "const", writes=[bf("bif")])
        fw.dma("sp", lambda e: e.dma_start(out=sn_tok[:], in_=sn[:, :]), "const", writes=[bf("sn_tok")])
        fw.dma("pool", lambda e: e.dma_start(out=wg[:], in_=w_in[:, 7168:7176].rearrange("(k p) c -> p k c", p=128)),
               "wg", writes=[bf("wg")])
        fw.dma("pool", lambda e: e.dma_start(out=wp[:], in_=w_pool.rearrange("g (i p) d -> p g i d", p=128)),
               "wp", writes=[bf("wp")])
        fw.op("dve", lambda e: e.tensor_copy(out=ident[:], in_=cst[:, C_ID:C_ID + 128]), reads=[bf("cst")], writes=[bf("ident")])
        fw.op("dve", lambda e: e.memset(epsT[:], EPS), writes=[bf("epsT")])
        fw.op("dve", lambda e: e.tensor_scalar(out=ps5[:], in0=ps5[:], scalar1=0.5, scalar2=None, op0=ALU.mult),
              reads=[bf("ps5")], writes=[bf("ps5")])
        fw.op("dve", lambda e: e.tensor_scalar(out=mh4[:], in0=mh4[:], scalar1=0.25, scalar2=None, op0=ALU.mult),
              reads=[bf("mh4")], writes=[bf("mh4")])
        fw.op("pool", lambda e: e.memset(nnT[:], 0.0), writes=[bf("nnT")])

        identf = cst[:, C_ID:C_ID + 128]

        wstate = {"i": 0}

        def load_w(col0):
            i = wstate["i"] % NW
            wstate["i"] += 1
            t = wslot[i]
            b = bf(f"wslot{i}")
            fw.dma("pool", lambda e: e.dma_start(out=t[:], in_=w_in[:, col0:col0 + 256].rearrange("(k p) c -> p k c", p=128)),
                   f"w{i}", writes=[b])
            return t, b

        def load_x_tile(j, xt, xtb, key):
            c0, n = NTILES[j]
            segs = [(0, 16, meta, 0), (16, TP, xp, 16), (SOFF, TT, xs, SOFF)]
            for (a, b_, src, base) in segs:
                lo = max(c0, a)
                hi = min(c0 + n, b_)
                if lo < hi:
                    fw.dma("sp", lambda e, lo=lo, hi=hi, src=src, base=base: e.dma_start(
                        out=xt[lo - c0:hi - c0, :], in_=src[lo - base:hi - base, :]), key, writes=[xtb])

        def rstd_from_ss(stile, stb, col_ss, col_out, scale):
            fw.op("act", lambda e: e.activation(out=stile[:, col_out:col_out + 1], in_=stile[:, col_ss:col_ss + 1],
                                                func=AF.Ln, scale=scale, bias=epsT[:, 0:1]),
                  reads=[stb, bf("epsT")], writes=[stb])
            fw.op("act", lambda e: e.activation(out=stile[:, col_out:col_out + 1], in_=stile[:, col_out:col_out + 1],
                                                func=AF.Exp, scale=-0.5), reads=[stb], writes=[stb])

        cv = Carve()
        NX1 = 8
        xts = [cv.get([128, D]) for _ in range(NX1)]
        xbs = [cv.get([128, D], BF16) for _ in range(3)]
        xtB = phase_bufs([f"xt{i}" for i in range(NX1)])
        xbB = phase_bufs(["xb0", "xb1", "xb2"])
        def p1_front(j):
            c0, n = NTILES[j]
            xt, xtb = xts[j % NX1], xtB[j % NX1]
            xb, xbb = xbs[j % 3], xbB[j % 3]
            stile, stb = stt[j % 4], bf(f"stt{j % 4}")
            load_x_tile(j, xt, xtb, f"xt{j % NX1}")
            fw.op("pool", lambda e: e.memset(stile[:], 0.0), writes=[stb])
            fw.op("act", lambda e: e.activation(out=junk[0:n, :], in_=xt[0:n, :], func=AF.Square, accum_out=stile[0:n, 0:1]),
                  reads=[xtb, stb], writes=[bf("junk"), stb])
            rstd_from_ss(stile, stb, 0, 1, 1.0 / D)
            fw.op("dve", lambda e: e.tensor_scalar(out=xb[0:n, :], in0=xt[0:n, :], scalar1=stile[0:n, 1:2], scalar2=None, op0=ALU.mult),
                  reads=[xtb, stb], writes=[xbb])
            bank = 4 + (j % 2)
            pv = psb(bank).rearrange("p (k t) -> p k t", t=128)
            for k in range(8):
                fw.op("pe", lambda e, k=k: e.transpose(out=pv[:, k, 0:n], in_=xb[0:n, k * 128:(k + 1) * 128], identity=ident[0:n, 0:n]),
                      reads=[xbb, bf("ident")], writes=[PB[bank]], inc=(k == 7))

        def p1_back(j):
            c0, n = NTILES[j]
            bank = 4 + (j % 2)
            pv = psb(bank).rearrange("p (k t) -> p k t", t=128)
            fw.op("dve", lambda e: e.tensor_tensor(
                out=xnT[:, :, c0:c0 + n], in0=pv[:, :, 0:n], in1=n1w[:, :].unsqueeze(2).to_broadcast([128, 8, n]), op=ALU.mult),
                reads=[PB[bank], bf("n1w")], writes=[bf("xnT")])

        p1_front(0)
        for j in range(18):
            if j + 1 < 18:
                p1_front(j + 1)
            p1_back(j)

        def proj_fm(wt, wb, f0, nf, tb, bank, src=None):
            c0, n = tb
            for k in range(8):
                fw.op("pe", lambda e, k=k: e.matmul(out=ps[0:nf, bank, 0:n], lhsT=wt[:, k, f0:f0 + nf], rhs=xnT[:, k, c0:c0 + n],
                                                    start=(k == 0), stop=(k == 7)),
                      reads=[wb, bf("xnT")], writes=[PB[bank]], inc=(k == 7))

        def ytile(np_, blk):
            return yT[0:np_, blk:blk + 2, :].rearrange("p a t -> p (a t)").bitcast(F32)

        G_ig = ytile(4, 8)
        G_sp = ytile(4, 10)
        G_P = ytile(4, 12)
        G_gg = ytile(4, 14)
        G_Mx = ytile(4, 0)
        Qt = ytile(96, 10)
        G_t = G_ig
        for nm_ in ("G_ig", "G_sp", "G_P", "G_gg", "G_Mx"):
            B[nm_] = Buf(nm_)
        B["Qt"] = B["G_sp"]
        B["G_t"] = B["G_ig"]
        fw.deferred = []
        fw.op("dve", lambda e: e.tensor_scalar(out=nbf[:], in0=bif[:, 1:2], scalar1=-1.0, scalar2=None, op0=ALU.mult),
              reads=[bf("bif")], writes=[bf("nbf")])
        for ti, tb in enumerate(TBLK):
            c0, n = tb
            b0, b1 = (ti % 2) * 2, (ti % 2) * 2 + 1
            proj_fm(wg, bf("wg"), 0, 4, tb, b0)
            proj_fm(wg, bf("wg"), 4, 4, tb, b1)
            fw.op("dve", lambda e, b0=b0, c0=c0, n=n: e.tensor_scalar(out=G_ig[:, c0:c0 + n], in0=ps[0:4, b0, 0:n], scalar1=bif[:, 0:1],
                                                                      scalar2=None, op0=ALU.add),
                  reads=[PB[b0], bf("bif")], writes=[bf("G_ig")])
            fw.op("act", lambda e, b1=b1, c0=c0, n=n: e.activation(out=G_sp[:, c0:c0 + n], in_=ps[0:4, b1, 0:n], func=AF.Exp, scale=-1.0,
                                                                  bias=nbf[:, 0:1]),
                  reads=[PB[b1], bf("nbf")], writes=[bf("G_sp")])
        fw.op("act", lambda e: e.activation(out=G_sp[:], in_=G_sp[:], func=AF.Ln, bias=1.0), reads=[bf("G_sp")], writes=[bf("G_sp")])
        fw.op("dve", lambda e: e.tensor_tensor_scan(out=G_P[:, 0:TP], data0=G_sp[:, 0:TP], data1=G_sp[:, 0:TP], initial=0.0,
                                                    op0=ALU.add, op1=ALU.max),
              reads=[bf("G_sp")], writes=[bf("G_P")])
        for j in range(NSEQ):
            a = SOFF + 8 * j
            fw.op("dve", lambda e, a=a: e.tensor_tensor_scan(out=G_P[:, a:a + 8], data0=G_sp[:, a:a + 8], data1=G_sp[:, a:a + 8],
                                                             initial=0.0, op0=ALU.add, op1=ALU.max),
                  reads=[bf("G_sp")], writes=[bf("G_P")])
        fw.op("dve", lambda e: e.tensor_tensor(out=G_gg[:], in0=G_ig[:], in1=G_P[:], op=ALU.add),
              reads=[bf("G_ig"), bf("G_P")], writes=[bf("G_gg")])
        fw.op("dve", lambda e: e.tensor_tensor_scan(out=G_Mx[:, 0:TP], data0=G_gg[:, 0:TP], data1=G_gg[:, 0:TP], initial=0.0,
                                                    op0=ALU.max, op1=ALU.max),
              reads=[bf("G_gg")], writes=[bf("G_Mx")])
        for j in range(NSEQ):
            a = SOFF + 8 * j
            fw.op("dve", lambda e, a=a, j=j: e.tensor_tensor_scan(out=G_Mx[:, a:a + 8], data0=G_gg[:, a:a + 8], data1=G_gg[:, a:a + 8],
                                                                  initial=m0T[:, j:j + 1], op0=ALU.max, op1=ALU.max),
                  reads=[bf("G_gg"), bf("m0T")], writes=[bf("G_Mx")])

        def real3(t):
            return t[:, 16:TP].rearrange("p (c l) -> p c l", l=128)

        def samp3(t):
            return t[:, SOFF:TT].rearrange("p (c l) -> p c l", l=8)

        Rprev_real = G_Mx[:, 15:1936:128].unsqueeze(2).to_broadcast([4, 16, 128])
        Rend_real = G_Mx[:, 143:TP:128].unsqueeze(2).to_broadcast([4, 16, 128])
        Rprev_s = m0T[:, :].unsqueeze(2).to_broadcast([4, 16, 8])
        Rend_s = G_Mx[:, SOFF + 7:TT:8].unsqueeze(2).to_broadcast([4, 16, 8])
        Rend_meta = G_Mx[:, 15:16].to_broadcast([4, 16])

        def qrow(src, kind, row0, escale=1.0):
            rd = [bf("G_gg"), bf("G_P"), bf("G_Mx"), bf("m0T")]
            if kind == "prev":
                fw.op("dve", lambda e: e.tensor_copy(out=G_t[:, 0:16], in_=src[:, 0:16]), reads=rd, writes=[bf("G_t")])
                fw.op("dve", lambda e: e.tensor_tensor(out=real3(G_t), in0=real3(src), in1=Rprev_real, op=ALU.subtract), reads=rd, writes=[bf("G_t")])
                fw.op("dve", lambda e: e.tensor_tensor(out=samp3(G_t), in0=samp3(src), in1=Rprev_s, op=ALU.subtract), reads=rd, writes=[bf("G_t")])
            else:
                fw.op("dve", lambda e: e.tensor_tensor(out=G_t[:, 0:16], in0=src[:, 0:16], in1=Rend_meta, op=ALU.subtract), reads=rd, writes=[bf("G_t")])
                fw.op("dve", lambda e: e.tensor_tensor(out=real3(G_t), in0=real3(src), in1=Rend_real, op=ALU.subtract), reads=rd, writes=[bf("G_t")])
                fw.op("dve", lambda e: e.tensor_tensor(out=samp3(G_t), in0=samp3(src), in1=Rend_s, op=ALU.subtract), reads=rd, writes=[bf("G_t")])
            fw.op("act", lambda e: e.activation(out=Qt[row0:row0 + 4, :], in_=G_t[:], func=AF.Exp, scale=escale), reads=[bf("G_t")], writes=[bf("Qt")])

        fw.op("pool", lambda e: e.memset(Qt[:], 0.0), reads=[bf("G_P")], writes=[bf("Qt")])
        qrow(G_gg, "prev", 0)
        qrow(G_P, "prev", 32, 2.0)
        qrow(G_gg, "end", 64)
        rdm = [bf("G_Mx"), bf("m0T")]
        fw.op("dve", lambda e: e.tensor_scalar(out=Dall[:, 0:1], in0=G_Mx[:, 15:16], scalar1=-1.0, scalar2=None, op0=ALU.mult),
              reads=rdm, writes=[bf("Dall")])
        fw.op("dve", lambda e: e.tensor_tensor(out=Dall[:, 1:17], in0=G_Mx[:, 15:1936:128], in1=G_Mx[:, 143:TP:128], op=ALU.subtract),
              reads=rdm, writes=[bf("Dall")])
        fw.op("dve", lambda e: e.tensor_tensor(out=Dall[:, 17:33], in0=m0T[:, :], in1=G_Mx[:, SOFF + 7:TT:8], op=ALU.subtract),
              reads=rdm, writes=[bf("Dall")])
        fw.op("act", lambda e: e.activation(out=Dall[:], in_=Dall[:], func=AF.Exp), reads=[bf("Dall")], writes=[bf("Dall")])
        fw.op("dve", lambda e: e.tensor_tensor(out=msm[:, 0:1], in0=G_Mx[:, TP - 1:TP], in1=G_P[:, TP - 1:TP], op=ALU.subtract),
              reads=[bf("G_Mx"), bf("G_P")], writes=[bf("msm")])
        fw.op("dve", lambda e: e.tensor_tensor(out=msm[:, 1:17], in0=G_Mx[:, SOFF + 7:TT:8], in1=G_P[:, SOFF + 7:TT:8], op=ALU.subtract),
              reads=[bf("G_Mx"), bf("G_P")], writes=[bf("msm")])
        fw.dma("sp", lambda e: e.dma_start(out=m_prompt[:, :], in_=msm[:, 0:1]), "smallout", reads=[bf("msm")])
        fw.dma("sp", lambda e: e.dma_start(out=m_sample.rearrange("j h -> h j"), in_=msm[:, 1:17], **NCD), "smallout", reads=[bf("msm")])
        for ct, (c0, L) in enumerate(CHUNKS):
            bank = 4 + (ct % 2)
            fw.op("pe", lambda e, c0=c0, L=L, bank=bank: e.transpose(out=ps[0:L, bank, 0:96], in_=Qt[:, c0:c0 + L], identity=identf[0:96, 0:96]),
                  reads=[bf("Qt"), bf("cst")], writes=[PB[bank]])
            fw.op("act", lambda e, ct=ct, L=L, bank=bank: e.activation(
                out=tokQ[0:L, ct, :].rearrange("p (a b) -> p a b", b=4),
                in_=ps[0:L, bank, 0:96].rearrange("p (a b) -> p a b", b=32)[:, :, 0:4], func=AF.Copy),
                  reads=[PB[bank]], writes=[bf("tokQ")])
        for h in range(4):
            bank = 6 + (h % 2)
            fw.op("pe", lambda e, h=h, bank=bank: e.matmul(out=ps[:, bank, 0:33], lhsT=cst[0:4, C_SEL + 128 * h:C_SEL + 128 * (h + 1)],
                                                           rhs=Dall[:, :], start=True, stop=True),
                  reads=[bf("Dall"), bf("cst")], writes=[PB[bank]])
            fw.op("act", lambda e, h=h, bank=bank: e.activation(out=dec_bc[:, h, :], in_=ps[:, bank, 0:33], func=AF.Copy),
                  reads=[PB[bank]], writes=[bf("dec_bc")])
        for dc in range(2):
            bank = 6 + dc
            fw.op("pe", lambda e, dc=dc, bank=bank: e.transpose(out=ps[:, bank, 0:64], in_=sn_tok[:, dc * 128:(dc + 1) * 128],
                                                                identity=identf[0:64, 0:64]),
                  reads=[bf("sn_tok"), bf("cst")], writes=[PB[bank]])
            fw.op("act", lambda e, dc=dc, bank=bank: e.activation(out=nT_f[:, dc, :], in_=ps[:, bank, 0:64], func=AF.Copy),
                  reads=[PB[bank]], writes=[bf("nT_f")])
        fw.op("dve", lambda e: e.tensor_copy(out=nT_b[:], in_=nT_f[:]), reads=[bf("nT_f")], writes=[bf("nT_b")])
        p2_items = fw.deferred
        fw.deferred = None
        p2_pending = set()
        last_pe_open = [False]

        def p2_release(n=1):
            k_ = 0
            while p2_items:
                if k_ >= n and not p2_pending and not last_pe_open[0]:
                    break
                it_ = p2_items.pop(0)
                fw.replay(it_)
                if it_[0] == "op":
                    _, e_, _rec, rd_, wr_, inc_ = it_
                    for b_ in rd_:
                        if b_.psum:
                            p2_pending.discard(b_.name)
                    for b_ in wr_:
                        if b_.psum:
                            p2_pending.add(b_.name)
                    last_pe_open[0] = (e_ == "pe" and not inc_)
                if not p2_pending and not last_pe_open[0]:
                    k_ += 1

        def p2_drain():
            p2_release(10 ** 9)
            for nm_ in ("G_ig", "G_sp", "G_P", "G_gg", "G_Mx"):
                bf("yT").r.extend(B[nm_].w + B[nm_].r)

        cv = Carve()
        u_ = [cv.get([128, UW]) for _ in range(2)]
        Aa = cv.get([128, UW])
        Ab = cv.get([128, UW])
        pooled_ = [cv.get([128, 2, TT], BF16) for _ in range(2)]
        sp_tok = cv.get([120, 2, D])
        th = cv.get([128, 512])
        szt = cv.get([128, 512], BF16)
        pp_stage = cv.get([16, 256])
        ps_stage = cv.get([128, D])
        snc = cv.get([128, 128])
        (uB0, uB1, AaB, AbB, pooledB0, pooledB1, sptB, thB, sztB, ppB, pssB, sncB) = phase_bufs(
            ["u0", "u1", "Aa", "Ab", "pooledT0", "pooledT1", "sp_tok", "th", "szt", "pp_stage", "ps_stage", "snc"])
        pooledB_ = [pooledB0, pooledB1]
        uB_ = [uB0, uB1]
        for t in range(2):
            fw.dma("sp", lambda e, t=t: e.dma_start(out=sp_tok[:, t, :], in_=spool[8 * t:8 * t + 8].rearrange("b r c -> (b r) c")),
                   "sptok", writes=[sptB])
        for i_ in range(2):
            fw.op("pool", lambda e, i_=i_: e.memset(u_[i_][:, 0:UP0], 0.0), writes=[uB_[i_]])
        fw.op("pool", lambda e: e.memset(Aa[:, 0:16], 0.0), writes=[AaB])
        fw.op("pool", lambda e: e.memset(Ab[:, 0:16], 0.0), writes=[AbB])
        fw.dma("sp", lambda e: e.dma_start(out=pool_sample[:, 0:7, :], in_=spool[:, 8:15, :]), "smallout")

        def snew(t):
            return bass.AP(t.tensor, t.offset + US0 + 15, [list(t.ap[0]), [23, 16], [1, 8]])

        def sprev(t, half):
            return bass.AP(t.tensor, t.offset + US0 + 23 * 8 * half, [list(t.ap[0]), [23, 8], [1, 15]])

        rot = {"b": 0}

        def nbank():
            b = rot["b"] % 4
            rot["b"] += 1
            return b

        tmp16 = cv.get([128, 16])
        (t16B,) = phase_bufs(["tmp16"])
        wts = {}

        def stage_A(g, ib):
            cb = 2 * g + ib
            u, uB = u_[ib], uB_[ib]
            su, sub = wts[g][0], wts[g][1]
            for t in range(2):
                bank = 4 + t
                fw.op("pe", lambda e, t=t: e.transpose(out=ps[:, bank, 0:120], in_=sp_tok[:, t, cb * 128:(cb + 1) * 128],
                                                       identity=identf[0:120, 0:120]),
                      reads=[sptB, bf("cst")], writes=[PB[bank]])
                fw.op("act", lambda e, t=t: e.activation(out=sprev(u, t), in_=ps[:, bank, 0:120].rearrange("p (b r) -> p b r", r=15), func=AF.Copy),
                      reads=[PB[bank]], writes=[uB])
            for tb in TBLK:
                c0, n = tb
                bank = nbank()
                proj_fm(su, sub, ib * 128, 128, tb, bank)
                npr = min(c0 + n, TP) - c0
                fw.op("act", lambda e: e.activation(out=u[:, UP0 + c0:UP0 + c0 + npr], in_=ps[:, bank, 0:npr], func=AF.Copy),
                      reads=[PB[bank]], writes=[uB])
                if c0 + n > TP:
                    fw.op("act", lambda e: e.activation(out=snew(u), in_=ps[:, bank, npr:npr + NS].rearrange("p (b r) -> p b r", r=8), func=AF.Copy),
                          reads=[PB[bank]], writes=[uB])
                    fw.op("act", lambda e: e.activation(out=snc[:, :], in_=ps[:, bank, npr:npr + NS], func=AF.Copy),
                          reads=[PB[bank]], writes=[sncB])
                p2_release(P2N)
            fw.op("pe", lambda e: e.transpose(out=ps[0:15, 6, 0:128], in_=u[:, UP0 + TP - 15:UP0 + TP], identity=identf),
                  reads=[uB, bf("cst")], writes=[PB[6]])
            pcol = (cb % 2) * 128
            fw.op("act", lambda e: e.activation(out=pp_stage[0:15, pcol:pcol + 128], in_=ps[0:15, 6, 0:128], func=AF.Copy),
                  reads=[PB[6]], writes=[ppB])
            fw.dma("sp", lambda e: e.dma_start(out=pool_prompt[:, cb * 128:(cb + 1) * 128], in_=pp_stage[0:15, pcol:pcol + 128]),
                   "ppout", reads=[ppB])
            fw.op("pe", lambda e: e.transpose(out=ps[:, 7, 0:128], in_=snc[:, :], identity=identf),
                  reads=[sncB, bf("cst")], writes=[PB[7]])
            fw.op("act", lambda e: e.activation(out=ps_stage[:, cb * 128:(cb + 1) * 128], in_=ps[:, 7, 0:128], func=AF.Copy),
                  reads=[PB[7]], writes=[pssB])

        def stage_B(g, ib):
            ops = []
            w = 2 ** (g + 1)
            pooledT, pooledB = pooled_[gpar[g]], pooledB_[gpar[g]]
            u, uB = u_[ib], uB_[ib]
            src, srcB = u, uB
            dsts = [(Aa, AaB), (Ab, AbB)]
            for lvl in range(g + 1):
                sh = 2 ** lvl
                dst, dstB = dsts[lvl % 2]
                ops.append(lambda src=src, dst=dst, sh=sh, srcB=srcB, dstB=dstB: fw.op(
                    "dve", lambda e: e.tensor_tensor(out=dst[:, 16:UW], in0=src[:, 16:UW], in1=src[:, 16 - sh:UW - sh], op=ALU.add),
                    reads=[srcB], writes=[dstB]))
                src, srcB = dst, dstB
            A, AB = src, srcB

            def tail():
                fw.op("dve", lambda e: e.scalar_tensor_tensor(
                    out=pooledT[:, ib, 0:TP], in0=A[:, UP0:UP0 + TP], scalar=1.0 / w, in1=u[:, UP0:UP0 + TP], op0=ALU.mult, op1=ALU.subtract),
                    reads=[AB, uB], writes=[pooledB])
                fw.op("dve", lambda e: e.scalar_tensor_tensor(
                    out=pooledT[:, ib, SOFF:TT].rearrange("p (b r) -> p b r", r=8), in0=snew(A), scalar=1.0 / w, in1=snew(u),
                    op0=ALU.mult, op1=ALU.subtract), reads=[AB, uB], writes=[pooledB])
                fw.op("dve", lambda e: e.tensor_tensor(out=tmp16[:, 0:16], in0=A[:, UP0:UP0 + 16],
                                                       in1=cst[:, C_RC + 16 * g:C_RC + 16 * (g + 1)], op=ALU.mult),
                      reads=[AB, bf("cst")], writes=[t16B])
                fw.op("dve", lambda e: e.tensor_tensor(out=pooledT[:, ib, 0:16], in0=tmp16[:, 0:16], in1=u[:, UP0:UP0 + 16], op=ALU.subtract),
                      reads=[t16B, uB], writes=[pooledB])
            ops.append(tail)
            return ops

        rot6 = {"i": 0}

        def nbank6():
            b_ = (0, 1, 2, 3, 6, 7)[rot6["i"] % 6]
            rot6["i"] += 1
            return b_

        def stage_C(g, filler=()):
            filler = list(filler)
            nunits = 10
            per = [len(filler) * (i + 1) // nunits - len(filler) * i // nunits for i in range(nunits)]
            unit_i = 0
            sz, szb = wts[g][2], wts[g][3]
            pooledT, pooledB = pooled_[gpar[g]], pooledB_[gpar[g]]
            for ob in range(2):
                cb = 2 * g + ob
                for tb in TBLK:
                    c0, n = tb
                    bm = nbank6()
                    for ib in range(2):
                        fw.op("pe", lambda e, ib=ib: e.matmul(
                            out=ps[:, bm, 0:n], lhsT=wp[:, g, ib, ob * 128:(ob + 1) * 128], rhs=pooledT[:, ib, c0:c0 + n],
                            start=(ib == 0), stop=(ib == 1)), reads=[bf("wp"), pooledB], writes=[PB[bm]], inc=(ib == 1))
                    bz = nbank6()
                    proj_fm(sz, szb, ob * 128, 128, tb, bz)
                    fw.op("act", lambda e: e.activation(out=th[:, 0:n], in_=ps[:, bz, 0:n], func=AF.Tanh, scale=0.5),
                          reads=[PB[bz]], writes=[thB])
                    fw.op("dve", lambda e: e.scalar_tensor_tensor(
                        out=szt[:, 0:n], in0=th[:, 0:n], scalar=1.0, in1=ps[:, bz, 0:n], op0=ALU.add, op1=ALU.mult),
                        reads=[thB, PB[bz]], writes=[sztB])
                    fw.op("dve", lambda e: e.scalar_tensor_tensor(
                        out=yT[:, cb, c0:c0 + n], in0=ps[:, bm, 0:n], scalar=ps5[:, cb:cb + 1], in1=szt[:, 0:n],
                        op0=ALU.mult, op1=ALU.mult), reads=[PB[bm], bf("ps5"), sztB], writes=[bf("yT")])
                    for _ in range(per[unit_i]):
                        filler.pop(0)()
                    unit_i += 1
                    p2_release(P2N)
            assert not filler

        GORDER = [1, 3, 2, 0]
        P2N = 2
        gpar = {g: i % 2 for i, g in enumerate(GORDER)}
        prev_g = None
        for g in GORDER:
            su, sub = load_w(256 * g)
            sz, szb = load_w(1024 + 256 * g)
            wts[g] = (su, sub, sz, szb)
            stage_A(g, 0)
            for o_ in stage_B(g, 0):
                o_()
            stage_A(g, 1)
            bops = stage_B(g, 1)
            if prev_g is not None:
                stage_C(prev_g, bops)
            else:
                for o_ in bops:
                    o_()
            prev_g = g
        p2_drain()
        stage_C(prev_g)
        for j in range(NSEQ):
            fw.dma("sp", lambda e, j=j: e.dma_start(out=pool_sample[j, 7:15, :], in_=ps_stage[8 * j:8 * j + 8, :]), "smallout", reads=[pssB])

        cv = Carve()
        qT = cv.get([128, 2, TT], BF16)
        kT = cv.get([128, 2, TT], BF16)
        gate_tmp_off = cv.off
        tho = [cv.get([128, 512]) for _ in range(2)]
        thz = [cv.get([128, 512]) for _ in range(2)]
        t1b = [cv.get([128, 512]) for _ in range(2)]
        zsb = [cv.get([128, 512]) for _ in range(2)]
        NV, NK, NS_, NCB, NH = 4, 4, 4, 3, 3
        vaug = [cv.get([128, 258], BF16) for _ in range(NV)]
        kw = [cv.get([128, 256], BF16) for _ in range(NK)]
        sTm = [cv.get([128, 128], BF16) for _ in range(NS_)]
        hn = [cv.get([128, 256], BF16) for _ in range(NH)]
        C_st = cv.get([128, 2, 257])
        C_bf = [cv.get([128, 2, 258], BF16) for _ in range(NCB)]
        zq = cv.get([128, 2, 1024], BF16)
        ktok_s = cv.get([128, 256], BF16)
        Wm = cv.get([128, 16])
        NCF, NCS = 6, 2
        Cf = [cv.get([128, 2, 256]) for _ in range(NCF)]
        Csb = [cv.get([128, 2, 256], BF16) for _ in range(NCS)]
        nn_tok = cv.get([64, 256])
        NHR = 6
        hraw = [cv.get([128, 256]) for _ in range(NHR)]
        names = ([f"hraw{i}" for i in range(NHR)] + ["qT", "kT", "zsb0", "zsb1"] + [f"tho{i}" for i in range(2)] + [f"thz{i}" for i in range(2)] + [f"t1b{i}" for i in range(2)]
                 + [f"vaug{i}" for i in range(NV)] + [f"kw{i}" for i in range(NK)] + [f"sTm{i}" for i in range(NS_)]
                 + [f"hn{i}" for i in range(NH)] + [f"C_bf{i}" for i in range(NCB)]
                 + ["C_st", "zq", "ktok_s", "Wm"] + [f"Cf{i}" for i in range(NCF)] + [f"Csb{i}" for i in range(NCS)]
                 + ["nn_tok"])
        phase_bufs(names)
        for i in range(NV):
            fw.op("pool", lambda e, i=i: e.memset(vaug[i][:, 256:258], 1.0), writes=[bf(f"vaug{i}")])
        fw.op("pool", lambda e: e.memset(zq[:], 0.0), writes=[bf("zq")])
        zq_diag = bass.AP(zq.tensor, zq.offset, [list(zq.ap[0]), [1024, 2], [136, 8], [1, 8]])

        wout = xnT[:].rearrange("p k t -> p (k t)")[:, 0:16 * D].rearrange("p (k c) -> p k c", c=D)
        woB = bf("xnT")

        woQ = [Buf(f"woq{i}") for i in range(4)]

        def load_wout():
            for q4 in range(4):
                fw.dma("pool", lambda e, q4=q4: e.dma_start(out=wout[:, 4 * q4:4 * q4 + 4, :],
                                                            in_=w_out[512 * q4:512 * (q4 + 1), :].rearrange("(k p) c -> p k c", p=128)),
                       f"wout{q4}", writes=[woB, woQ[q4]])

        cnt = {"st": 0, "cf": 0, "co": 0, "kj": 0}
        SLAST = NCH - 1

        def head_program(h, drain_prev):
            wq, wqb = load_w(2048 + 256 * h)
            wk, wkb = load_w(3072 + 256 * h)
            wo, wob = load_w(5120 + 256 * h)
            wz, wzb = load_w(6144 + 256 * h)
            def qk_bank():
                b_ = (0, 1, 3, 4, 5, 6, 7)[rotqk["i"] % 7]
                rotqk["i"] += 1
                return b_
            for dc in range(2):
                for tb in TBLK:
                    c0, n = tb
                    bank = qk_bank()
                    proj_fm(wq, wqb, dc * 128, 128, tb, bank)
                    fw.op("act", lambda e: e.activation(out=qT[:, dc, c0:c0 + n], in_=ps[:, bank, 0:n], func=AF.Copy),
                          reads=[PB[bank]], writes=[bf("qT")])
                    bank = qk_bank()
                    proj_fm(wk, wkb, dc * 128, 128, tb, bank)
                    fw.op("act", lambda e: e.activation(out=kT[:, dc, c0:c0 + n], in_=ps[:, bank, 0:n], func=AF.Copy, scale=1.0 / 16),
                          reads=[PB[bank]], writes=[bf("kT")])
                    if drain_prev:
                        drain_prev.pop(0)()
            while drain_prev:
                drain_prev.pop(0)()
            wv, wvb = load_w(4096 + 256 * h)
            for dc in range(2):
                yb = 8 + 2 * h + dc
                for ti, tb in enumerate(TBLK):
                    c0, n = tb
                    i2 = ti % 2
                    bo = nbank()
                    proj_fm(wo, wob, dc * 128, 128, tb, bo)
                    bz = nbank()
                    proj_fm(wz, wzb, dc * 128, 128, tb, bz)
                    fw.op("act", lambda e: e.activation(out=tho[i2][:, 0:n], in_=ps[:, bo, 0:n], func=AF.Tanh, scale=0.5),
                          reads=[PB[bo]], writes=[bf(f"tho{i2}")])
                    fw.op("act", lambda e: e.activation(out=thz[i2][:, 0:n], in_=ps[:, bz, 0:n], func=AF.Tanh, scale=0.5),
                          reads=[PB[bz]], writes=[bf(f"thz{i2}")])
                    fw.op("act", lambda e: e.activation(out=zsb[i2][:, 0:n], in_=ps[:, bz, 0:n], func=AF.Copy, scale=mh4[:, 2 * h + dc:2 * h + dc + 1]),
                          reads=[PB[bz], bf("mh4")], writes=[bf(f"zsb{i2}")])
                    fw.op("dve", lambda e: e.scalar_tensor_tensor(out=t1b[i2][:, 0:n], in0=thz[i2][:, 0:n], scalar=1.0,
                                                                  in1=zsb[i2][:, 0:n], op0=ALU.add, op1=ALU.mult),
                          reads=[bf(f"thz{i2}"), bf(f"zsb{i2}")], writes=[bf(f"t1b{i2}")])
                    fw.op("pool", lambda e: e.tensor_tensor(out=tho[i2][:, 0:n], in0=tho[i2][:, 0:n], in1=t1b[i2][:, 0:n], op=ALU.mult),
                          reads=[bf(f"tho{i2}"), bf(f"t1b{i2}")], writes=[bf(f"tho{i2}")])
                    fw.op("pool", lambda e: e.tensor_tensor(out=yT[:, yb, c0:c0 + n], in0=tho[i2][:, 0:n], in1=t1b[i2][:, 0:n], op=ALU.add),
                          reads=[bf(f"tho{i2}"), bf(f"t1b{i2}")], writes=[bf("yT")])
            fw.op("pool", lambda e: e.memset(C_st[:], 0.0), writes=[bf("C_st")])
            fw.op("dve", lambda e: e.tensor_scalar(out=Wm[:], in0=cst[:, C_BDS:C_BDS + 16], scalar1=tokQ[:, SLAST, 8 + h:9 + h], scalar2=None,
                                                   op0=ALU.mult), reads=[bf("cst"), bf("tokQ")], writes=[bf("Wm")])
            P0a = PB[0]
            P2a = PB[0]
            P0b = PB[2]
            P2b = PB[2]
            P2c = PB[2]
            pk = psb(0)[:, 512:768]
            ph_p = psb(2)[:, 256:512].rearrange("p (d t) -> p d t", t=128)
            ph_s = psb(2)[:, 520:776].rearrange("p (d t) -> p d t", t=128)
            NP = SLAST

            pre_v = (h == 3)
            if pre_v:
                vall = arena[:, gate_tmp_off:gate_tmp_off + NCH * 129].bitcast(BF16).rearrange("p (c e) -> p c e", e=258)
                vallB = Buf("vall")
                for nm_ in ("tho0", "tho1", "thz0", "thz1", "t1b0", "t1b1", "zsb0", "zsb1"):
                    vallB.w.extend(B[nm_].w + B[nm_].r)
                fw.op("pool", lambda e: e.memset(vall[:, :, 256:258], 1.0), writes=[vallB])
                for ct_ in range(NCH):
                    c0_, L_ = CHUNKS[ct_]
                    bk_ = nbank()
                    for k in range(8):
                        fw.op("pe", lambda e, k=k: e.matmul(out=ps[0:L_, bk_, 0:256], lhsT=xnT[:, k, c0_:c0_ + L_], rhs=wv[:, k, :],
                                                            start=(k == 0), stop=(k == 7)),
                              reads=[bf("xnT"), wvb], writes=[PB[bk_]], inc=(k == 7))
                    fw.op("act", lambda e: e.activation(out=vall[0:L_, ct_, 0:256], in_=ps[0:L_, bk_, 0:256], func=AF.Copy),
                          reads=[PB[bk_]], writes=[vallB])
                load_wout()

            def slot_v(ct):
                if pre_v:
                    return vall[:, ct, :], vallB
                i = NV - 1 if ct == SLAST else ct % (NV - 1)
                return vaug[i], bf(f"vaug{i}")

            def slot_s(ct):
                i = NS_ - 1 if ct == SLAST else ct % (NS_ - 1)
                return sTm[i], bf(f"sTm{i}")

            def nbank_of(ct):
                return 1 if ct == SLAST else 4 + (ct % 2)

            def PE_A(ct):
                c0, L = CHUNKS[ct]
                for k in range(8):
                    if pre_v:
                        break
                    fw.op("pe", lambda e, k=k: e.matmul(out=ps[0:L, 0, 0:256], lhsT=xnT[:, k, c0:c0 + L], rhs=wv[:, k, :],
                                                        start=(k == 0), stop=(k == 7)),
                          reads=[bf("xnT"), wvb], writes=[P0a], inc=(k == 7))
                for dc in range(2):
                    fw.op("pe", lambda e, dc=dc: e.transpose(out=pk[0:L, dc * 128:(dc + 1) * 128], in_=kT[:, dc, c0:c0 + L], identity=ident[:, :]),
                          reads=[bf("kT"), bf("ident")], writes=[P2a], inc=(dc == 1))
                for dc in range(2):
                    fw.op("pe", lambda e, dc=dc: e.matmul(out=ps[0:L, 2, 0:L], lhsT=kT[:, dc, c0:c0 + L], rhs=qT[:, dc, c0:c0 + L],
                                                          start=(dc == 0), stop=(dc == 1)),
                          reads=[bf("kT"), bf("qT")], writes=[P0b], inc=(dc == 1))

            def EV_A(ct):
                c0, L = CHUNKS[ct]
                is_s = (ct == SLAST)
                va, vaB = slot_v(ct)
                if not pre_v:
                    fw.op("act", lambda e: e.activation(out=va[0:L, 0:256], in_=ps[0:L, 0, 0:256], func=AF.Copy), reads=[P0a], writes=[vaB])
                if not is_s:
                    kwt, kwB = kw[ct % 2], bf(f"kw{ct % 2}")
                    fw.op("act", lambda e: e.activation(out=kwt[0:L, :], in_=pk[0:L, 0:256], func=AF.Copy, scale=tokQ[0:L, ct, 8 + h:9 + h]),
                          reads=[P2a, bf("tokQ")], writes=[kwB])
                else:
                    fw.op("act", lambda e: e.activation(out=ktok_s[:, :], in_=pk[:, 0:256], func=AF.Copy), reads=[P2a], writes=[bf("ktok_s")])
                mcol = C_BD if is_s else C_CM
                sm_, smB = slot_s(ct)
                fw.op("dve", lambda e: e.scalar_tensor_tensor(out=sm_[0:L, 0:L], in0=ps[0:L, 2, 0:L], scalar=tokQ[0:L, ct, h:h + 1],
                                                              in1=cst[0:L, mcol:mcol + L], op0=ALU.mult, op1=ALU.mult),
                      reads=[P0b, bf("tokQ"), bf("cst")], writes=[smB])

            def PE_U(ct):
                c0, L = CHUNKS[ct]
                va, vaB = slot_v(ct)
                kwt, kwB = kw[ct % 2], bf(f"kw{ct % 2}")
                for dc in range(2):
                    fw.op("pe", lambda e, dc=dc: e.matmul(out=ps[:, 6 + dc, 0:257], lhsT=kwt[0:L, dc * 128:(dc + 1) * 128], rhs=va[0:L, 0:257],
                                                          start=True, stop=True),
                          reads=[kwB, vaB], writes=[PB[6 + dc]], inc=True)

            def ST(ct):
                fw.op("dve", lambda e: e.scalar_tensor_tensor(out=C_st[:], in0=C_st[:], scalar=dec_bc[:, h, ct:ct + 1], in1=ps[:, 6:8, 0:257],
                                                              op0=ALU.mult, op1=ALU.add),
                      reads=[bf("C_st"), bf("dec_bc"), PB[6], PB[7]], writes=[bf("C_st")])
                if ct < NP - 1:
                    cb_, cbB = C_bf[ct % NCB], bf(f"C_bf{ct % NCB}")
                    fw.op("dve", lambda e: e.tensor_copy(out=cb_[:, :, 0:257], in_=C_st[:]), reads=[bf("C_st")], writes=[cbB])
                else:
                    fw.dma("sp", lambda e: e.dma_start(out=C_prompt[h].rearrange("(dc p) e -> p dc e", p=128), in_=C_st[:, :, 0:256]),
                           "smallout", reads=[bf("C_st")])
                    fw.dma("sp", lambda e: e.dma_start(out=n_prompt[h].rearrange("(dc p o) -> p dc o", p=128, o=1), in_=C_st[:, :, 256:257], **NCD),
                           "smallout", reads=[bf("C_st")])

            def NR(ct):
                c0, L = CHUNKS[ct]
                nb_ = nbank_of(ct)
                va, vaB = slot_v(ct)
                sm_, smB = slot_s(ct)
                last_only = (ct == 0)
                stop_ = last_only
                fw.op("pe", lambda e: e.matmul(out=ps[0:L, nb_, 0:257], lhsT=sm_[0:L, 0:L], rhs=va[0:L, 0:257], start=True, stop=stop_),
                      reads=[smB, vaB], writes=[PB[nb_]], inc=True)
                if ct > 0 and ct != SLAST:
                    cb_, cbB = C_bf[(ct - 1) % NCB], bf(f"C_bf{(ct - 1) % NCB}")
                    for dc in range(2):
                        fw.op("pe", lambda e, dc=dc: e.matmul(out=ps[0:L, nb_, 0:257], lhsT=qT[:, dc, c0:c0 + L], rhs=cb_[:, dc, 0:257],
                                                              start=False, stop=(dc == 1)),
                              reads=[bf("qT"), cbB], writes=[PB[nb_]], inc=(dc == 1))

            def issue_loads(j):
                ci = j % NCF
                fw.dma("sp", lambda e: e.dma_start(out=Cf[ci][:], in_=sC[j, h].rearrange("(dc p) e -> p dc e", p=128)),
                       f"cf{ci}", writes=[bf(f"Cf{ci}")])

            def zq_refresh(half):
                zd_new = bass.AP(zq.tensor, zq.offset + 64 * half, [list(zq.ap[0]), [1024, 2], [136, 8], [1, 8]])
                zd_old = bass.AP(zq.tensor, zq.offset + 64 * (1 - half), [list(zq.ap[0]), [1024, 2], [136, 8], [1, 8]])
                fw.op("dve", lambda e: e.memset(zd_old, 0.0), writes=[bf("zq")])
                fw.op("dve", lambda e: e.tensor_copy(
                    out=zd_new, in_=qT[:, :, SOFF + 64 * half:SOFF + 64 * half + 64].rearrange("p d (b r) -> p d b r", r=8)),
                    reads=[bf("qT")], writes=[bf("zq")])

            def T0(j):
                ci, si_ = j % NCF, j % NCS
                fw.op("act", lambda e: e.activation(out=Csb[si_][:], in_=Cf[ci][:], func=AF.Copy), reads=[bf(f"Cf{ci}")], writes=[bf(f"Csb{si_}")])
                kj = 2 + (j % 2)
                fw.op("act", lambda e: e.activation(out=kw[kj][:, :], in_=ktok_s[:, :], func=AF.Copy, scale=Wm[:, j:j + 1]),
                      reads=[bf("ktok_s"), bf("Wm")], writes=[bf(f"kw{kj}")])

            def T1(j):
                va, vaB = slot_v(SLAST)
                si_ = j % NCS
                kj = 2 + (j % 2)
                jj = j % 8
                for dc in range(2):
                    fw.op("pe", lambda e, dc=dc: e.matmul(out=ps[:, 1, 0:256], lhsT=zq[:, dc, jj * 128:(jj + 1) * 128],
                                                          rhs=Csb[si_][:, dc, :], start=False, stop=False),
                          reads=[bf("zq"), bf(f"Csb{si_}")], writes=[PB[1]], inc=False)
                    lastmm = (j == NSEQ - 1 and dc == 1)
                    fw.op("pe", lambda e, dc=dc, lastmm=lastmm: e.matmul(
                        out=ps[:, 1, 256:257], lhsT=zq[:, dc, jj * 128:(jj + 1) * 128], rhs=nT_b[:, dc, 4 * j + h:4 * j + h + 1],
                        start=False, stop=lastmm), reads=[bf("zq"), bf("nT_b")], writes=[PB[1]], inc=True)
                for dc in range(2):
                    fw.op("pe", lambda e, dc=dc: e.matmul(out=ps[:, 3, dc * 256:(dc + 1) * 256], lhsT=kw[kj][:, dc * 128:(dc + 1) * 128],
                                                          rhs=va[:, 0:256], start=True, stop=True),
                          reads=[bf(f"kw{kj}"), vaB], writes=[PB[3]], inc=(dc == 1))
                for dc in range(2):
                    fw.op("pe", lambda e, dc=dc: e.matmul(out=ps[:, 2, 256 + dc:257 + dc], lhsT=kw[kj][:, dc * 128:(dc + 1) * 128],
                                                          rhs=va[:, 256:257], start=True, stop=True),
                          reads=[bf(f"kw{kj}"), vaB], writes=[P2c], inc=(dc == 1))

            def T2(j):
                ci = j % NCF
                fw.op("dve", lambda e: e.scalar_tensor_tensor(
                    out=nnT[:, :, 4 * j + h], in0=nT_f[:, :, 4 * j + h], scalar=dec_bc[:, h, 17 + j:18 + j],
                    in1=ps[:, 2, 256:258], op0=ALU.mult, op1=ALU.add),
                    reads=[bf("nT_f"), bf("dec_bc"), P2c], writes=[bf("nnT")])
                fw.op("dve", lambda e: e.scalar_tensor_tensor(
                    out=Cf[ci][:], in0=Cf[ci][:], scalar=dec_bc[:, h, 17 + j:18 + j], in1=ps[:, 3, :].rearrange("p (d e) -> p d e", e=256),
                    op0=ALU.mult, op1=ALU.add),
                    reads=[bf(f"Cf{ci}"), bf("dec_bc"), PB[3]], writes=[bf(f"Cf{ci}")])
                fw.dma("sp", lambda e: e.dma_start(out=C_sample[j, h].rearrange("(dc p) e -> p dc e", p=128), in_=Cf[ci][:]),
                       f"cf{ci}", reads=[bf(f"Cf{ci}")])

            hstate = {}

            def hr_slot(ct):
                i = NHR - 1 if ct == SLAST else ct % (NHR - 1)
                return hraw[i], bf(f"hraw{i}")

            def H1(ct):
                c0, L = CHUNKS[ct]
                bank = nbank_of(ct)
                si = cnt["st"] % 8
                cnt["st"] += 1
                stile, stb = stt[si], bf(f"stt{si}")
                hstate[ct] = (stile, stb)
                hr, hrB = hr_slot(ct)
                P_ = PB[bank]
                fw.op("pool", lambda e: e.memset(stile[:], 0.0), writes=[stb])
                fw.op("act", lambda e: e.activation(out=stile[0:L, 0:1], in_=ps[0:L, bank, 256:257], func=AF.Square), reads=[P_], writes=[stb])
                fw.op("act", lambda e: e.activation(out=hr[0:L, :], in_=ps[0:L, bank, 0:256], func=AF.Identity, scale=1.0 / 256,
                                                    accum_out=stile[0:L, 1:2]), reads=[P_, stb], writes=[stb, hrB])
                fw.op("act", lambda e: e.activation(out=junk[0:L, 0:256], in_=ps[0:L, bank, 0:256], func=AF.Square, scale=1.0 / 4096,
                                                    accum_out=stile[0:L, 2:3]), reads=[P_, stb], writes=[stb, bf("junk")])

            def H2(ct):
                c0, L = CHUNKS[ct]
                stile, stb = hstate[ct]
                fw.op("dve", lambda e: e.tensor_scalar(out=stile[0:L, 8:9], in0=stile[0:L, 1:2], scalar1=1.0 / 256, scalar2=None, op0=ALU.mult),
                      reads=[stb], writes=[stb])
                fw.op("dve", lambda e: e.tensor_tensor(out=stile[0:L, 3:4], in0=stile[0:L, 0:1], in1=tokQ[0:L, ct, 4 + h:5 + h], op=ALU.max),
                      reads=[stb, bf("tokQ")], writes=[stb])
                fw.op("dve", lambda e: e.scalar_tensor_tensor(out=stile[0:L, 4:5], in0=stile[0:L, 8:9], scalar=stile[0:L, 8:9], in1=stile[0:L, 2:3],
                                                              op0=ALU.mult, op1=ALU.subtract), reads=[stb], writes=[stb])
                fw.op("dve", lambda e: e.scalar_tensor_tensor(out=stile[0:L, 5:6], in0=stile[0:L, 3:4], scalar=EPS / 65536.0, in1=stile[0:L, 4:5],
                                                              op0=ALU.mult, op1=ALU.subtract), reads=[stb], writes=[stb])

            def H3(ct):
                c0, L = CHUNKS[ct]
                stile, stb = hstate[ct]
                fw.op("act", lambda e: e.activation(out=stile[0:L, 6:7], in_=stile[0:L, 5:6], func=AF.Ln), reads=[stb], writes=[stb])
                fw.op("act", lambda e: e.activation(out=stile[0:L, 7:8], in_=stile[0:L, 6:7], func=AF.Exp, scale=-0.5), reads=[stb], writes=[stb])

            def H4(ct):
                c0, L = CHUNKS[ct]
                stile, stb = hstate[ct]
                hr, hrB = hr_slot(ct)
                hi_ = (NH - 1) if ct == SLAST else ct % (NH - 1)
                hslot, hB = hn[hi_], bf(f"hn{hi_}")
                fw.op("dve", lambda e: e.tensor_scalar(out=hslot[0:L, :], in0=hr[0:L, :], scalar1=stile[0:L, 8:9], scalar2=stile[0:L, 7:8],
                                                       op0=ALU.subtract, op1=ALU.mult), reads=[hrB, stb], writes=[hB])

            def H5pe(ct):
                c0, L = CHUNKS[ct]
                ph = ph_s if ct == SLAST else ph_p
                hi_ = (NH - 1) if ct == SLAST else ct % (NH - 1)
                hslot, hB = hn[hi_], bf(f"hn{hi_}")
                for dc in range(2):
                    fw.op("pe", lambda e, dc=dc: e.transpose(out=ph[:, dc, 0:L], in_=hslot[0:L, dc * 128:(dc + 1) * 128], identity=ident[0:L, 0:L]),
                          reads=[hB, bf("ident")], writes=[P2b], inc=(dc == 1))

            def H5ev(ct):
                c0, L = CHUNKS[ct]
                ph = ph_s if ct == SLAST else ph_p
                yv = yT[:, 8 + 2 * h:10 + 2 * h, c0:c0 + L]
                fw.op("dve", lambda e: e.tensor_tensor(out=yv, in0=ph[:, :, 0:L], in1=yv, op=ALU.mult),
                      reads=[P2b, bf("yT")], writes=[bf("yT")])

            for j in range(3):
                issue_loads(j)
            PE_A(SLAST)
            EV_A(SLAST)
            va_s, vaB_s = slot_v(SLAST)
            sm_s, smB_s = slot_s(SLAST)
            fw.op("pe", lambda e: e.matmul(out=ps[:, 1, 0:257], lhsT=sm_s[:, :], rhs=va_s[:, 0:257], start=True, stop=False),
                  reads=[smB_s, vaB_s], writes=[PB[1]], inc=True)
            zq_refresh(0)
            NSTEP = NP + 9

            def step(T):
                def ok(c):
                    return 0 <= c < NP
                if ok(T - 3):
                    ST(T - 3)
                if ok(T - 1):
                    EV_A(T - 1)
                if ok(T - 3):
                    NR(T - 3)
                if ok(T - 2):
                    PE_U(T - 2)
                TS = NSEQ + 2
                if ok(T - 9):
                    H5ev(T - 9)
                if T == TS + 5:
                    H5ev(SLAST)
                if ok(T - 8):
                    H5pe(T - 8)
                if T == TS + 4:
                    H5pe(SLAST)
                if 0 <= T + 2 < NSEQ and T + 2 >= 3:
                    issue_loads(T + 2)
                if 0 <= T - 1 < NSEQ:
                    T0(T - 1)
                if 0 <= T - 3 < NSEQ:
                    T2(T - 3)
                if 0 <= T - 2 < NSEQ:
                    T1(T - 2)
                if ok(T - 5):
                    H2(T - 5)
                if T == TS + 1:
                    H2(SLAST)
                if ok(T - 7):
                    H4(T - 7)
                if T == TS + 3:
                    H4(SLAST)
                if ok(T - 6):
                    H3(T - 6)
                if T == TS + 2:
                    H3(SLAST)
                if ok(T - 4):
                    H1(T - 4)
                if T == TS:
                    H1(SLAST)
                if ok(T):
                    PE_A(T)
                if T - 2 == 7:
                    zq_refresh(1)

            NMAIN = NP + 4
            for T in range(NMAIN):
                step(T)
            return [lambda T=T: step(T) for T in range(NMAIN, NSTEP + 1)]

        rotqk = {"i": 0}
        drain_ = []
        for h_ in range(4):
            drain_ = head_program(h_, drain_)
        for f_ in drain_:
            f_()
        for dc in range(2):
            bank = 4 + dc
            fw.op("pe", lambda e, dc=dc, bank=bank: e.transpose(out=ps[0:64, bank, 0:128], in_=nnT[:, dc, :], identity=identf),
                  reads=[bf("nnT"), bf("cst")], writes=[PB[bank]])
            fw.op("act", lambda e, dc=dc, bank=bank: e.activation(out=nn_tok[:, dc * 128:(dc + 1) * 128], in_=ps[0:64, bank, 0:128], func=AF.Copy),
                  reads=[PB[bank]], writes=[bf("nn_tok")])
        fw.dma("sp", lambda e: e.dma_start(out=n_sample[:, :], in_=nn_tok[:, :]), "smallout", reads=[bf("nn_tok")])

        cv = Carve()
        NX5 = 6
        xts = [cv.get([128, D]) for _ in range(NX5)]
        rts = [cv.get([128, D]) for _ in range(3)]
        ots = [cv.get([128, D]) for _ in range(3)]
        nfw = cv.get([128, D])
        phase_bufs(["nfw"])
        fw.dma("sp", lambda e: e.dma_start(out=nfw[:], in_=normf_w.partition_broadcast(128)), "const", writes=[bf("nfw")])
        xtB = phase_bufs([f"xt{i}" for i in range(NX5)])
        rtB = phase_bufs(["rt0", "rt1", "rt2"])
        otB = phase_bufs(["ot0", "ot1", "ot2"])
        def p5_mm(j, qs):
            c0, n = NTILES[j]
            b0 = (j % 4) * 2
            for q4 in qs:
                for half in range(2):
                    for ic in range(4 * q4, 4 * q4 + 4):
                        fw.op("pe", lambda e, ic=ic, half=half: e.matmul(out=ps[0:n, b0 + half, :], lhsT=yT[:, ic, c0:c0 + n],
                                                                         rhs=wout[:, ic, half * 512:(half + 1) * 512],
                                                                         start=(ic == 0), stop=(ic == 15)),
                              reads=[bf("yT"), woQ[q4]], writes=[PB[b0 + half]], inc=(ic == 15 or ic % 4 == 3))

        def p5_epi(j):
            c0, n = NTILES[j]
            xt, xtb = xts[j % NX5], xtB[j % NX5]
            rt, rtb = rts[j % 3], rtB[j % 3]
            ot, otb = ots[j % 3], otB[j % 3]
            stile, stb = stt[j % 4], bf(f"stt{j % 4}")
            b0 = (j % 4) * 2
            jn = j + NX5 - 1
            if jn < 18:
                load_x_tile(jn, xts[jn % NX5], xtB[jn % NX5], f"xt{jn % NX5}")
            fw.op("dve", lambda e: e.tensor_tensor(out=rt[0:n, :].rearrange("p (a b) -> p a b", b=512),
                                                   in0=ps[0:n, b0:b0 + 2, :], in1=xt[0:n, :].rearrange("p (a b) -> p a b", b=512), op=ALU.add),
                  reads=[PB[b0], PB[b0 + 1], xtb], writes=[rtb])
            fw.op("pool", lambda e: e.memset(stile[:], 0.0), writes=[stb])
            fw.op("act", lambda e: e.activation(out=junk[0:n, :], in_=rt[0:n, :], func=AF.Square, accum_out=stile[0:n, 0:1]),
                  reads=[rtb, stb], writes=[bf("junk"), stb])
            rstd_from_ss(stile, stb, 0, 1, 1.0 / D)
            fw.op("dve", lambda e: e.scalar_tensor_tensor(out=ot[0:n, :], in0=rt[0:n, :], scalar=stile[0:n, 1:2],
                                                          in1=nfw[0:n, :], op0=ALU.mult, op1=ALU.mult),
                  reads=[rtb, stb, bf("nfw")], writes=[otb])
            for (a_, b_, dst, base) in [(16, TP, y_prompt, 16), (SOFF, TT, y_sample, SOFF)]:
                lo = max(c0, a_)
                hi = min(c0 + n, b_)
                if lo < hi:
                    fw.dma("sp", lambda e, lo=lo, hi=hi, dst=dst, base=base: e.dma_start(out=dst[lo - base:hi - base, :],
                                                                                        in_=ot[lo - c0:hi - c0, :]),
                           f"ot{j % 3}", reads=[otb])

        for j_ in range(NX5 - 1):
            load_x_tile(j_, xts[j_], xtB[j_], f"xt{j_}")
        for q4 in range(4):
            for j in range(4):
                p5_mm(j, [q4])
        for j in range(4):
            p5_epi(j)
        for j in range(4, 18):
            p5_mm(j, [0, 1, 2, 3])
            p5_epi(j)
        fw.finish("sp")
        fw.emit(block)
    return nc


_NC = None


def _prep(inputs, i):
    f = lambda a: np.ascontiguousarray(np.asarray(a, dtype=np.float32))
    sl = slice(NSEQ * i, NSEQ * (i + 1))
    return {
        "xp": f(inputs["x_prompt"][i]),
        "xs": f(inputs["x_sample"][sl]).reshape(NS, D),
        "spool": f(inputs["state_pool"][0, sl]),
        "sC": f(inputs["state_C"][0, sl]),
        "sn": f(inputs["state_n"][0, sl]).reshape(NSEQ * 4, 256),
        "sm": f(inputs["state_m"][0, sl]),
        "meta": f(inputs["meta_tokens"]),
        "norm1_w": f(inputs["norm1_w"][0]),
        "w_in": f(inputs["w_in"][0]),
        "b_if": f(inputs["b_if"][0]),
        "w_pool": f(inputs["w_pool"][0]),
        "pool_scale": f(inputs["pool_scale"][0]),
        "mhln_w": f(inputs["mhln_w"][0]).reshape(D),
        "w_out": f(inputs["w_out"][0]),
        "normf_w": f(inputs["normf_w"]),
        "consts": make_consts(),
    }


def _assemble(results):
    n = len(results)
    st = lambda k: np.stack([np.asarray(results[i][k], dtype=np.float32) for i in range(n)])
    y_prompt = st("y_prompt")
    y_sample = st("y_sample").reshape(n * NSEQ, 8, D)
    pool_prompt = st("pool_prompt")[None]
    C_prompt = st("C_prompt")[None]
    n_prompt = st("n_prompt")[None]
    m_prompt = st("m_prompt").reshape(n, 4)[None]
    pool_sample = st("pool_sample").reshape(n * NSEQ, 15, D)[None]
    C_sample = st("C_sample").reshape(n * NSEQ, 4, 256, 256)[None]
    n_sample = st("n_sample").reshape(n * NSEQ, 4, 256)[None]
    m_sample = st("m_sample").reshape(n * NSEQ, 4)[None]
    return (y_prompt, y_sample, pool_prompt, C_prompt, n_prompt, m_prompt, pool_sample, C_sample, n_sample, m_sample)


def kernel(**inputs):
    global _NC
    if _NC is None:
        _NC = build_nc()
    in_maps = [_prep(inputs, i) for i in range(8)]
    res = run_bass_kernel_spmd(_NC, in_maps, core_ids=list(range(8)))
    return _assemble(res.results)
```

```python
import contextlib
import numpy as np
import concourse.bass as bass
import concourse.mybir as mybir
from concourse.bass_utils import run_bass_kernel_spmd

F32 = mybir.dt.float32
BF16 = mybir.dt.bfloat16
AF = mybir.ActivationFunctionType
ALU = mybir.AluOpType

D = 1024
TP = 2064
NS = 128
TT = TP + NS
SOFF = TP
NSEQ = 16
EPS = 1e-6
DPROJ = 7176
TBLK = [(0, 512), (512, 512), (1024, 512), (1536, 512), (2048, 144)]
CHUNKS = [(0, 16)] + [(16 + 128 * c, 128) for c in range(16)] + [(SOFF, 128)]
NCH = len(CHUNKS)
NTILES = [(128 * j, min(128, TT - 128 * j)) for j in range(18)]
UW = 2096 + 23 * NSEQ
UP0 = 32
US0 = 2096

C_ID = 0
C_CM = 128
C_BD = 256
C_SEL = 384
C_BDS = 896
C_RC = 912
C_N = 976


class Buf:
    __slots__ = ("name", "w", "r", "psum")

    def __init__(self, name):
        self.name = name
        self.w = []
        self.r = []
        self.psum = name.startswith("ps") and name[2:].isdigit()


class _Rec:
    def __getattr__(self, name):
        return lambda *a, **k: (name, a, k)


_REC = _Rec()


class FW:
    ENG = ("pe", "act", "dve", "pool", "sp")

    def __init__(self, nc, sems):
        self.nc = nc
        self.free_sems = list(sems)
        self.sem = {}
        self.cnt = {}
        for e in self.ENG:
            self.sem[e] = self.free_sems.pop()
            self.cnt[e] = 0
        self.known = {e: {} for e in self.ENG}
        self.prog = {e: [] for e in self.ENG}

    def _needs(self, e, reads, writes):
        need = {}

        def add(ev):
            k, v = ev
            if k == "pe" and e == "pe":
                return
            if isinstance(k, tuple):
                v = self.cnt[k]
            if need.get(k, 0) < v:
                need[k] = v

        for b in reads:
            for ev in b.w:
                add(ev)
            if b.psum:
                for ev in b.r:
                    if ev[0] != e:
                        add(ev)
        for b in writes:
            for ev in b.w:
                add(ev)
            for ev in b.r:
                add(ev)
        out = []
        kn = self.known[e]
        for k, v in need.items():
            if kn.get(k, 0) < v:
                kn[k] = v
                out.append((k, v))
        return out

    def _commit(self, ev, reads, writes):
        for b in reads:
            b.r.append(ev)
            if len(b.r) > 16:
                d = {}
                for k, v in b.r:
                    if d.get(k, 0) < v:
                        d[k] = v
                b.r = list(d.items())
        for b in writes:
            b.w = [ev]
            b.r = []

    deferred = None

    def op(self, e, fn, reads=(), writes=(), inc=True):
        rec = fn(_REC) if callable(fn) else fn
        if self.deferred is not None:
            self.deferred.append(("op", e, rec, tuple(reads), tuple(writes), inc))
            return
        waits = self._needs(e, reads, writes)
        if inc:
            self.cnt[e] += 1
            ev = (e, self.cnt[e])
        else:
            ev = (e, self.cnt[e] + 1)
        self.prog[e].append((waits, rec, (e, 1) if inc else None))
        self._commit(ev, reads, writes)

    def replay(self, item):
        if item[0] == "op":
            _, e, rec, reads, writes, inc = item
            self.op(e, rec, reads, writes, inc)
        else:
            _, q, rec, key, reads, writes = item
            self.dma(q, rec, key, reads, writes)

    def dma(self, q, fn, key, reads=(), writes=()):
        rec = fn(_REC) if callable(fn) else fn
        if self.deferred is not None:
            self.deferred.append(("dma", q, rec, key, tuple(reads), tuple(writes)))
            return
        key = ("d", key)
        if key not in self.sem:
            self.sem[key] = self.free_sems.pop()
            self.cnt[key] = 0
        waits = self._needs(q, reads, writes)
        self.cnt[key] += 16
        ev = (key, self.cnt[key])
        self.prog[q].append((waits, rec, (key, 16)))
        self._commit(ev, reads, writes)

    def all_events(self):
        evs = []
        for k, v in self.cnt.items():
            if v > 0 and k != "sp":
                evs.append((k, v))
        return evs

    def finish(self, q="sp"):
        waits = []
        for k, v in self.cnt.items():
            if v > 0 and k != q and self.known[q].get(k, 0) < v:
                waits.append((k, v))
        self.prog[q].append((waits, None, None))

    def emit(self, block):
        def run(e):
            def body(engine):
                for waits, fn, inc in self.prog[e]:
                    for k, v in waits:
                        engine.wait_ge(self.sem[k], v)
                    if fn is None:
                        continue
                    ins = getattr(engine, fn[0])(*fn[1], **fn[2])
                    if inc is not None:
                        ins.then_inc(self.sem[inc[0]], inc[1])
            return body
        block.tensor(run("pe"))
        block.scalar(run("act"))
        block.vector(run("dve"))
        block.gpsimd(run("pool"))
        block.sync(run("sp"))


def make_consts():
    c = np.zeros((128, C_N), np.float32)
    p = np.arange(128)
    c[:, C_ID:C_ID + 128] = np.eye(128, dtype=np.float32)
    c[:, C_CM:C_CM + 128] = (p[:, None] <= p[None, :]).astype(np.float32)
    c[:, C_BD:C_BD + 128] = ((p[:, None] <= p[None, :]) & ((p[:, None] // 8) == (p[None, :] // 8))).astype(np.float32)
    for h in range(4):
        c[h, C_SEL + 128 * h:C_SEL + 128 * (h + 1)] = 1.0
    c[:, C_BDS:C_BDS + 16] = ((p[:, None] // 8) == np.arange(16)[None, :]).astype(np.float32)
    for g, w in enumerate((2, 4, 8, 16)):
        pos = np.arange(16)
        c[:, C_RC + 16 * g:C_RC + 16 * (g + 1)] = (1.0 / np.minimum(w, pos + 1)).astype(np.float32)[None, :]
    return c


def build_nc():
    nc = bass.Bass("TRN2", target_bir_lowering=False)

    def din(name, shape):
        return nc.dram_tensor(name, list(shape), F32, kind="ExternalInput").ap()

    def dout(name, shape):
        return nc.dram_tensor(name, list(shape), F32, kind="ExternalOutput").ap()

    xp = din("xp", [2048, D])
    xs = din("xs", [NS, D])
    spool = din("spool", [NSEQ, 15, D])
    sC = din("sC", [NSEQ, 4, 256, 256])
    sn = din("sn", [NSEQ * 4, 256])
    sm = din("sm", [NSEQ, 4])
    meta = din("meta", [16, D])
    norm1_w = din("norm1_w", [D])
    w_in = din("w_in", [D, DPROJ])
    b_if = din("b_if", [8])
    w_pool = din("w_pool", [4, 256, 256])
    pool_scale = din("pool_scale", [D])
    mhln_w = din("mhln_w", [D])
    w_out = din("w_out", [2048, D])
    normf_w = din("normf_w", [D])
    consts = din("consts", [128, C_N])

    y_prompt = dout("y_prompt", [2048, D])
    y_sample = dout("y_sample", [NS, D])
    pool_prompt = dout("pool_prompt", [15, D])
    C_prompt = dout("C_prompt", [4, 256, 256])
    n_prompt = dout("n_prompt", [4, 256])
    m_prompt = dout("m_prompt", [4, 1])
    pool_sample = dout("pool_sample", [NSEQ, 15, D])
    C_sample = dout("C_sample", [NSEQ, 4, 256, 256])
    n_sample = dout("n_sample", [NSEQ * 4, 256])
    m_sample = dout("m_sample", [NSEQ, 4])

    with contextlib.ExitStack() as st:
        E = st.enter_context

        def sb(name, shape, dt=F32):
            return E(nc.sbuf_tensor(name, list(shape), dt))

        xnT = sb("xnT", [128, 8, TT], BF16)
        yT = sb("yT", [128, 16, TT], BF16)
        NW = 4
        wslot = [sb(f"wslot{i}", [128, 8, 256], BF16) for i in range(NW)]
        cst = sb("cst", [128, C_N])
        ident = sb("ident", [128, 128], BF16)
        tokQ = sb("tokQ", [128, NCH, 12])
        dec_bc = sb("dec_bc", [128, 4, 33])
        n1w = sb("n1w", [128, 8])
        ps5 = sb("ps5", [128, 8])
        mh4 = sb("mh4", [128, 8])
        epsT = sb("epsT", [128, 1])
        wg = sb("wg", [128, 8, 8], BF16)
        wp = sb("wp", [128, 4, 2, 256], BF16)
        nT_f = sb("nT_f", [128, 2, 64])
        nT_b = sb("nT_b", [128, 2, 64], BF16)
        nnT = sb("nnT", [128, 2, 64])
        sn_tok = sb("sn_tok", [64, 256])
        stt = [sb(f"stt{i}", [128, 16]) for i in range(8)]
        junk = sb("junk", [128, 1024], BF16)
        Dall = sb("Dall", [4, 33])
        m0T = sb("m0T", [4, 16])
        bif = sb("bif", [4, 2])
        nbf = sb("nbf", [4, 1])
        msm = sb("msm", [4, 17])

        ARENA = 18600
        arena = sb("arena", [128, ARENA])

        class Carve:
            def __init__(self):
                self.off = 0

            def get(self, shape, dt=F32):
                n = int(np.prod(shape[1:]))
                words = n if dt == F32 else (n + 1) // 2
                a = arena[0:shape[0], self.off:self.off + words]
                self.off += words
                assert self.off <= ARENA, self.off
                if dt != F32:
                    a = a.bitcast(dt)
                if len(shape) == 3:
                    a = a.rearrange("p (a b) -> p a b", b=shape[2])
                elif len(shape) == 4:
                    a = a.rearrange("p (a b c) -> p a b c", b=shape[2], c=shape[3])
                return a

        ps = E(nc.psum_tensor("ps", [128, 8, 512], F32))
        sems = [E(nc.semaphore(f"s{i}")) for i in range(80)]
        block = E(nc.Block())
        fw = FW(nc, sems)

        B = {}

        def bf(name):
            if name not in B:
                B[name] = Buf(name)
            return B[name]

        PB = [bf(f"ps{i}") for i in range(8)]

        def phase_bufs(names):
            evs = fw.all_events()
            out = []
            for n in names:
                b = Buf(n)
                b.w = list(evs)
                B[n] = b
                out.append(b)
            return out

        def psb(bank):
            return ps[:, bank, :].bitcast(BF16)

        fw.dma("sp", lambda e: e.dma_start(out=cst[:], in_=consts[:, :]), "const", writes=[bf("cst")])
        NCD = dict(allow_slow_non_contiguous=True)
        fw.dma("sp", lambda e: e.dma_start(out=n1w[:], in_=norm1_w.rearrange("(k p) -> p k", p=128), **NCD), "const", writes=[bf("n1w")])
        fw.dma("sp", lambda e: e.dma_start(out=ps5[:], in_=pool_scale.rearrange("(k p) -> p k", p=128), **NCD), "const", writes=[bf("ps5")])
        fw.dma("sp", lambda e: e.dma_start(out=mh4[:], in_=mhln_w.rearrange("(k p) -> p k", p=128), **NCD), "const", writes=[bf("mh4")])
        fw.dma("sp", lambda e: e.dma_start(out=m0T[:], in_=sm.rearrange("j h -> h j"), **NCD), "const", writes=[bf("m0T")])
        fw.dma("sp", lambda e: e.dma_start(out=bif[:], in_=b_if.rearrange("(t h) -> h t", h=4), **NCD), "const", writes=[bf("bif")])
        fw.dma("sp", lambda e: e.dma_start(out=sn_tok[:], in_=sn[:, :]), "const", writes=[bf("sn_tok")])
        fw.dma("pool", lambda e: e.dma_start(out=wg[:], in_=w_in[:, 7168:7176].rearrange("(k p) c -> p k c", p=128)),
               "wg", writes=[bf("wg")])
        fw.dma("pool", lambda e: e.dma_start(out=wp[:], in_=w_pool.rearrange("g (i p) d -> p g i d", p=128)),
               "wp", writes=[bf("wp")])
        fw.op("dve", lambda e: e.tensor_copy(out=ident[:], in_=cst[:, C_ID:C_ID + 128]), reads=[bf("cst")], writes=[bf("ident")])
        fw.op("dve", lambda e: e.memset(epsT[:], EPS), writes=[bf("epsT")])
        fw.op("dve", lambda e: e.tensor_scalar(out=ps5[:], in0=ps5[:], scalar1=0.5, scalar2=None, op0=ALU.mult),
              reads=[bf("ps5")], writes=[bf("ps5")])
        fw.op("dve", lambda e: e.tensor_scalar(out=mh4[:], in0=mh4[:], scalar1=0.25, scalar2=None, op0=ALU.mult),
              reads=[bf("mh4")], writes=[bf("mh4")])
        fw.op("pool", lambda e: e.memset(nnT[:], 0.0), writes=[bf("nnT")])

        identf = cst[:, C_ID:C_ID + 128]

        wstate = {"i": 0}

        def load_w(col0):
            i = wstate["i"] % NW
            wstate["i"] += 1
            t = wslot[i]
            b = bf(f"wslot{i}")
            fw.dma("pool", lambda e: e.dma_start(out=t[:], in_=w_in[:, col0:col0 + 256].rearrange("(k p) c -> p k c", p=128)),
                   f"w{i}", writes=[b])
            return t, b

        def load_x_tile(j, xt, xtb, key):
            c0, n = NTILES[j]
            segs = [(0, 16, meta, 0), (16, TP, xp, 16), (SOFF, TT, xs, SOFF)]
            for (a, b_, src, base) in segs:
                lo = max(c0, a)
                hi = min(c0 + n, b_)
                if lo < hi:
                    fw.dma("sp", lambda e, lo=lo, hi=hi, src=src, base=base: e.dma_start(
                        out=xt[lo - c0:hi - c0, :], in_=src[lo - base:hi - base, :]), key, writes=[xtb])

        def rstd_from_ss(stile, stb, col_ss, col_out, scale):
            fw.op("act", lambda e: e.activation(out=stile[:, col_out:col_out + 1], in_=stile[:, col_ss:col_ss + 1],
                                                func=AF.Ln, scale=scale, bias=epsT[:, 0:1]),
                  reads=[stb, bf("epsT")], writes=[stb])
            fw.op("act", lambda e: e.activation(out=stile[:, col_out:col_out + 1], in_=stile[:, col_out:col_out + 1],
                                                func=AF.Exp, scale=-0.5), reads=[stb], writes=[stb])

        cv = Carve()
        NX1 = 8
        xts = [cv.get([128, D]) for _ in range(NX1)]
        xbs = [cv.get([128, D], BF16) for _ in range(3)]
        xtB = phase_bufs([f"xt{i}" for i in range(NX1)])
        xbB = phase_bufs(["xb0", "xb1", "xb2"])
        def p1_front(j):
            c0, n = NTILES[j]
            xt, xtb = xts[j % NX1], xtB[j % NX1]
            xb, xbb = xbs[j % 3], xbB[j % 3]
            stile, stb = stt[j % 4], bf(f"stt{j % 4}")
            load_x_tile(j, xt, xtb, f"xt{j % NX1}")
            fw.op("pool", lambda e: e.memset(stile[:], 0.0), writes=[stb])
            fw.op("act", lambda e: e.activation(out=junk[0:n, :], in_=xt[0:n, :], func=AF.Square, accum_out=stile[0:n, 0:1]),
                  reads=[xtb, stb], writes=[bf("junk"), stb])
            rstd_from_ss(stile, stb, 0, 1, 1.0 / D)
            fw.op("dve", lambda e: e.tensor_scalar(out=xb[0:n, :], in0=xt[0:n, :], scalar1=stile[0:n, 1:2], scalar2=None, op0=ALU.mult),
                  reads=[xtb, stb], writes=[xbb])
            bank = 4 + (j % 2)
            pv = psb(bank).rearrange("p (k t) -> p k t", t=128)
            for k in range(8):
                fw.op("pe", lambda e, k=k: e.transpose(out=pv[:, k, 0:n], in_=xb[0:n, k * 128:(k + 1) * 128], identity=ident[0:n, 0:n]),
                      reads=[xbb, bf("ident")], writes=[PB[bank]], inc=(k == 7))

        def p1_back(j):
            c0, n = NTILES[j]
            bank = 4 + (j % 2)
            pv = psb(bank).rearrange("p (k t) -> p k t", t=128)
            fw.op("dve", lambda e: e.tensor_tensor(
                out=xnT[:, :, c0:c0 + n], in0=pv[:, :, 0:n], in1=n1w[:, :].unsqueeze(2).to_broadcast([128, 8, n]), op=ALU.mult),
                reads=[PB[bank], bf("n1w")], writes=[bf("xnT")])

        p1_front(0)
        for j in range(18):
            if j + 1 < 18:
                p1_front(j + 1)
            p1_back(j)

        def proj_fm(wt, wb, f0, nf, tb, bank, src=None):
            c0, n = tb
            for k in range(8):
                fw.op("pe", lambda e, k=k: e.matmul(out=ps[0:nf, bank, 0:n], lhsT=wt[:, k, f0:f0 + nf], rhs=xnT[:, k, c0:c0 + n],
                                                    start=(k == 0), stop=(k == 7)),
                      reads=[wb, bf("xnT")], writes=[PB[bank]], inc=(k == 7))

        def ytile(np_, blk):
            return yT[0:np_, blk:blk + 2, :].rearrange("p a t -> p (a t)").bitcast(F32)

        G_ig = ytile(4, 8)
        G_sp = ytile(4, 10)
        G_P = ytile(4, 12)
        G_gg = ytile(4, 14)
        G_Mx = ytile(4, 0)
        Qt = ytile(96, 10)
        G_t = G_ig
        for nm_ in ("G_ig", "G_sp", "G_P", "G_gg", "G_Mx"):
            B[nm_] = Buf(nm_)
        B["Qt"] = B["G_sp"]
        B["G_t"] = B["G_ig"]
        fw.deferred = []
        fw.op("dve", lambda e: e.tensor_scalar(out=nbf[:], in0=bif[:, 1:2], scalar1=-1.0, scalar2=None, op0=ALU.mult),
              reads=[bf("bif")], writes=[bf("nbf")])
        for ti, tb in enumerate(TBLK):
            c0, n = tb
            b0, b1 = (ti % 2) * 2, (ti % 2) * 2 + 1
            proj_fm(wg, bf("wg"), 0, 4, tb, b0)
            proj_fm(wg, bf("wg"), 4, 4, tb, b1)
            fw.op("dve", lambda e, b0=b0, c0=c0, n=n: e.tensor_scalar(out=G_ig[:, c0:c0 + n], in0=ps[0:4, b0, 0:n], scalar1=bif[:, 0:1],
                                                                      scalar2=None, op0=ALU.add),
                  reads=[PB[b0], bf("bif")], writes=[bf("G_ig")])
            fw.op("act", lambda e, b1=b1, c0=c0, n=n: e.activation(out=G_sp[:, c0:c0 + n], in_=ps[0:4, b1, 0:n], func=AF.Exp, scale=-1.0,
                                                                  bias=nbf[:, 0:1]),
                  reads=[PB[b1], bf("nbf")], writes=[bf("G_sp")])
        fw.op("act", lambda e: e.activation(out=G_sp[:], in_=G_sp[:], func=AF.Ln, bias=1.0), reads=[bf("G_sp")], writes=[bf("G_sp")])
        fw.op("dve", lambda e: e.tensor_tensor_scan(out=G_P[:, 0:TP], data0=G_sp[:, 0:TP], data1=G_sp[:, 0:TP], initial=0.0,
                                                    op0=ALU.add, op1=ALU.max),
              reads=[bf("G_sp")], writes=[bf("G_P")])
        for j in range(NSEQ):
            a = SOFF + 8 * j
            fw.op("dve", lambda e, a=a: e.tensor_tensor_scan(out=G_P[:, a:a + 8], data0=G_sp[:, a:a + 8], data1=G_sp[:, a:a + 8],
                                                             initial=0.0, op0=ALU.add, op1=ALU.max),
                  reads=[bf("G_sp")], writes=[bf("G_P")])
        fw.op("dve", lambda e: e.tensor_tensor(out=G_gg[:], in0=G_ig[:], in1=G_P[:], op=ALU.add),
              reads=[bf("G_ig"), bf("G_P")], writes=[bf("G_gg")])
        fw.op("dve", lambda e: e.tensor_tensor_scan(out=G_Mx[:, 0:TP], data0=G_gg[:, 0:TP], data1=G_gg[:, 0:TP], initial=0.0,
                                                    op0=ALU.max, op1=ALU.max),
              reads=[bf("G_gg")], writes=[bf("G_Mx")])
        for j in range(NSEQ):
            a = SOFF + 8 * j
            fw.op("dve", lambda e, a=a, j=j: e.tensor_tensor_scan(out=G_Mx[:, a:a + 8], data0=G_gg[:, a:a + 8], data1=G_gg[:, a:a + 8],
                                                                  initial=m0T[:, j:j + 1], op0=ALU.max, op1=ALU.max),
                  reads=[bf("G_gg"), bf("m0T")], writes=[bf("G_Mx")])

        def real3(t):
            return t[:, 16:TP].rearrange("p (c l) -> p c l", l=128)

        def samp3(t):
            return t[:, SOFF:TT].rearrange("p (c l) -> p c l", l=8)

        Rprev_real = G_Mx[:, 15:1936:128].unsqueeze(2).to_broadcast([4, 16, 128])
        Rend_real = G_Mx[:, 143:TP:128].unsqueeze(2).to_broadcast([4, 16, 128])
        Rprev_s = m0T[:, :].unsqueeze(2).to_broadcast([4, 16, 8])
        Rend_s = G_Mx[:, SOFF + 7:TT:8].unsqueeze(2).to_broadcast([4, 16, 8])
        Rend_meta = G_Mx[:, 15:16].to_broadcast([4, 16])

        def qrow(src, kind, row0, escale=1.0):
            rd = [bf("G_gg"), bf("G_P"), bf("G_Mx"), bf("m0T")]
            if kind == "prev":
                fw.op("dve", lambda e: e.tensor_copy(out=G_t[:, 0:16], in_=src[:, 0:16]), reads=rd, writes=[bf("G_t")])
                fw.op("dve", lambda e: e.tensor_tensor(out=real3(G_t), in0=real3(src), in1=Rprev_real, op=ALU.subtract), reads=rd, writes=[bf("G_t")])
                fw.op("dve", lambda e: e.tensor_tensor(out=samp3(G_t), in0=samp3(src), in1=Rprev_s, op=ALU.subtract), reads=rd, writes=[bf("G_t")])
            else:
                fw.op("dve", lambda e: e.tensor_tensor(out=G_t[:, 0:16], in0=src[:, 0:16], in1=Rend_meta, op=ALU.subtract), reads=rd, writes=[bf("G_t")])
                fw.op("dve", lambda e: e.tensor_tensor(out=real3(G_t), in0=real3(src), in1=Rend_real, op=ALU.subtract), reads=rd, writes=[bf("G_t")])
                fw.op("dve", lambda e: e.tensor_tensor(out=samp3(G_t), in0=samp3(src), in1=Rend_s, op=ALU.subtract), reads=rd, writes=[bf("G_t")])
            fw.op("act", lambda e: e.activation(out=Qt[row0:row0 + 4, :], in_=G_t[:], func=AF.Exp, scale=escale), reads=[bf("G_t")], writes=[bf("Qt")])

        fw.op("pool", lambda e: e.memset(Qt[:], 0.0), reads=[bf("G_P")], writes=[bf("Qt")])
        qrow(G_gg, "prev", 0)
        qrow(G_P, "prev", 32, 2.0)
        qrow(G_gg, "end", 64)
        rdm = [bf("G_Mx"), bf("m0T")]
        fw.op("dve", lambda e: e.tensor_scalar(out=Dall[:, 0:1], in0=G_Mx[:, 15:16], scalar1=-1.0, scalar2=None, op0=ALU.mult),
              reads=rdm, writes=[bf("Dall")])
        fw.op("dve", lambda e: e.tensor_tensor(out=Dall[:, 1:17], in0=G_Mx[:, 15:1936:128], in1=G_Mx[:, 143:TP:128], op=ALU.subtract),
              reads=rdm, writes=[bf("Dall")])
        fw.op("dve", lambda e: e.tensor_tensor(out=Dall[:, 17:33], in0=m0T[:, :], in1=G_Mx[:, SOFF + 7:TT:8], op=ALU.subtract),
              reads=rdm, writes=[bf("Dall")])
        fw.op("act", lambda e: e.activation(out=Dall[:], in_=Dall[:], func=AF.Exp), reads=[bf("Dall")], writes=[bf("Dall")])
        fw.op("dve", lambda e: e.tensor_tensor(out=msm[:, 0:1], in0=G_Mx[:, TP - 1:TP], in1=G_P[:, TP - 1:TP], op=ALU.subtract),
              reads=[bf("G_Mx"), bf("G_P")], writes=[bf("msm")])
        fw.op("dve", lambda e: e.tensor_tensor(out=msm[:, 1:17], in0=G_Mx[:, SOFF + 7:TT:8], in1=G_P[:, SOFF + 7:TT:8], op=ALU.subtract),
              reads=[bf("G_Mx"), bf("G_P")], writes=[bf("msm")])
        fw.dma("sp", lambda e: e.dma_start(out=m_prompt[:, :], in_=msm[:, 0:1]), "smallout", reads=[bf("msm")])
        fw.dma("sp", lambda e: e.dma_start(out=m_sample.rearrange("j h -> h j"), in_=msm[:, 1:17], **NCD), "smallout", reads=[bf("msm")])
        for ct, (c0, L) in enumerate(CHUNKS):
            bank = 4 + (ct % 2)
            fw.op("pe", lambda e, c0=c0, L=L, bank=bank: e.transpose(out=ps[0:L, bank, 0:96], in_=Qt[:, c0:c0 + L], identity=identf[0:96, 0:96]),
                  reads=[bf("Qt"), bf("cst")], writes=[PB[bank]])
            fw.op("act", lambda e, ct=ct, L=L, bank=bank: e.activation(
                out=tokQ[0:L, ct, :].rearrange("p (a b) -> p a b", b=4),
                in_=ps[0:L, bank, 0:96].rearrange("p (a b) -> p a b", b=32)[:, :, 0:4], func=AF.Copy),
                  reads=[PB[bank]], writes=[bf("tokQ")])
        for h in range(4):
            bank = 6 + (h % 2)
            fw.op("pe", lambda e, h=h, bank=bank: e.matmul(out=ps[:, bank, 0:33], lhsT=cst[0:4, C_SEL + 128 * h:C_SEL + 128 * (h + 1)],
                                                           rhs=Dall[:, :], start=True, stop=True),
                  reads=[bf("Dall"), bf("cst")], writes=[PB[bank]])
            fw.op("act", lambda e, h=h, bank=bank: e.activation(out=dec_bc[:, h, :], in_=ps[:, bank, 0:33], func=AF.Copy),
                  reads=[PB[bank]], writes=[bf("dec_bc")])
        for dc in range(2):
            bank = 6 + dc
            fw.op("pe", lambda e, dc=dc, bank=bank: e.transpose(out=ps[:, bank, 0:64], in_=sn_tok[:, dc * 128:(dc + 1) * 128],
                                                                identity=identf[0:64, 0:64]),
                  reads=[bf("sn_tok"), bf("cst")], writes=[PB[bank]])
            fw.op("act", lambda e, dc=dc, bank=bank: e.activation(out=nT_f[:, dc, :], in_=ps[:, bank, 0:64], func=AF.Copy),
                  reads=[PB[bank]], writes=[bf("nT_f")])
        fw.op("dve", lambda e: e.tensor_copy(out=nT_b[:], in_=nT_f[:]), reads=[bf("nT_f")], writes=[bf("nT_b")])
        p2_items = fw.deferred
        fw.deferred = None
        p2_pending = set()
        last_pe_open = [False]

        def p2_release(n=1):
            k_ = 0
            while p2_items:
                if k_ >= n and not p2_pending and not last_pe_open[0]:
                    break
                it_ = p2_items.pop(0)
                fw.replay(it_)
                if it_[0] == "op":
                    _, e_, _rec, rd_, wr_, inc_ = it_
                    for b_ in rd_:
                        if b_.psum:
                            p2_pending.discard(b_.name)
                    for b_ in wr_:
                        if b_.psum:
                            p2_pending.add(b_.name)
                    last_pe_open[0] = (e_ == "pe" and not inc_)
                if not p2_pending and not last_pe_open[0]:
                    k_ += 1

        def p2_drain():
            p2_release(10 ** 9)
            for nm_ in ("G_ig", "G_sp", "G_P", "G_gg", "G_Mx"):
                bf("yT").r.extend(B[nm_].w + B[nm_].r)

        cv = Carve()
        u_ = [cv.get([128, UW]) for _ in range(2)]
        Aa = cv.get([128, UW])
        Ab = cv.get([128, UW])
        pooled_ = [cv.get([128, 2, TT], BF16) for _ in range(2)]
        sp_tok = cv.get([120, 2, D])
        th = cv.get([128, 512])
        szt = cv.get([128, 512], BF16)
        pp_stage = cv.get([16, 256])
        ps_stage = cv.get([128, D])
        snc = cv.get([128, 128])
        (uB0, uB1, AaB, AbB, pooledB0, pooledB1, sptB, thB, sztB, ppB, pssB, sncB) = phase_bufs(
            ["u0", "u1", "Aa", "Ab", "pooledT0", "pooledT1", "sp_tok", "th", "szt", "pp_stage", "ps_stage", "snc"])
        pooledB_ = [pooledB0, pooledB1]
        uB_ = [uB0, uB1]
        for t in range(2):
            fw.dma("sp", lambda e, t=t: e.dma_start(out=sp_tok[:, t, :], in_=spool[8 * t:8 * t + 8].rearrange("b r c -> (b r) c")),
                   "sptok", writes=[sptB])
        for i_ in range(2):
            fw.op("pool", lambda e, i_=i_: e.memset(u_[i_][:, 0:UP0], 0.0), writes=[uB_[i_]])
        fw.op("pool", lambda e: e.memset(Aa[:, 0:16], 0.0), writes=[AaB])
        fw.op("pool", lambda e: e.memset(Ab[:, 0:16], 0.0), writes=[AbB])
        fw.dma("sp", lambda e: e.dma_start(out=pool_sample[:, 0:7, :], in_=spool[:, 8:15, :]), "smallout")

        def snew(t):
            return bass.AP(t.tensor, t.offset + US0 + 15, [list(t.ap[0]), [23, 16], [1, 8]])

        def sprev(t, half):
            return bass.AP(t.tensor, t.offset + US0 + 23 * 8 * half, [list(t.ap[0]), [23, 8], [1, 15]])

        rot = {"b": 0}

        def nbank():
            b = rot["b"] % 4
            rot["b"] += 1
            return b

        tmp16 = cv.get([128, 16])
        (t16B,) = phase_bufs(["tmp16"])
        wts = {}

        def stage_A(g, ib):
            cb = 2 * g + ib
            u, uB = u_[ib], uB_[ib]
            su, sub = wts[g][0], wts[g][1]
            for t in range(2):
                bank = 4 + t
                fw.op("pe", lambda e, t=t: e.transpose(out=ps[:, bank, 0:120], in_=sp_tok[:, t, cb * 128:(cb + 1) * 128],
                                                       identity=identf[0:120, 0:120]),
                      reads=[sptB, bf("cst")], writes=[PB[bank]])
                fw.op("act", lambda e, t=t: e.activation(out=sprev(u, t), in_=ps[:, bank, 0:120].rearrange("p (b r) -> p b r", r=15), func=AF.Copy),
                      reads=[PB[bank]], writes=[uB])
            for tb in TBLK:
                c0, n = tb
                bank = nbank()
                proj_fm(su, sub, ib * 128, 128, tb, bank)
                npr = min(c0 + n, TP) - c0
                fw.op("act", lambda e: e.activation(out=u[:, UP0 + c0:UP0 + c0 + npr], in_=ps[:, bank, 0:npr], func=AF.Copy),
                      reads=[PB[bank]], writes=[uB])
                if c0 + n > TP:
                    fw.op("act", lambda e: e.activation(out=snew(u), in_=ps[:, bank, npr:npr + NS].rearrange("p (b r) -> p b r", r=8), func=AF.Copy),
                          reads=[PB[bank]], writes=[uB])
                    fw.op("act", lambda e: e.activation(out=snc[:, :], in_=ps[:, bank, npr:npr + NS], func=AF.Copy),
                          reads=[PB[bank]], writes=[sncB])
                p2_release(P2N)
            fw.op("pe", lambda e: e.transpose(out=ps[0:15, 6, 0:128], in_=u[:, UP0 + TP - 15:UP0 + TP], identity=identf),
                  reads=[uB, bf("cst")], writes=[PB[6]])
            pcol = (cb % 2) * 128
            fw.op("act", lambda e: e.activation(out=pp_stage[0:15, pcol:pcol + 128], in_=ps[0:15, 6, 0:128], func=AF.Copy),
                  reads=[PB[6]], writes=[ppB])
            fw.dma("sp", lambda e: e.dma_start(out=pool_prompt[:, cb * 128:(cb + 1) * 128], in_=pp_stage[0:15, pcol:pcol + 128]),
                   "ppout", reads=[ppB])
            fw.op("pe", lambda e: e.transpose(out=ps[:, 7, 0:128], in_=snc[:, :], identity=identf),
                  reads=[sncB, bf("cst")], writes=[PB[7]])
            fw.op("act", lambda e: e.activation(out=ps_stage[:, cb * 128:(cb + 1) * 128], in_=ps[:, 7, 0:128], func=AF.Copy),
                  reads=[PB[7]], writes=[pssB])

        def stage_B(g, ib):
            ops = []
            w = 2 ** (g + 1)
            pooledT, pooledB = pooled_[gpar[g]], pooledB_[gpar[g]]
            u, uB = u_[ib], uB_[ib]
            src, srcB = u, uB
            dsts = [(Aa, AaB), (Ab, AbB)]
            for lvl in range(g + 1):
                sh = 2 ** lvl
                dst, dstB = dsts[lvl % 2]
                ops.append(lambda src=src, dst=dst, sh=sh, srcB=srcB, dstB=dstB: fw.op(
                    "dve", lambda e: e.tensor_tensor(out=dst[:, 16:UW], in0=src[:, 16:UW], in1=src[:, 16 - sh:UW - sh], op=ALU.add),
                    reads=[srcB], writes=[dstB]))
                src, srcB = dst, dstB
            A, AB = src, srcB

            def tail():
                fw.op("dve", lambda e: e.scalar_tensor_tensor(
                    out=pooledT[:, ib, 0:TP], in0=A[:, UP0:UP0 + TP], scalar=1.0 / w, in1=u[:, UP0:UP0 + TP], op0=ALU.mult, op1=ALU.subtract),
                    reads=[AB, uB], writes=[pooledB])
                fw.op("dve", lambda e: e.scalar_tensor_tensor(
                    out=pooledT[:, ib, SOFF:TT].rearrange("p (b r) -> p b r", r=8), in0=snew(A), scalar=1.0 / w, in1=snew(u),
                    op0=ALU.mult, op1=ALU.subtract), reads=[AB, uB], writes=[pooledB])
                fw.op("dve", lambda e: e.tensor_tensor(out=tmp16[:, 0:16], in0=A[:, UP0:UP0 + 16],
                                                       in1=cst[:, C_RC + 16 * g:C_RC + 16 * (g + 1)], op=ALU.mult),
                      reads=[AB, bf("cst")], writes=[t16B])
                fw.op("dve", lambda e: e.tensor_tensor(out=pooledT[:, ib, 0:16], in0=tmp16[:, 0:16], in1=u[:, UP0:UP0 + 16], op=ALU.subtract),
                      reads=[t16B, uB], writes=[pooledB])
            ops.append(tail)
            return ops

        rot6 = {"i": 0}

        def nbank6():
            b_ = (0, 1, 2, 3, 6, 7)[rot6["i"] % 6]
            rot6["i"] += 1
            return b_

        def stage_C(g, filler=()):
            filler = list(filler)
            nunits = 10
            per = [len(filler) * (i + 1) // nunits - len(filler) * i // nunits for i in range(nunits)]
            unit_i = 0
            sz, szb = wts[g][2], wts[g][3]
            pooledT, pooledB = pooled_[gpar[g]], pooledB_[gpar[g]]
            for ob in range(2):
                cb = 2 * g + ob
                for tb in TBLK:
                    c0, n = tb
                    bm = nbank6()
                    for ib in range(2):
                        fw.op("pe", lambda e, ib=ib: e.matmul(
                            out=ps[:, bm, 0:n], lhsT=wp[:, g, ib, ob * 128:(ob + 1) * 128], rhs=pooledT[:, ib, c0:c0 + n],
                            start=(ib == 0), stop=(ib == 1)), reads=[bf("wp"), pooledB], writes=[PB[bm]], inc=(ib == 1))
                    bz = nbank6()
                    proj_fm(sz, szb, ob * 128, 128, tb, bz)
                    fw.op("act", lambda e: e.activation(out=th[:, 0:n], in_=ps[:, bz, 0:n], func=AF.Tanh, scale=0.5),
                          reads=[PB[bz]], writes=[thB])
                    fw.op("dve", lambda e: e.scalar_tensor_tensor(
                        out=szt[:, 0:n], in0=th[:, 0:n], scalar=1.0, in1=ps[:, bz, 0:n], op0=ALU.add, op1=ALU.mult),
                        reads=[thB, PB[bz]], writes=[sztB])
                    fw.op("dve", lambda e: e.scalar_tensor_tensor(
                        out=yT[:, cb, c0:c0 + n], in0=ps[:, bm, 0:n], scalar=ps5[:, cb:cb + 1], in1=szt[:, 0:n],
                        op0=ALU.mult, op1=ALU.mult), reads=[PB[bm], bf("ps5"), sztB], writes=[bf("yT")])
                    for _ in range(per[unit_i]):
                        filler.pop(0)()
                    unit_i += 1
                    p2_release(P2N)
            assert not filler

        GORDER = [1, 3, 2, 0]
        P2N = 2
        gpar = {g: i % 2 for i, g in enumerate(GORDER)}
        prev_g = None
        for g in GORDER:
            su, sub = load_w(256 * g)
            sz, szb = load_w(1024 + 256 * g)
            wts[g] = (su, sub, sz, szb)
            stage_A(g, 0)
            for o_ in stage_B(g, 0):
                o_()
            stage_A(g, 1)
            bops = stage_B(g, 1)
            if prev_g is not None:
                stage_C(prev_g, bops)
            else:
                for o_ in bops:
                    o_()
            prev_g = g
        p2_drain()
        stage_C(prev_g)
        for j in range(NSEQ):
            fw.dma("sp", lambda e, j=j: e.dma_start(out=pool_sample[j, 7:15, :], in_=ps_stage[8 * j:8 * j + 8, :]), "smallout", reads=[pssB])

        cv = Carve()
        qT = cv.get([128, 2, TT], BF16)
        kT = cv.get([128, 2, TT], BF16)
        gate_tmp_off = cv.off
        tho = [cv.get([128, 512]) for _ in range(2)]
        thz = [cv.get([128, 512]) for _ in range(2)]
        t1b = [cv.get([128, 512]) for _ in range(2)]
        zsb = [cv.get([128, 512]) for _ in range(2)]
        NV, NK, NS_, NCB, NH = 4, 4, 4, 3, 3
        vaug = [cv.get([128, 258], BF16) for _ in range(NV)]
        kw = [cv.get([128, 256], BF16) for _ in range(NK)]
        sTm = [cv.get([128, 128], BF16) for _ in range(NS_)]
        hn = [cv.get([128, 256], BF16) for _ in range(NH)]
        C_st = cv.get([128, 2, 257])
        C_bf = [cv.get([128, 2, 258], BF16) for _ in range(NCB)]
        zq = cv.get([128, 2, 1024], BF16)
        ktok_s = cv.get([128, 256], BF16)
        Wm = cv.get([128, 16])
        NCF, NCS = 6, 2
        Cf = [cv.get([128, 2, 256]) for _ in range(NCF)]
        Csb = [cv.get([128, 2, 256], BF16) for _ in range(NCS)]
        nn_tok = cv.get([64, 256])
        NHR = 6
        hraw = [cv.get([128, 256]) for _ in range(NHR)]
        names = ([f"hraw{i}" for i in range(NHR)] + ["qT", "kT", "zsb0", "zsb1"] + [f"tho{i}" for i in range(2)] + [f"thz{i}" for i in range(2)] + [f"t1b{i}" for i in range(2)]
                 + [f"vaug{i}" for i in range(NV)] + [f"kw{i}" for i in range(NK)] + [f"sTm{i}" for i in range(NS_)]
                 + [f"hn{i}" for i in range(NH)] + [f"C_bf{i}" for i in range(NCB)]
                 + ["C_st", "zq", "ktok_s", "Wm"] + [f"Cf{i}" for i in range(NCF)] + [f"Csb{i}" for i in range(NCS)]
                 + ["nn_tok"])
        phase_bufs(names)
        for i in range(NV):
            fw.op("pool", lambda e, i=i: e.memset(vaug[i][:, 256:258], 1.0), writes=[bf(f"vaug{i}")])
        fw.op("pool", lambda e: e.memset(zq[:], 0.0), writes=[bf("zq")])
        zq_diag = bass.AP(zq.tensor, zq.offset, [list(zq.ap[0]), [1024, 2], [136, 8], [1, 8]])

        wout = xnT[:].rearrange("p k t -> p (k t)")[:, 0:16 * D].rearrange("p (k c) -> p k c", c=D)
        woB = bf("xnT")

        woQ = [Buf(f"woq{i}") for i in range(4)]

        def load_wout():
            for q4 in range(4):
                fw.dma("pool", lambda e, q4=q4: e.dma_start(out=wout[:, 4 * q4:4 * q4 + 4, :],
                                                            in_=w_out[512 * q4:512 * (q4 + 1), :].rearrange("(k p) c -> p k c", p=128)),
                       f"wout{q4}", writes=[woB, woQ[q4]])

        cnt = {"st": 0, "cf": 0, "co": 0, "kj": 0}
        SLAST = NCH - 1

        def head_program(h, drain_prev):
            wq, wqb = load_w(2048 + 256 * h)
            wk, wkb = load_w(3072 + 256 * h)
            wo, wob = load_w(5120 + 256 * h)
            wz, wzb = load_w(6144 + 256 * h)
            def qk_bank():
                b_ = (0, 1, 3, 4, 5, 6, 7)[rotqk["i"] % 7]
                rotqk["i"] += 1
                return b_
            for dc in range(2):
                for tb in TBLK:
                    c0, n = tb
                    bank = qk_bank()
                    proj_fm(wq, wqb, dc * 128, 128, tb, bank)
                    fw.op("act", lambda e: e.activation(out=qT[:, dc, c0:c0 + n], in_=ps[:, bank, 0:n], func=AF.Copy),
                          reads=[PB[bank]], writes=[bf("qT")])
                    bank = qk_bank()
                    proj_fm(wk, wkb, dc * 128, 128, tb, bank)
                    fw.op("act", lambda e: e.activation(out=kT[:, dc, c0:c0 + n], in_=ps[:, bank, 0:n], func=AF.Copy, scale=1.0 / 16),
                          reads=[PB[bank]], writes=[bf("kT")])
                    if drain_prev:
                        drain_prev.pop(0)()
            while drain_prev:
                drain_prev.pop(0)()
            wv, wvb = load_w(4096 + 256 * h)
            for dc in range(2):
                yb = 8 + 2 * h + dc
                for ti, tb in enumerate(TBLK):
                    c0, n = tb
                    i2 = ti % 2
                    bo = nbank()
                    proj_fm(wo, wob, dc * 128, 128, tb, bo)
                    bz = nbank()
                    proj_fm(wz, wzb, dc * 128, 128, tb, bz)
                    fw.op("act", lambda e: e.activation(out=tho[i2][:, 0:n], in_=ps[:, bo, 0:n], func=AF.Tanh, scale=0.5),
                          reads=[PB[bo]], writes=[bf(f"tho{i2}")])
                    fw.op("act", lambda e: e.activation(out=thz[i2][:, 0:n], in_=ps[:, bz, 0:n], func=AF.Tanh, scale=0.5),
                          reads=[PB[bz]], writes=[bf(f"thz{i2}")])
                    fw.op("act", lambda e: e.activation(out=zsb[i2][:, 0:n], in_=ps[:, bz, 0:n], func=AF.Copy, scale=mh4[:, 2 * h + dc:2 * h + dc + 1]),
                          reads=[PB[bz], bf("mh4")], writes=[bf(f"zsb{i2}")])
                    fw.op("dve", lambda e: e.scalar_tensor_tensor(out=t1b[i2][:, 0:n], in0=thz[i2][:, 0:n], scalar=1.0,
                                                                  in1=zsb[i2][:, 0:n], op0=ALU.add, op1=ALU.mult),
                          reads=[bf(f"thz{i2}"), bf(f"zsb{i2}")], writes=[bf(f"t1b{i2}")])
                    fw.op("dve", lambda e: e.tensor_tensor(out=tho[i2][:, 0:n], in0=tho[i2][:, 0:n], in1=t1b[i2][:, 0:n], op=ALU.mult),
                          reads=[bf(f"tho{i2}"), bf(f"t1b{i2}")], writes=[bf(f"tho{i2}")])
                    fw.op("pool", lambda e: e.tensor_tensor(out=yT[:, yb, c0:c0 + n], in0=tho[i2][:, 0:n], in1=t1b[i2][:, 0:n], op=ALU.add),
                          reads=[bf(f"tho{i2}"), bf(f"t1b{i2}")], writes=[bf("yT")])
            fw.op("pool", lambda e: e.memset(C_st[:], 0.0), writes=[bf("C_st")])
            fw.op("dve", lambda e: e.tensor_scalar(out=Wm[:], in0=cst[:, C_BDS:C_BDS + 16], scalar1=tokQ[:, SLAST, 8 + h:9 + h], scalar2=None,
                                                   op0=ALU.mult), reads=[bf("cst"), bf("tokQ")], writes=[bf("Wm")])
            P0a = PB[0]
            P2a = PB[0]
            P0b = PB[2]
            P2b = PB[2]
            P2c = PB[2]
            pk = psb(0)[:, 512:768]
            ph_p = psb(2)[:, 256:512].rearrange("p (d t) -> p d t", t=128)
            ph_s = psb(2)[:, 520:776].rearrange("p (d t) -> p d t", t=128)
            NP = SLAST

            pre_v = (h == 3)
            if pre_v:
                vall = arena[:, gate_tmp_off:gate_tmp_off + NCH * 129].bitcast(BF16).rearrange("p (c e) -> p c e", e=258)
                vallB = Buf("vall")
                for nm_ in ("tho0", "tho1", "thz0", "thz1", "t1b0", "t1b1", "zsb0", "zsb1"):
                    vallB.w.extend(B[nm_].w + B[nm_].r)
                fw.op("pool", lambda e: e.memset(vall[:, :, 256:258], 1.0), writes=[vallB])
                for ct_ in range(NCH):
                    c0_, L_ = CHUNKS[ct_]
                    bk_ = nbank()
                    for k in range(8):
                        fw.op("pe", lambda e, k=k: e.matmul(out=ps[0:L_, bk_, 0:256], lhsT=xnT[:, k, c0_:c0_ + L_], rhs=wv[:, k, :],
                                                            start=(k == 0), stop=(k == 7)),
                              reads=[bf("xnT"), wvb], writes=[PB[bk_]], inc=(k == 7))
                    fw.op("act", lambda e: e.activation(out=vall[0:L_, ct_, 0:256], in_=ps[0:L_, bk_, 0:256], func=AF.Copy),
                          reads=[PB[bk_]], writes=[vallB])
                load_wout()

            def slot_v(ct):
                if pre_v:
                    return vall[:, ct, :], vallB
                i = NV - 1 if ct == SLAST else ct % (NV - 1)
                return vaug[i], bf(f"vaug{i}")

            def slot_s(ct):
                i = NS_ - 1 if ct == SLAST else ct % (NS_ - 1)
                return sTm[i], bf(f"sTm{i}")

            def nbank_of(ct):
                return 1 if ct == SLAST else 4 + (ct % 2)

            def PE_A(ct):
                c0, L = CHUNKS[ct]
                for k in range(8):
                    if pre_v:
                        break
                    fw.op("pe", lambda e, k=k: e.matmul(out=ps[0:L, 0, 0:256], lhsT=xnT[:, k, c0:c0 + L], rhs=wv[:, k, :],
                                                        start=(k == 0), stop=(k == 7)),
                          reads=[bf("xnT"), wvb], writes=[P0a], inc=(k == 7))
                for dc in range(2):
                    fw.op("pe", lambda e, dc=dc: e.transpose(out=pk[0:L, dc * 128:(dc + 1) * 128], in_=kT[:, dc, c0:c0 + L], identity=ident[:, :]),
                          reads=[bf("kT"), bf("ident")], writes=[P2a], inc=(dc == 1))
                for dc in range(2):
                    fw.op("pe", lambda e, dc=dc: e.matmul(out=ps[0:L, 2, 0:L], lhsT=kT[:, dc, c0:c0 + L], rhs=qT[:, dc, c0:c0 + L],
                                                          start=(dc == 0), stop=(dc == 1)),
                          reads=[bf("kT"), bf("qT")], writes=[P0b], inc=(dc == 1))

            def EV_A(ct):
                c0, L = CHUNKS[ct]
                is_s = (ct == SLAST)
                va, vaB = slot_v(ct)
                if not pre_v:
                    fw.op("act", lambda e: e.activation(out=va[0:L, 0:256], in_=ps[0:L, 0, 0:256], func=AF.Copy), reads=[P0a], writes=[vaB])
                if not is_s:
                    kwt, kwB = kw[ct % 2], bf(f"kw{ct % 2}")
                    fw.op("act", lambda e: e.activation(out=kwt[0:L, :], in_=pk[0:L, 0:256], func=AF.Copy, scale=tokQ[0:L, ct, 8 + h:9 + h]),
                          reads=[P2a, bf("tokQ")], writes=[kwB])
                else:
                    fw.op("act", lambda e: e.activation(out=ktok_s[:, :], in_=pk[:, 0:256], func=AF.Copy), reads=[P2a], writes=[bf("ktok_s")])
                mcol = C_BD if is_s else C_CM
                sm_, smB = slot_s(ct)
                fw.op("dve", lambda e: e.scalar_tensor_tensor(out=sm_[0:L, 0:L], in0=ps[0:L, 2, 0:L], scalar=tokQ[0:L, ct, h:h + 1],
                                                              in1=cst[0:L, mcol:mcol + L], op0=ALU.mult, op1=ALU.mult),
                      reads=[P0b, bf("tokQ"), bf("cst")], writes=[smB])

            def PE_U(ct):
                c0, L = CHUNKS[ct]
                va, vaB = slot_v(ct)
                kwt, kwB = kw[ct % 2], bf(f"kw{ct % 2}")
                for dc in range(2):
                    fw.op("pe", lambda e, dc=dc: e.matmul(out=ps[:, 6 + dc, 0:257], lhsT=kwt[0:L, dc * 128:(dc + 1) * 128], rhs=va[0:L, 0:257],
                                                          start=True, stop=True),
                          reads=[kwB, vaB], writes=[PB[6 + dc]], inc=True)

            def ST(ct):
                fw.op("dve", lambda e: e.scalar_tensor_tensor(out=C_st[:], in0=C_st[:], scalar=dec_bc[:, h, ct:ct + 1], in1=ps[:, 6:8, 0:257],
                                                              op0=ALU.mult, op1=ALU.add),
                      reads=[bf("C_st"), bf("dec_bc"), PB[6], PB[7]], writes=[bf("C_st")])
                if ct < NP - 1:
                    cb_, cbB = C_bf[ct % NCB], bf(f"C_bf{ct % NCB}")
                    fw.op("dve", lambda e: e.tensor_copy(out=cb_[:, :, 0:257], in_=C_st[:]), reads=[bf("C_st")], writes=[cbB])
                else:
                    fw.dma("sp", lambda e: e.dma_start(out=C_prompt[h].rearrange("(dc p) e -> p dc e", p=128), in_=C_st[:, :, 0:256]),
                           "smallout", reads=[bf("C_st")])
                    fw.dma("sp", lambda e: e.dma_start(out=n_prompt[h].rearrange("(dc p o) -> p dc o", p=128, o=1), in_=C_st[:, :, 256:257], **NCD),
                           "smallout", reads=[bf("C_st")])

            def NR(ct):
                c0, L = CHUNKS[ct]
                nb_ = nbank_of(ct)
                va, vaB = slot_v(ct)
                sm_, smB = slot_s(ct)
                last_only = (ct == 0)
                stop_ = last_only
                fw.op("pe", lambda e: e.matmul(out=ps[0:L, nb_, 0:257], lhsT=sm_[0:L, 0:L], rhs=va[0:L, 0:257], start=True, stop=stop_),
                      reads=[smB, vaB], writes=[PB[nb_]], inc=True)
                if ct > 0 and ct != SLAST:
                    cb_, cbB = C_bf[(ct - 1) % NCB], bf(f"C_bf{(ct - 1) % NCB}")
                    for dc in range(2):
                        fw.op("pe", lambda e, dc=dc: e.matmul(out=ps[0:L, nb_, 0:257], lhsT=qT[:, dc, c0:c0 + L], rhs=cb_[:, dc, 0:257],
                                                              start=False, stop=(dc == 1)),
                              reads=[bf("qT"), cbB], writes=[PB[nb_]], inc=(dc == 1))

            def issue_loads(j):
                ci = j % NCF
                fw.dma("sp", lambda e: e.dma_start(out=Cf[ci][:], in_=sC[j, h].rearrange("(dc p) e -> p dc e", p=128)),
                       f"cf{ci}", writes=[bf(f"Cf{ci}")])

            def zq_refresh(half):
                zd_new = bass.AP(zq.tensor, zq.offset + 64 * half, [list(zq.ap[0]), [1024, 2], [136, 8], [1, 8]])
                zd_old = bass.AP(zq.tensor, zq.offset + 64 * (1 - half), [list(zq.ap[0]), [1024, 2], [136, 8], [1, 8]])
                fw.op("dve", lambda e: e.memset(zd_old, 0.0), writes=[bf("zq")])
                fw.op("dve", lambda e: e.tensor_copy(
                    out=zd_new, in_=qT[:, :, SOFF + 64 * half:SOFF + 64 * half + 64].rearrange("p d (b r) -> p d b r", r=8)),
                    reads=[bf("qT")], writes=[bf("zq")])

            def T0(j):
                ci, si_ = j % NCF, j % NCS
                fw.op("act", lambda e: e.activation(out=Csb[si_][:], in_=Cf[ci][:], func=AF.Copy), reads=[bf(f"Cf{ci}")], writes=[bf(f"Csb{si_}")])
                kj = 2 + (j % 2)
                fw.op("act", lambda e: e.activation(out=kw[kj][:, :], in_=ktok_s[:, :], func=AF.Copy, scale=Wm[:, j:j + 1]),
                      reads=[bf("ktok_s"), bf("Wm")], writes=[bf(f"kw{kj}")])

            def T1(j):
                va, vaB = slot_v(SLAST)
                si_ = j % NCS
                kj = 2 + (j % 2)
                jj = j % 8
                for dc in range(2):
                    fw.op("pe", lambda e, dc=dc: e.matmul(out=ps[:, 1, 0:256], lhsT=zq[:, dc, jj * 128:(jj + 1) * 128],
                                                          rhs=Csb[si_][:, dc, :], start=False, stop=False),
                          reads=[bf("zq"), bf(f"Csb{si_}")], writes=[PB[1]], inc=False)
                    lastmm = (j == NSEQ - 1 and dc == 1)
                    fw.op("pe", lambda e, dc=dc, lastmm=lastmm: e.matmul(
                        out=ps[:, 1, 256:257], lhsT=zq[:, dc, jj * 128:(jj + 1) * 128], rhs=nT_b[:, dc, 4 * j + h:4 * j + h + 1],
                        start=False, stop=lastmm), reads=[bf("zq"), bf("nT_b")], writes=[PB[1]], inc=True)
                for dc in range(2):
                    fw.op("pe", lambda e, dc=dc: e.matmul(out=ps[:, 3, dc * 256:(dc + 1) * 256], lhsT=kw[kj][:, dc * 128:(dc + 1) * 128],
                                                          rhs=va[:, 0:256], start=True, stop=True),
                          reads=[bf(f"kw{kj}"), vaB], writes=[PB[3]], inc=(dc == 1))
                for dc in range(2):
                    fw.op("pe", lambda e, dc=dc: e.matmul(out=ps[:, 2, 256 + dc:257 + dc], lhsT=kw[kj][:, dc * 128:(dc + 1) * 128],
                                                          rhs=va[:, 256:257], start=True, stop=True),
                          reads=[bf(f"kw{kj}"), vaB], writes=[P2c], inc=(dc == 1))

            def T2(j):
                ci = j % NCF
                fw.op("dve", lambda e: e.scalar_tensor_tensor(
                    out=nnT[:, :, 4 * j + h], in0=nT_f[:, :, 4 * j + h], scalar=dec_bc[:, h, 17 + j:18 + j],
                    in1=ps[:, 2, 256:258], op0=ALU.mult, op1=ALU.add),
                    reads=[bf("nT_f"), bf("dec_bc"), P2c], writes=[bf("nnT")])
                fw.op("dve", lambda e: e.scalar_tensor_tensor(
                    out=Cf[ci][:], in0=Cf[ci][:], scalar=dec_bc[:, h, 17 + j:18 + j], in1=ps[:, 3, :].rearrange("p (d e) -> p d e", e=256),
                    op0=ALU.mult, op1=ALU.add),
                    reads=[bf(f"Cf{ci}"), bf("dec_bc"), PB[3]], writes=[bf(f"Cf{ci}")])
                fw.dma("sp", lambda e: e.dma_start(out=C_sample[j, h].rearrange("(dc p) e -> p dc e", p=128), in_=Cf[ci][:]),
                       f"cf{ci}", reads=[bf(f"Cf{ci}")])

            hstate = {}

            def hr_slot(ct):
                i = NHR - 1 if ct == SLAST else ct % (NHR - 1)
                return hraw[i], bf(f"hraw{i}")

            def H1(ct):
                c0, L = CHUNKS[ct]
                bank = nbank_of(ct)
                si = cnt["st"] % 8
                cnt["st"] += 1
                stile, stb = stt[si], bf(f"stt{si}")
                hstate[ct] = (stile, stb)
                hr, hrB = hr_slot(ct)
                P_ = PB[bank]
                fw.op("pool", lambda e: e.memset(stile[:], 0.0), writes=[stb])
                fw.op("act", lambda e: e.activation(out=stile[0:L, 0:1], in_=ps[0:L, bank, 256:257], func=AF.Square), reads=[P_], writes=[stb])
                fw.op("act", lambda e: e.activation(out=hr[0:L, :], in_=ps[0:L, bank, 0:256], func=AF.Identity, scale=1.0 / 256,
                                                    accum_out=stile[0:L, 1:2]), reads=[P_, stb], writes=[stb, hrB])
                fw.op("act", lambda e: e.activation(out=junk[0:L, 0:256], in_=ps[0:L, bank, 0:256], func=AF.Square, scale=1.0 / 4096,
                                                    accum_out=stile[0:L, 2:3]), reads=[P_, stb], writes=[stb, bf("junk")])

            def H2(ct):
                c0, L = CHUNKS[ct]
                stile, stb = hstate[ct]
                fw.op("dve", lambda e: e.tensor_scalar(out=stile[0:L, 8:9], in0=stile[0:L, 1:2], scalar1=1.0 / 256, scalar2=None, op0=ALU.mult),
                      reads=[stb], writes=[stb])
                fw.op("dve", lambda e: e.tensor_tensor(out=stile[0:L, 3:4], in0=stile[0:L, 0:1], in1=tokQ[0:L, ct, 4 + h:5 + h], op=ALU.max),
                      reads=[stb, bf("tokQ")], writes=[stb])
                fw.op("dve", lambda e: e.scalar_tensor_tensor(out=stile[0:L, 4:5], in0=stile[0:L, 8:9], scalar=stile[0:L, 8:9], in1=stile[0:L, 2:3],
                                                              op0=ALU.mult, op1=ALU.subtract), reads=[stb], writes=[stb])
                fw.op("dve", lambda e: e.scalar_tensor_tensor(out=stile[0:L, 5:6], in0=stile[0:L, 3:4], scalar=EPS / 65536.0, in1=stile[0:L, 4:5],
                                                              op0=ALU.mult, op1=ALU.subtract), reads=[stb], writes=[stb])

            def H3(ct):
                c0, L = CHUNKS[ct]
                stile, stb = hstate[ct]
                fw.op("act", lambda e: e.activation(out=stile[0:L, 6:7], in_=stile[0:L, 5:6], func=AF.Ln), reads=[stb], writes=[stb])
                fw.op("act", lambda e: e.activation(out=stile[0:L, 7:8], in_=stile[0:L, 6:7], func=AF.Exp, scale=-0.5), reads=[stb], writes=[stb])

            def H4(ct):
                c0, L = CHUNKS[ct]
                stile, stb = hstate[ct]
                hr, hrB = hr_slot(ct)
                hi_ = (NH - 1) if ct == SLAST else ct % (NH - 1)
                hslot, hB = hn[hi_], bf(f"hn{hi_}")
                fw.op("dve", lambda e: e.tensor_scalar(out=hslot[0:L, :], in0=hr[0:L, :], scalar1=stile[0:L, 8:9], scalar2=stile[0:L, 7:8],
                                                       op0=ALU.subtract, op1=ALU.mult), reads=[hrB, stb], writes=[hB])

            def H5pe(ct):
                c0, L = CHUNKS[ct]
                ph = ph_s if ct == SLAST else ph_p
                hi_ = (NH - 1) if ct == SLAST else ct % (NH - 1)
                hslot, hB = hn[hi_], bf(f"hn{hi_}")
                for dc in range(2):
                    fw.op("pe", lambda e, dc=dc: e.transpose(out=ph[:, dc, 0:L], in_=hslot[0:L, dc * 128:(dc + 1) * 128], identity=ident[0:L, 0:L]),
                          reads=[hB, bf("ident")], writes=[P2b], inc=(dc == 1))

            def H5ev(ct):
                c0, L = CHUNKS[ct]
                ph = ph_s if ct == SLAST else ph_p
                yv = yT[:, 8 + 2 * h:10 + 2 * h, c0:c0 + L]
                fw.op("dve", lambda e: e.tensor_tensor(out=yv, in0=ph[:, :, 0:L], in1=yv, op=ALU.mult),
                      reads=[P2b, bf("yT")], writes=[bf("yT")])

            for j in range(3):
                issue_loads(j)
            PE_A(SLAST)
            EV_A(SLAST)
            va_s, vaB_s = slot_v(SLAST)
            sm_s, smB_s = slot_s(SLAST)
            fw.op("pe", lambda e: e.matmul(out=ps[:, 1, 0:257], lhsT=sm_s[:, :], rhs=va_s[:, 0:257], start=True, stop=False),
                  reads=[smB_s, vaB_s], writes=[PB[1]], inc=True)
            zq_refresh(0)
            NSTEP = NP + 9

            def step(T):
                def ok(c):
                    return 0 <= c < NP
                if ok(T - 3):
                    ST(T - 3)
                if ok(T - 1):
                    EV_A(T - 1)
                if ok(T - 3):
                    NR(T - 3)
                if ok(T - 2):
                    PE_U(T - 2)
                if 0 <= T + 2 < NSEQ and T + 2 >= 3:
                    issue_loads(T + 2)
                if 0 <= T - 1 < NSEQ:
                    T0(T - 1)
                if 0 <= T - 3 < NSEQ:
                    T2(T - 3)
                if 0 <= T - 2 < NSEQ:
                    T1(T - 2)
                TS = NSEQ + 2
                if ok(T - 5):
                    H2(T - 5)
                if T == TS + 1:
                    H2(SLAST)
                if ok(T - 7):
                    H4(T - 7)
                if T == TS + 3:
                    H4(SLAST)
                if ok(T - 6):
                    H3(T - 6)
                if T == TS + 2:
                    H3(SLAST)
                if ok(T - 4):
                    H1(T - 4)
                if T == TS:
                    H1(SLAST)
                if ok(T - 9):
                    H5ev(T - 9)
                if T == TS + 5:
                    H5ev(SLAST)
                if ok(T - 8):
                    H5pe(T - 8)
                if T == TS + 4:
                    H5pe(SLAST)
                if ok(T):
                    PE_A(T)
                if T - 2 == 7:
                    zq_refresh(1)

            NMAIN = NP + 4
            for T in range(NMAIN):
                step(T)
            return [lambda T=T: step(T) for T in range(NMAIN, NSTEP + 1)]

        rotqk = {"i": 0}
        drain_ = []
        for h_ in range(4):
            drain_ = head_program(h_, drain_)
        for f_ in drain_:
            f_()
        for dc in range(2):
            bank = 4 + dc
            fw.op("pe", lambda e, dc=dc, bank=bank: e.transpose(out=ps[0:64, bank, 0:128], in_=nnT[:, dc, :], identity=identf),
                  reads=[bf("nnT"), bf("cst")], writes=[PB[bank]])
            fw.op("act", lambda e, dc=dc, bank=bank: e.activation(out=nn_tok[:, dc * 128:(dc + 1) * 128], in_=ps[0:64, bank, 0:128], func=AF.Copy),
                  reads=[PB[bank]], writes=[bf("nn_tok")])
        fw.dma("sp", lambda e: e.dma_start(out=n_sample[:, :], in_=nn_tok[:, :]), "smallout", reads=[bf("nn_tok")])

        cv = Carve()
        NX5 = 6
        xts = [cv.get([128, D]) for _ in range(NX5)]
        rts = [cv.get([128, D]) for _ in range(3)]
        ots = [cv.get([128, D]) for _ in range(3)]
        nfw = cv.get([128, D])
        phase_bufs(["nfw"])
        fw.dma("sp", lambda e: e.dma_start(out=nfw[:], in_=normf_w.partition_broadcast(128)), "const", writes=[bf("nfw")])
        xtB = phase_bufs([f"xt{i}" for i in range(NX5)])
        rtB = phase_bufs(["rt0", "rt1", "rt2"])
        otB = phase_bufs(["ot0", "ot1", "ot2"])
        def p5_mm(j, qs):
            c0, n = NTILES[j]
            b0 = (j % 4) * 2
            for q4 in qs:
                for half in range(2):
                    for ic in range(4 * q4, 4 * q4 + 4):
                        fw.op("pe", lambda e, ic=ic, half=half: e.matmul(out=ps[0:n, b0 + half, :], lhsT=yT[:, ic, c0:c0 + n],
                                                                         rhs=wout[:, ic, half * 512:(half + 1) * 512],
                                                                         start=(ic == 0), stop=(ic == 15)),
                              reads=[bf("yT"), woQ[q4]], writes=[PB[b0 + half]], inc=(ic == 15 or ic % 4 == 3))

        def p5_epi(j):
            c0, n = NTILES[j]
            xt, xtb = xts[j % NX5], xtB[j % NX5]
            rt, rtb = rts[j % 3], rtB[j % 3]
            ot, otb = ots[j % 3], otB[j % 3]
            stile, stb = stt[j % 4], bf(f"stt{j % 4}")
            b0 = (j % 4) * 2
            jn = j + NX5 - 1
            if jn < 18:
                load_x_tile(jn, xts[jn % NX5], xtB[jn % NX5], f"xt{jn % NX5}")
            fw.op("dve", lambda e: e.tensor_tensor(out=rt[0:n, :].rearrange("p (a b) -> p a b", b=512),
                                                   in0=ps[0:n, b0:b0 + 2, :], in1=xt[0:n, :].rearrange("p (a b) -> p a b", b=512), op=ALU.add),
                  reads=[PB[b0], PB[b0 + 1], xtb], writes=[rtb])
            fw.op("pool", lambda e: e.memset(stile[:], 0.0), writes=[stb])
            fw.op("act", lambda e: e.activation(out=junk[0:n, :], in_=rt[0:n, :], func=AF.Square, accum_out=stile[0:n, 0:1]),
                  reads=[rtb, stb], writes=[bf("junk"), stb])
            rstd_from_ss(stile, stb, 0, 1, 1.0 / D)
            fw.op("dve", lambda e: e.scalar_tensor_tensor(out=ot[0:n, :], in0=rt[0:n, :], scalar=stile[0:n, 1:2],
                                                          in1=nfw[0:n, :], op0=ALU.mult, op1=ALU.mult),
                  reads=[rtb, stb, bf("nfw")], writes=[otb])
            for (a_, b_, dst, base) in [(16, TP, y_prompt, 16), (SOFF, TT, y_sample, SOFF)]:
                lo = max(c0, a_)
                hi = min(c0 + n, b_)
                if lo < hi:
                    fw.dma("sp", lambda e, lo=lo, hi=hi, dst=dst, base=base: e.dma_start(out=dst[lo - base:hi - base, :],
                                                                                        in_=ot[lo - c0:hi - c0, :]),
                           f"ot{j % 3}", reads=[otb])

        for j_ in range(NX5 - 1):
            load_x_tile(j_, xts[j_], xtB[j_], f"xt{j_}")
        for q4 in range(4):
            for j in range(4):
                p5_mm(j, [q4])
        for j in range(4):
            p5_epi(j)
        for j in range(4, 18):
            p5_mm(j, [0, 1, 2, 3])
            p5_epi(j)
        fw.finish("sp")
        fw.emit(block)
    return nc


_NC = None


def _prep(inputs, i):
    f = lambda a: np.ascontiguousarray(np.asarray(a, dtype=np.float32))
    sl = slice(NSEQ * i, NSEQ * (i + 1))
    return {
        "xp": f(inputs["x_prompt"][i]),
        "xs": f(inputs["x_sample"][sl]).reshape(NS, D),
        "spool": f(inputs["state_pool"][0, sl]),
        "sC": f(inputs["state_C"][0, sl]),
        "sn": f(inputs["state_n"][0, sl]).reshape(NSEQ * 4, 256),
        "sm": f(inputs["state_m"][0, sl]),
        "meta": f(inputs["meta_tokens"]),
        "norm1_w": f(inputs["norm1_w"][0]),
        "w_in": f(inputs["w_in"][0]),
        "b_if": f(inputs["b_if"][0]),
        "w_pool": f(inputs["w_pool"][0]),
        "pool_scale": f(inputs["pool_scale"][0]),
        "mhln_w": f(inputs["mhln_w"][0]).reshape(D),
        "w_out": f(inputs["w_out"][0]),
        "normf_w": f(inputs["normf_w"]),
        "consts": make_consts(),
    }


def _assemble(results):
    n = len(results)
    st = lambda k: np.stack([np.asarray(results[i][k], dtype=np.float32) for i in range(n)])
    y_prompt = st("y_prompt")
    y_sample = st("y_sample").reshape(n * NSEQ, 8, D)
    pool_prompt = st("pool_prompt")[None]
    C_prompt = st("C_prompt")[None]
    n_prompt = st("n_prompt")[None]
    m_prompt = st("m_prompt").reshape(n, 4)[None]
    pool_sample = st("pool_sample").reshape(n * NSEQ, 15, D)[None]
    C_sample = st("C_sample").reshape(n * NSEQ, 4, 256, 256)[None]
    n_sample = st("n_sample").reshape(n * NSEQ, 4, 256)[None]
    m_sample = st("m_sample").reshape(n * NSEQ, 4)[None]
    return (y_prompt, y_sample, pool_prompt, C_prompt, n_prompt, m_prompt, pool_sample, C_sample, n_sample, m_sample)


def kernel(**inputs):
    global _NC
    if _NC is None:
        _NC = build_nc()
    in_maps = [_prep(inputs, i) for i in range(8)]
    res = run_bass_kernel_spmd(_NC, in_maps, core_ids=list(range(8)))
    return _assemble(res.results)
```

```python
import contextlib
import numpy as np
import concourse.bass as bass
import concourse.mybir as mybir
from concourse.bass_utils import run_bass_kernel_spmd

F32 = mybir.dt.float32
BF16 = mybir.dt.bfloat16
AF = mybir.ActivationFunctionType
ALU = mybir.AluOpType

D = 1024
TP = 2064
NS = 128
TT = TP + NS
SOFF = TP
NSEQ = 16
EPS = 1e-6
DPROJ = 7176
TBLK = [(0, 512), (512, 512), (1024, 512), (1536, 512), (2048, 144)]
CHUNKS = [(0, 16)] + [(16 + 128 * c, 128) for c in range(16)] + [(SOFF, 128)]
NCH = len(CHUNKS)
NTILES = [(128 * j, min(128, TT - 128 * j)) for j in range(18)]
UW = 2096 + 23 * NSEQ
UP0 = 32
US0 = 2096

C_ID = 0
C_CM = 128
C_BD = 256
C_SEL = 384
C_BDS = 896
C_RC = 912
C_N = 976


class Buf:
    __slots__ = ("name", "w", "r", "psum")

    def __init__(self, name):
        self.name = name
        self.w = []
        self.r = []
        self.psum = name.startswith("ps") and name[2:].isdigit()


class _Rec:
    def __getattr__(self, name):
        return lambda *a, **k: (name, a, k)


_REC = _Rec()


class FW:
    ENG = ("pe", "act", "dve", "pool", "sp")

    def __init__(self, nc, sems):
        self.nc = nc
        self.free_sems = list(sems)
        self.sem = {}
        self.cnt = {}
        for e in self.ENG:
            self.sem[e] = self.free_sems.pop()
            self.cnt[e] = 0
        self.known = {e: {} for e in self.ENG}
        self.prog = {e: [] for e in self.ENG}

    def _needs(self, e, reads, writes):
        need = {}

        def add(ev):
            k, v = ev
            if k == "pe" and e == "pe":
                return
            if isinstance(k, tuple):
                v = self.cnt[k]
            if need.get(k, 0) < v:
                need[k] = v

        for b in reads:
            for ev in b.w:
                add(ev)
            if b.psum:
                for ev in b.r:
                    if ev[0] != e:
                        add(ev)
        for b in writes:
            for ev in b.w:
                add(ev)
            for ev in b.r:
                add(ev)
        out = []
        kn = self.known[e]
        for k, v in need.items():
            if kn.get(k, 0) < v:
                kn[k] = v
                out.append((k, v))
        return out

    def _commit(self, ev, reads, writes):
        for b in reads:
            b.r.append(ev)
            if len(b.r) > 16:
                d = {}
                for k, v in b.r:
                    if d.get(k, 0) < v:
                        d[k] = v
                b.r = list(d.items())
        for b in writes:
            b.w = [ev]
            b.r = []

    deferred = None

    def op(self, e, fn, reads=(), writes=(), inc=True):
        rec = fn(_REC) if callable(fn) else fn
        if self.deferred is not None:
            self.deferred.append(("op", e, rec, tuple(reads), tuple(writes), inc))
            return
        waits = self._needs(e, reads, writes)
        if inc:
            self.cnt[e] += 1
            ev = (e, self.cnt[e])
        else:
            ev = (e, self.cnt[e] + 1)
        self.prog[e].append((waits, rec, (e, 1) if inc else None))
        self._commit(ev, reads, writes)

    def replay(self, item):
        if item[0] == "op":
            _, e, rec, reads, writes, inc = item
            self.op(e, rec, reads, writes, inc)
        else:
            _, q, rec, key, reads, writes = item
            self.dma(q, rec, key, reads, writes)

    def dma(self, q, fn, key, reads=(), writes=()):
        rec = fn(_REC) if callable(fn) else fn
        if self.deferred is not None:
            self.deferred.append(("dma", q, rec, key, tuple(reads), tuple(writes)))
            return
        key = ("d", key)
        if key not in self.sem:
            self.sem[key] = self.free_sems.pop()
            self.cnt[key] = 0
        waits = self._needs(q, reads, writes)
        self.cnt[key] += 16
        ev = (key, self.cnt[key])
        self.prog[q].append((waits, rec, (key, 16)))
        self._commit(ev, reads, writes)

    def all_events(self):
        evs = []
        for k, v in self.cnt.items():
            if v > 0 and k != "sp":
                evs.append((k, v))
        return evs

    def finish(self, q="sp"):
        waits = []
        for k, v in self.cnt.items():
            if v > 0 and k != q and self.known[q].get(k, 0) < v:
                waits.append((k, v))
        self.prog[q].append((waits, None, None))

    def emit(self, block):
        def run(e):
            def body(engine):
                for waits, fn, inc in self.prog[e]:
                    for k, v in waits:
                        engine.wait_ge(self.sem[k], v)
                    if fn is None:
                        continue
                    ins = getattr(engine, fn[0])(*fn[1], **fn[2])
                    if inc is not None:
                        ins.then_inc(self.sem[inc[0]], inc[1])
            return body
        block.tensor(run("pe"))
        block.scalar(run("act"))
        block.vector(run("dve"))
        block.gpsimd(run("pool"))
        block.sync(run("sp"))


def make_consts():
    c = np.zeros((128, C_N), np.float32)
    p = np.arange(128)
    c[:, C_ID:C_ID + 128] = np.eye(128, dtype=np.float32)
    c[:, C_CM:C_CM + 128] = (p[:, None] <= p[None, :]).astype(np.float32)
    c[:, C_BD:C_BD + 128] = ((p[:, None] <= p[None, :]) & ((p[:, None] // 8) == (p[None, :] // 8))).astype(np.float32)
    for h in range(4):
        c[h, C_SEL + 128 * h:C_SEL + 128 * (h + 1)] = 1.0
    c[:, C_BDS:C_BDS + 16] = ((p[:, None] // 8) == np.arange(16)[None, :]).astype(np.float32)
    for g, w in enumerate((2, 4, 8, 16)):
        pos = np.arange(16)
        c[:, C_RC + 16 * g:C_RC + 16 * (g + 1)] = (1.0 / np.minimum(w, pos + 1)).astype(np.float32)[None, :]
    return c


def build_nc():
    nc = bass.Bass("TRN2", target_bir_lowering=False)

    def din(name, shape):
        return nc.dram_tensor(name, list(shape), F32, kind="ExternalInput").ap()

    def dout(name, shape):
        return nc.dram_tensor(name, list(shape), F32, kind="ExternalOutput").ap()

    xp = din("xp", [2048, D])
    xs = din("xs", [NS, D])
    spool = din("spool", [NSEQ, 15, D])
    sC = din("sC", [NSEQ, 4, 256, 256])
    sn = din("sn", [NSEQ * 4, 256])
    sm = din("sm", [NSEQ, 4])
    meta = din("meta", [16, D])
    norm1_w = din("norm1_w", [D])
    w_in = din("w_in", [D, DPROJ])
    b_if = din("b_if", [8])
    w_pool = din("w_pool", [4, 256, 256])
    pool_scale = din("pool_scale", [D])
    mhln_w = din("mhln_w", [D])
    w_out = din("w_out", [2048, D])
    normf_w = din("normf_w", [D])
    consts = din("consts", [128, C_N])

    y_prompt = dout("y_prompt", [2048, D])
    y_sample = dout("y_sample", [NS, D])
    pool_prompt = dout("pool_prompt", [15, D])
    C_prompt = dout("C_prompt", [4, 256, 256])
    n_prompt = dout("n_prompt", [4, 256])
    m_prompt = dout("m_prompt", [4, 1])
    pool_sample = dout("pool_sample", [NSEQ, 15, D])
    C_sample = dout("C_sample", [NSEQ, 4, 256, 256])
    n_sample = dout("n_sample", [NSEQ * 4, 256])
    m_sample = dout("m_sample", [NSEQ, 4])

    with contextlib.ExitStack() as st:
        E = st.enter_context

        def sb(name, shape, dt=F32):
            return E(nc.sbuf_tensor(name, list(shape), dt))

        xnT = sb("xnT", [128, 8, TT], BF16)
        yT = sb("yT", [128, 16, TT], BF16)
        NW = 4
        wslot = [sb(f"wslot{i}", [128, 8, 256], BF16) for i in range(NW)]
        cst = sb("cst", [128, C_N])
        ident = sb("ident", [128, 128], BF16)
        tokQ = sb("tokQ", [128, NCH, 12])
        dec_bc = sb("dec_bc", [128, 4, 33])
        n1w = sb("n1w", [128, 8])
        ps5 = sb("ps5", [128, 8])
        mh4 = sb("mh4", [128, 8])
        epsT = sb("epsT", [128, 1])
        wg = sb("wg", [128, 8, 8], BF16)
        wp = sb("wp", [128, 4, 2, 256], BF16)
        nT_f = sb("nT_f", [128, 2, 64])
        nT_b = sb("nT_b", [128, 2, 64], BF16)
        nnT = sb("nnT", [128, 2, 64])
        sn_tok = sb("sn_tok", [64, 256])
        stt = [sb(f"stt{i}", [128, 16]) for i in range(8)]
        junk = sb("junk", [128, 1024], BF16)
        Dall = sb("Dall", [4, 33])
        m0T = sb("m0T", [4, 16])
        bif = sb("bif", [4, 2])
        nbf = sb("nbf", [4, 1])
        msm = sb("msm", [4, 17])

        ARENA = 18600
        arena = sb("arena", [128, ARENA])

        class Carve:
            def __init__(self):
                self.off = 0

            def get(self, shape, dt=F32):
                n = int(np.prod(shape[1:]))
                words = n if dt == F32 else (n + 1) // 2
                a = arena[0:shape[0], self.off:self.off + words]
                self.off += words
                assert self.off <= ARENA, self.off
                if dt != F32:
                    a = a.bitcast(dt)
                if len(shape) == 3:
                    a = a.rearrange("p (a b) -> p a b", b=shape[2])
                elif len(shape) == 4:
                    a = a.rearrange("p (a b c) -> p a b c", b=shape[2], c=shape[3])
                return a

        ps = E(nc.psum_tensor("ps", [128, 8, 512], F32))
        sems = [E(nc.semaphore(f"s{i}")) for i in range(80)]
        block = E(nc.Block())
        fw = FW(nc, sems)

        B = {}

        def bf(name):
            if name not in B:
                B[name] = Buf(name)
            return B[name]

        PB = [bf(f"ps{i}") for i in range(8)]

        def phase_bufs(names):
            evs = fw.all_events()
            out = []
            for n in names:
                b = Buf(n)
                b.w = list(evs)
                B[n] = b
                out.append(b)
            return out

        def psb(bank):
            return ps[:, bank, :].bitcast(BF16)

        fw.dma("sp", lambda e: e.dma_start(out=cst[:], in_=consts[:, :]), "const", writes=[bf("cst")])
        NCD = dict(allow_slow_non_contiguous=True)
        fw.dma("sp", lambda e: e.dma_start(out=n1w[:], in_=norm1_w.rearrange("(k p) -> p k", p=128), **NCD), "const", writes=[bf("n1w")])
        fw.dma("sp", lambda e: e.dma_start(out=ps5[:], in_=pool_scale.rearrange("(k p) -> p k", p=128), **NCD), "const", writes=[bf("ps5")])
        fw.dma("sp", lambda e: e.dma_start(out=mh4[:], in_=mhln_w.rearrange("(k p) -> p k", p=128), **NCD), "const", writes=[bf("mh4")])
        fw.dma("sp", lambda e: e.dma_start(out=m0T[:], in_=sm.rearrange("j h -> h j"), **NCD), "const", writes=[bf("m0T")])
        fw.dma("sp", lambda e: e.dma_start(out=bif[:], in_=b_if.rearrange("(t h) -> h t", h=4), **NCD), "const", writes=[bf("bif")])
        fw.dma("sp", lambda e: e.dma_start(out=sn_tok[:], in_=sn[:, :]), "const", writes=[bf("sn_tok")])
        fw.dma("pool", lambda e: e.dma_start(out=wg[:], in_=w_in[:, 7168:7176].rearrange("(k p) c -> p k c", p=128)),
               "wg", writes=[bf("wg")])
        fw.dma("pool", lambda e: e.dma_start(out=wp[:], in_=w_pool.rearrange("g (i p) d -> p g i d", p=128)),
               "wp", writes=[bf("wp")])
        fw.op("dve", lambda e: e.tensor_copy(out=ident[:], in_=cst[:, C_ID:C_ID + 128]), reads=[bf("cst")], writes=[bf("ident")])
        fw.op("dve", lambda e: e.memset(epsT[:], EPS), writes=[bf("epsT")])
        fw.op("dve", lambda e: e.tensor_scalar(out=ps5[:], in0=ps5[:], scalar1=0.5, scalar2=None, op0=ALU.mult),
              reads=[bf("ps5")], writes=[bf("ps5")])
        fw.op("dve", lambda e: e.tensor_scalar(out=mh4[:], in0=mh4[:], scalar1=0.25, scalar2=None, op0=ALU.mult),
              reads=[bf("mh4")], writes=[bf("mh4")])
        fw.op("pool", lambda e: e.memset(nnT[:], 0.0), writes=[bf("nnT")])

        identf = cst[:, C_ID:C_ID + 128]

        wstate = {"i": 0}

        def load_w(col0):
            i = wstate["i"] % NW
            wstate["i"] += 1
            t = wslot[i]
            b = bf(f"wslot{i}")
            fw.dma("pool", lambda e: e.dma_start(out=t[:], in_=w_in[:, col0:col0 + 256].rearrange("(k p) c -> p k c", p=128)),
                   f"w{i}", writes=[b])
            return t, b

        def load_x_tile(j, xt, xtb, key):
            c0, n = NTILES[j]
            segs = [(0, 16, meta, 0), (16, TP, xp, 16), (SOFF, TT, xs, SOFF)]
            for (a, b_, src, base) in segs:
                lo = max(c0, a)
                hi = min(c0 + n, b_)
                if lo < hi:
                    fw.dma("sp", lambda e, lo=lo, hi=hi, src=src, base=base: e.dma_start(
                        out=xt[lo - c0:hi - c0, :], in_=src[lo - base:hi - base, :]), key, writes=[xtb])

        def rstd_from_ss(stile, stb, col_ss, col_out, scale):
            fw.op("act", lambda e: e.activation(out=stile[:, col_out:col_out + 1], in_=stile[:, col_ss:col_ss + 1],
                                                func=AF.Ln, scale=scale, bias=epsT[:, 0:1]),
                  reads=[stb, bf("epsT")], writes=[stb])
            fw.op("act", lambda e: e.activation(out=stile[:, col_out:col_out + 1], in_=stile[:, col_out:col_out + 1],
                                                func=AF.Exp, scale=-0.5), reads=[stb], writes=[stb])

        cv = Carve()
        NX1 = 8
        xts = [cv.get([128, D]) for _ in range(NX1)]
        xbs = [cv.get([128, D], BF16) for _ in range(3)]
        xtB = phase_bufs([f"xt{i}" for i in range(NX1)])
        xbB = phase_bufs(["xb0", "xb1", "xb2"])
        def p1_front(j):
            c0, n = NTILES[j]
            xt, xtb = xts[j % NX1], xtB[j % NX1]
            xb, xbb = xbs[j % 3], xbB[j % 3]
            stile, stb = stt[j % 4], bf(f"stt{j % 4}")
            load_x_tile(j, xt, xtb, f"xt{j % NX1}")
            fw.op("pool", lambda e: e.memset(stile[:], 0.0), writes=[stb])
            fw.op("act", lambda e: e.activation(out=junk[0:n, :], in_=xt[0:n, :], func=AF.Square, accum_out=stile[0:n, 0:1]),
                  reads=[xtb, stb], writes=[bf("junk"), stb])
            rstd_from_ss(stile, stb, 0, 1, 1.0 / D)
            fw.op("dve", lambda e: e.tensor_scalar(out=xb[0:n, :], in0=xt[0:n, :], scalar1=stile[0:n, 1:2], scalar2=None, op0=ALU.mult),
                  reads=[xtb, stb], writes=[xbb])
            bank = 4 + (j % 2)
            pv = psb(bank).rearrange("p (k t) -> p k t", t=128)
            for k in range(8):
                fw.op("pe", lambda e, k=k: e.transpose(out=pv[:, k, 0:n], in_=xb[0:n, k * 128:(k + 1) * 128], identity=ident[0:n, 0:n]),
                      reads=[xbb, bf("ident")], writes=[PB[bank]], inc=(k == 7))

        def p1_back(j):
            c0, n = NTILES[j]
            bank = 4 + (j % 2)
            pv = psb(bank).rearrange("p (k t) -> p k t", t=128)
            fw.op("dve", lambda e: e.tensor_tensor(
                out=xnT[:, :, c0:c0 + n], in0=pv[:, :, 0:n], in1=n1w[:, :].unsqueeze(2).to_broadcast([128, 8, n]), op=ALU.mult),
                reads=[PB[bank], bf("n1w")], writes=[bf("xnT")])

        p1_front(0)
        for j in range(18):
            if j + 1 < 18:
                p1_front(j + 1)
            p1_back(j)

        def proj_fm(wt, wb, f0, nf, tb, bank, src=None):
            c0, n = tb
            for k in range(8):
                fw.op("pe", lambda e, k=k: e.matmul(out=ps[0:nf, bank, 0:n], lhsT=wt[:, k, f0:f0 + nf], rhs=xnT[:, k, c0:c0 + n],
                                                    start=(k == 0), stop=(k == 7)),
                      reads=[wb, bf("xnT")], writes=[PB[bank]], inc=(k == 7))

        def ytile(np_, blk):
            return yT[0:np_, blk:blk + 2, :].rearrange("p a t -> p (a t)").bitcast(F32)

        G_ig = ytile(4, 8)
        G_sp = ytile(4, 10)
        G_P = ytile(4, 12)
        G_gg = ytile(4, 14)
        G_Mx = ytile(4, 0)
        Qt = ytile(96, 10)
        G_t = G_ig
        for nm_ in ("G_ig", "G_sp", "G_P", "G_gg", "G_Mx"):
            B[nm_] = Buf(nm_)
        B["Qt"] = B["G_sp"]
        B["G_t"] = B["G_ig"]
        fw.deferred = []
        fw.op("dve", lambda e: e.tensor_scalar(out=nbf[:], in0=bif[:, 1:2], scalar1=-1.0, scalar2=None, op0=ALU.mult),
              reads=[bf("bif")], writes=[bf("nbf")])
        for ti, tb in enumerate(TBLK):
            c0, n = tb
            b0, b1 = (ti % 2) * 2, (ti % 2) * 2 + 1
            proj_fm(wg, bf("wg"), 0, 4, tb, b0)
            proj_fm(wg, bf("wg"), 4, 4, tb, b1)
            fw.op("dve", lambda e, b0=b0, c0=c0, n=n: e.tensor_scalar(out=G_ig[:, c0:c0 + n], in0=ps[0:4, b0, 0:n], scalar1=bif[:, 0:1],
                                                                      scalar2=None, op0=ALU.add),
                  reads=[PB[b0], bf("bif")], writes=[bf("G_ig")])
            fw.op("act", lambda e, b1=b1, c0=c0, n=n: e.activation(out=G_sp[:, c0:c0 + n], in_=ps[0:4, b1, 0:n], func=AF.Exp, scale=-1.0,
                                                                  bias=nbf[:, 0:1]),
                  reads=[PB[b1], bf("nbf")], writes=[bf("G_sp")])
        fw.op("act", lambda e: e.activation(out=G_sp[:], in_=G_sp[:], func=AF.Ln, bias=1.0), reads=[bf("G_sp")], writes=[bf("G_sp")])
        fw.op("dve", lambda e: e.tensor_tensor_scan(out=G_P[:, 0:TP], data0=G_sp[:, 0:TP], data1=G_sp[:, 0:TP], initial=0.0,
                                                    op0=ALU.add, op1=ALU.max),
              reads=[bf("G_sp")], writes=[bf("G_P")])
        for j in range(NSEQ):
            a = SOFF + 8 * j
            fw.op("dve", lambda e, a=a: e.tensor_tensor_scan(out=G_P[:, a:a + 8], data0=G_sp[:, a:a + 8], data1=G_sp[:, a:a + 8],
                                                             initial=0.0, op0=ALU.add, op1=ALU.max),
                  reads=[bf("G_sp")], writes=[bf("G_P")])
        fw.op("dve", lambda e: e.tensor_tensor(out=G_gg[:], in0=G_ig[:], in1=G_P[:], op=ALU.add),
              reads=[bf("G_ig"), bf("G_P")], writes=[bf("G_gg")])
        fw.op("dve", lambda e: e.tensor_tensor_scan(out=G_Mx[:, 0:TP], data0=G_gg[:, 0:TP], data1=G_gg[:, 0:TP], initial=0.0,
                                                    op0=ALU.max, op1=ALU.max),
              reads=[bf("G_gg")], writes=[bf("G_Mx")])
        for j in range(NSEQ):
            a = SOFF + 8 * j
            fw.op("dve", lambda e, a=a, j=j: e.tensor_tensor_scan(out=G_Mx[:, a:a + 8], data0=G_gg[:, a:a + 8], data1=G_gg[:, a:a + 8],
                                                                  initial=m0T[:, j:j + 1], op0=ALU.max, op1=ALU.max),
                  reads=[bf("G_gg"), bf("m0T")], writes=[bf("G_Mx")])

        def real3(t):
            return t[:, 16:TP].rearrange("p (c l) -> p c l", l=128)

        def samp3(t):
            return t[:, SOFF:TT].rearrange("p (c l) -> p c l", l=8)

        Rprev_real = G_Mx[:, 15:1936:128].unsqueeze(2).to_broadcast([4, 16, 128])
        Rend_real = G_Mx[:, 143:TP:128].unsqueeze(2).to_broadcast([4, 16, 128])
        Rprev_s = m0T[:, :].unsqueeze(2).to_broadcast([4, 16, 8])
        Rend_s = G_Mx[:, SOFF + 7:TT:8].unsqueeze(2).to_broadcast([4, 16, 8])
        Rend_meta = G_Mx[:, 15:16].to_broadcast([4, 16])

        def qrow(src, kind, row0, escale=1.0):
            rd = [bf("G_gg"), bf("G_P"), bf("G_Mx"), bf("m0T")]
            if kind == "prev":
                fw.op("dve", lambda e: e.tensor_copy(out=G_t[:, 0:16], in_=src[:, 0:16]), reads=rd, writes=[bf("G_t")])
                fw.op("dve", lambda e: e.tensor_tensor(out=real3(G_t), in0=real3(src), in1=Rprev_real, op=ALU.subtract), reads=rd, writes=[bf("G_t")])
                fw.op("dve", lambda e: e.tensor_tensor(out=samp3(G_t), in0=samp3(src), in1=Rprev_s, op=ALU.subtract), reads=rd, writes=[bf("G_t")])
            else:
                fw.op("dve", lambda e: e.tensor_tensor(out=G_t[:, 0:16], in0=src[:, 0:16], in1=Rend_meta, op=ALU.subtract), reads=rd, writes=[bf("G_t")])
                fw.op("dve", lambda e: e.tensor_tensor(out=real3(G_t), in0=real3(src), in1=Rend_real, op=ALU.subtract), reads=rd, writes=[bf("G_t")])
                fw.op("dve", lambda e: e.tensor_tensor(out=samp3(G_t), in0=samp3(src), in1=Rend_s, op=ALU.subtract), reads=rd, writes=[bf("G_t")])
            fw.op("act", lambda e: e.activation(out=Qt[row0:row0 + 4, :], in_=G_t[:], func=AF.Exp, scale=escale), reads=[bf("G_t")], writes=[bf("Qt")])

        fw.op("pool", lambda e: e.memset(Qt[:], 0.0), reads=[bf("G_P")], writes=[bf("Qt")])
        qrow(G_gg, "prev", 0)
        qrow(G_P, "prev", 32, 2.0)
        qrow(G_gg, "end", 64)
        rdm = [bf("G_Mx"), bf("m0T")]
        fw.op("dve", lambda e: e.tensor_scalar(out=Dall[:, 0:1], in0=G_Mx[:, 15:16], scalar1=-1.0, scalar2=None, op0=ALU.mult),
              reads=rdm, writes=[bf("Dall")])
        fw.op("dve", lambda e: e.tensor_tensor(out=Dall[:, 1:17], in0=G_Mx[:, 15:1936:128], in1=G_Mx[:, 143:TP:128], op=ALU.subtract),
              reads=rdm, writes=[bf("Dall")])
        fw.op("dve", lambda e: e.tensor_tensor(out=Dall[:, 17:33], in0=m0T[:, :], in1=G_Mx[:, SOFF + 7:TT:8], op=ALU.subtract),
              reads=rdm, writes=[bf("Dall")])
        fw.op("act", lambda e: e.activation(out=Dall[:], in_=Dall[:], func=AF.Exp), reads=[bf("Dall")], writes=[bf("Dall")])
        fw.op("dve", lambda e: e.tensor_tensor(out=msm[:, 0:1], in0=G_Mx[:, TP - 1:TP], in1=G_P[:, TP - 1:TP], op=ALU.subtract),
              reads=[bf("G_Mx"), bf("G_P")], writes=[bf("msm")])
        fw.op("dve", lambda e: e.tensor_tensor(out=msm[:, 1:17], in0=G_Mx[:, SOFF + 7:TT:8], in1=G_P[:, SOFF + 7:TT:8], op=ALU.subtract),
              reads=[bf("G_Mx"), bf("G_P")], writes=[bf("msm")])
        fw.dma("sp", lambda e: e.dma_start(out=m_prompt[:, :], in_=msm[:, 0:1]), "smallout", reads=[bf("msm")])
        fw.dma("sp", lambda e: e.dma_start(out=m_sample.rearrange("j h -> h j"), in_=msm[:, 1:17], **NCD), "smallout", reads=[bf("msm")])
        for ct, (c0, L) in enumerate(CHUNKS):
            bank = 4 + (ct % 2)
            fw.op("pe", lambda e, c0=c0, L=L, bank=bank: e.transpose(out=ps[0:L, bank, 0:96], in_=Qt[:, c0:c0 + L], identity=identf[0:96, 0:96]),
                  reads=[bf("Qt"), bf("cst")], writes=[PB[bank]])
            fw.op("act", lambda e, ct=ct, L=L, bank=bank: e.activation(
                out=tokQ[0:L, ct, :].rearrange("p (a b) -> p a b", b=4),
                in_=ps[0:L, bank, 0:96].rearrange("p (a b) -> p a b", b=32)[:, :, 0:4], func=AF.Copy),
                  reads=[PB[bank]], writes=[bf("tokQ")])
        for h in range(4):
            bank = 6 + (h % 2)
            fw.op("pe", lambda e, h=h, bank=bank: e.matmul(out=ps[:, bank, 0:33], lhsT=cst[0:4, C_SEL + 128 * h:C_SEL + 128 * (h + 1)],
                                                           rhs=Dall[:, :], start=True, stop=True),
                  reads=[bf("Dall"), bf("cst")], writes=[PB[bank]])
            fw.op("act", lambda e, h=h, bank=bank: e.activation(out=dec_bc[:, h, :], in_=ps[:, bank, 0:33], func=AF.Copy),
                  reads=[PB[bank]], writes=[bf("dec_bc")])
        for dc in range(2):
            bank = 6 + dc
            fw.op("pe", lambda e, dc=dc, bank=bank: e.transpose(out=ps[:, bank, 0:64], in_=sn_tok[:, dc * 128:(dc + 1) * 128],
                                                                identity=identf[0:64, 0:64]),
                  reads=[bf("sn_tok"), bf("cst")], writes=[PB[bank]])
            fw.op("act", lambda e, dc=dc, bank=bank: e.activation(out=nT_f[:, dc, :], in_=ps[:, bank, 0:64], func=AF.Copy),
                  reads=[PB[bank]], writes=[bf("nT_f")])
        fw.op("dve", lambda e: e.tensor_copy(out=nT_b[:], in_=nT_f[:]), reads=[bf("nT_f")], writes=[bf("nT_b")])
        p2_items = fw.deferred
        fw.deferred = None
        p2_pending = set()
        last_pe_open = [False]

        def p2_release(n=1):
            k_ = 0
            while p2_items:
                if k_ >= n and not p2_pending and not last_pe_open[0]:
                    break
                it_ = p2_items.pop(0)
                fw.replay(it_)
                if it_[0] == "op":
                    _, e_, _rec, rd_, wr_, inc_ = it_
                    for b_ in rd_:
                        if b_.psum:
                            p2_pending.discard(b_.name)
                    for b_ in wr_:
                        if b_.psum:
                            p2_pending.add(b_.name)
                    last_pe_open[0] = (e_ == "pe" and not inc_)
                if not p2_pending and not last_pe_open[0]:
                    k_ += 1

        def p2_drain():
            p2_release(10 ** 9)
            for nm_ in ("G_ig", "G_sp", "G_P", "G_gg", "G_Mx"):
                bf("yT").r.extend(B[nm_].w + B[nm_].r)

        cv = Carve()
        u_ = [cv.get([128, UW]) for _ in range(2)]
        Aa = cv.get([128, UW])
        Ab = cv.get([128, UW])
        pooled_ = [cv.get([128, 2, TT], BF16) for _ in range(2)]
        sp_tok = cv.get([120, 2, D])
        th = cv.get([128, 512])
        szt = cv.get([128, 512], BF16)
        pp_stage = cv.get([16, 256])
        ps_stage = cv.get([128, D])
        snc = cv.get([128, 128])
        (uB0, uB1, AaB, AbB, pooledB0, pooledB1, sptB, thB, sztB, ppB, pssB, sncB) = phase_bufs(
            ["u0", "u1", "Aa", "Ab", "pooledT0", "pooledT1", "sp_tok", "th", "szt", "pp_stage", "ps_stage", "snc"])
        pooledB_ = [pooledB0, pooledB1]
        uB_ = [uB0, uB1]
        for t in range(2):
            fw.dma("sp", lambda e, t=t: e.dma_start(out=sp_tok[:, t, :], in_=spool[8 * t:8 * t + 8].rearrange("b r c -> (b r) c")),
                   "sptok", writes=[sptB])
        for i_ in range(2):
            fw.op("pool", lambda e, i_=i_: e.memset(u_[i_][:, 0:UP0], 0.0), writes=[uB_[i_]])
        fw.op("pool", lambda e: e.memset(Aa[:, 0:16], 0.0), writes=[AaB])
        fw.op("pool", lambda e: e.memset(Ab[:, 0:16], 0.0), writes=[AbB])
        fw.dma("sp", lambda e: e.dma_start(out=pool_sample[:, 0:7, :], in_=spool[:, 8:15, :]), "smallout")

        def snew(t):
            return bass.AP(t.tensor, t.offset + US0 + 15, [list(t.ap[0]), [23, 16], [1, 8]])

        def sprev(t, half):
            return bass.AP(t.tensor, t.offset + US0 + 23 * 8 * half, [list(t.ap[0]), [23, 8], [1, 15]])

        rot = {"b": 0}

        def nbank():
            b = rot["b"] % 4
            rot["b"] += 1
            return b

        tmp16 = cv.get([128, 16])
        (t16B,) = phase_bufs(["tmp16"])
        wts = {}

        def stage_A(g, ib):
            cb = 2 * g + ib
            u, uB = u_[ib], uB_[ib]
            su, sub = wts[g][0], wts[g][1]
            for t in range(2):
                bank = 4 + t
                fw.op("pe", lambda e, t=t: e.transpose(out=ps[:, bank, 0:120], in_=sp_tok[:, t, cb * 128:(cb + 1) * 128],
                                                       identity=identf[0:120, 0:120]),
                      reads=[sptB, bf("cst")], writes=[PB[bank]])
                fw.op("act", lambda e, t=t: e.activation(out=sprev(u, t), in_=ps[:, bank, 0:120].rearrange("p (b r) -> p b r", r=15), func=AF.Copy),
                      reads=[PB[bank]], writes=[uB])
            for tb in TBLK:
                c0, n = tb
                bank = nbank()
                proj_fm(su, sub, ib * 128, 128, tb, bank)
                npr = min(c0 + n, TP) - c0
                fw.op("act", lambda e: e.activation(out=u[:, UP0 + c0:UP0 + c0 + npr], in_=ps[:, bank, 0:npr], func=AF.Copy),
                      reads=[PB[bank]], writes=[uB])
                if c0 + n > TP:
                    fw.op("act", lambda e: e.activation(out=snew(u), in_=ps[:, bank, npr:npr + NS].rearrange("p (b r) -> p b r", r=8), func=AF.Copy),
                          reads=[PB[bank]], writes=[uB])
                    fw.op("act", lambda e: e.activation(out=snc[:, :], in_=ps[:, bank, npr:npr + NS], func=AF.Copy),
                          reads=[PB[bank]], writes=[sncB])
                p2_release(P2N)
            fw.op("pe", lambda e: e.transpose(out=ps[0:15, 6, 0:128], in_=u[:, UP0 + TP - 15:UP0 + TP], identity=identf),
                  reads=[uB, bf("cst")], writes=[PB[6]])
            pcol = (cb % 2) * 128
            fw.op("act", lambda e: e.activation(out=pp_stage[0:15, pcol:pcol + 128], in_=ps[0:15, 6, 0:128], func=AF.Copy),
                  reads=[PB[6]], writes=[ppB])
            fw.dma("sp", lambda e: e.dma_start(out=pool_prompt[:, cb * 128:(cb + 1) * 128], in_=pp_stage[0:15, pcol:pcol + 128]),
                   "ppout", reads=[ppB])
            fw.op("pe", lambda e: e.transpose(out=ps[:, 7, 0:128], in_=snc[:, :], identity=identf),
                  reads=[sncB, bf("cst")], writes=[PB[7]])
            fw.op("act", lambda e: e.activation(out=ps_stage[:, cb * 128:(cb + 1) * 128], in_=ps[:, 7, 0:128], func=AF.Copy),
                  reads=[PB[7]], writes=[pssB])

        def stage_B(g, ib):
            ops = []
            w = 2 ** (g + 1)
            pooledT, pooledB = pooled_[gpar[g]], pooledB_[gpar[g]]
            u, uB = u_[ib], uB_[ib]
            src, srcB = u, uB
            dsts = [(Aa, AaB), (Ab, AbB)]
            for lvl in range(g + 1):
                sh = 2 ** lvl
                dst, dstB = dsts[lvl % 2]
                ops.append(lambda src=src, dst=dst, sh=sh, srcB=srcB, dstB=dstB: fw.op(
                    "dve", lambda e: e.tensor_tensor(out=dst[:, 16:UW], in0=src[:, 16:UW], in1=src[:, 16 - sh:UW - sh], op=ALU.add),
                    reads=[srcB], writes=[dstB]))
                src, srcB = dst, dstB
            A, AB = src, srcB

            def tail():
                fw.op("dve", lambda e: e.scalar_tensor_tensor(
                    out=pooledT[:, ib, 0:TP], in0=A[:, UP0:UP0 + TP], scalar=1.0 / w, in1=u[:, UP0:UP0 + TP], op0=ALU.mult, op1=ALU.subtract),
                    reads=[AB, uB], writes=[pooledB])
                fw.op("dve", lambda e: e.scalar_tensor_tensor(
                    out=pooledT[:, ib, SOFF:TT].rearrange("p (b r) -> p b r", r=8), in0=snew(A), scalar=1.0 / w, in1=snew(u),
                    op0=ALU.mult, op1=ALU.subtract), reads=[AB, uB], writes=[pooledB])
                fw.op("dve", lambda e: e.tensor_tensor(out=tmp16[:, 0:16], in0=A[:, UP0:UP0 + 16],
                                                       in1=cst[:, C_RC + 16 * g:C_RC + 16 * (g + 1)], op=ALU.mult),
                      reads=[AB, bf("cst")], writes=[t16B])
                fw.op("dve", lambda e: e.tensor_tensor(out=pooledT[:, ib, 0:16], in0=tmp16[:, 0:16], in1=u[:, UP0:UP0 + 16], op=ALU.subtract),
                      reads=[t16B, uB], writes=[pooledB])
            ops.append(tail)
            return ops

        rot6 = {"i": 0}

        def nbank6():
            b_ = (0, 1, 2, 3, 6, 7)[rot6["i"] % 6]
            rot6["i"] += 1
            return b_

        def stage_C(g, filler=()):
            filler = list(filler)
            nunits = 10
            per = [len(filler) * (i + 1) // nunits - len(filler) * i // nunits for i in range(nunits)]
            unit_i = 0
            sz, szb = wts[g][2], wts[g][3]
            pooledT, pooledB = pooled_[gpar[g]], pooledB_[gpar[g]]
            for ob in range(2):
                cb = 2 * g + ob
                for tb in TBLK:
                    c0, n = tb
                    bm = nbank6()
                    for ib in range(2):
                        fw.op("pe", lambda e, ib=ib: e.matmul(
                            out=ps[:, bm, 0:n], lhsT=wp[:, g, ib, ob * 128:(ob + 1) * 128], rhs=pooledT[:, ib, c0:c0 + n],
                            start=(ib == 0), stop=(ib == 1)), reads=[bf("wp"), pooledB], writes=[PB[bm]], inc=(ib == 1))
                    bz = nbank6()
                    proj_fm(sz, szb, ob * 128, 128, tb, bz)
                    fw.op("act", lambda e: e.activation(out=th[:, 0:n], in_=ps[:, bz, 0:n], func=AF.Tanh, scale=0.5),
                          reads=[PB[bz]], writes=[thB])
                    fw.op("dve", lambda e: e.scalar_tensor_tensor(
                        out=szt[:, 0:n], in0=th[:, 0:n], scalar=1.0, in1=ps[:, bz, 0:n], op0=ALU.add, op1=ALU.mult),
                        reads=[thB, PB[bz]], writes=[sztB])
                    fw.op("dve", lambda e: e.scalar_tensor_tensor(
                        out=yT[:, cb, c0:c0 + n], in0=ps[:, bm, 0:n], scalar=ps5[:, cb:cb + 1], in1=szt[:, 0:n],
                        op0=ALU.mult, op1=ALU.mult), reads=[PB[bm], bf("ps5"), sztB], writes=[bf("yT")])
                    for _ in range(per[unit_i]):
                        filler.pop(0)()
                    unit_i += 1
                    p2_release(P2N)
            assert not filler

        GORDER = [1, 3, 2, 0]
        P2N = 2
        gpar = {g: i % 2 for i, g in enumerate(GORDER)}
        prev_g = None
        for g in GORDER:
            su, sub = load_w(256 * g)
            sz, szb = load_w(1024 + 256 * g)
            wts[g] = (su, sub, sz, szb)
            stage_A(g, 0)
            for o_ in stage_B(g, 0):
                o_()
            stage_A(g, 1)
            bops = stage_B(g, 1)
            if prev_g is not None:
                stage_C(prev_g, bops)
            else:
                for o_ in bops:
                    o_()
            prev_g = g
        p2_drain()
        stage_C(prev_g)
        for j in range(NSEQ):
            fw.dma("sp", lambda e, j=j: e.dma_start(out=pool_sample[j, 7:15, :], in_=ps_stage[8 * j:8 * j + 8, :]), "smallout", reads=[pssB])

        cv = Carve()
        qT = cv.get([128, 2, TT], BF16)
        kT = cv.get([128, 2, TT], BF16)
        gate_tmp_off = cv.off
        tho = [cv.get([128, 512]) for _ in range(2)]
        thz = [cv.get([128, 512]) for _ in range(2)]
        t1b = [cv.get([128, 512]) for _ in range(2)]
        zsb = [cv.get([128, 512]) for _ in range(2)]
        NV, NK, NS_, NCB, NH = 4, 4, 4, 3, 3
        vaug = [cv.get([128, 258], BF16) for _ in range(NV)]
        kw = [cv.get([128, 256], BF16) for _ in range(NK)]
        sTm = [cv.get([128, 128], BF16) for _ in range(NS_)]
        hn = [cv.get([128, 256], BF16) for _ in range(NH)]
        C_st = cv.get([128, 2, 257])
        C_bf = [cv.get([128, 2, 258], BF16) for _ in range(NCB)]
        zq = cv.get([128, 2, 1024], BF16)
        ktok_s = cv.get([128, 256], BF16)
        Wm = cv.get([128, 16])
        NCF, NCS = 6, 2
        Cf = [cv.get([128, 2, 256]) for _ in range(NCF)]
        Csb = [cv.get([128, 2, 256], BF16) for _ in range(NCS)]
        nn_tok = cv.get([64, 256])
        NHR = 6
        hraw = [cv.get([128, 256]) for _ in range(NHR)]
        names = ([f"hraw{i}" for i in range(NHR)] + ["qT", "kT", "zsb0", "zsb1"] + [f"tho{i}" for i in range(2)] + [f"thz{i}" for i in range(2)] + [f"t1b{i}" for i in range(2)]
                 + [f"vaug{i}" for i in range(NV)] + [f"kw{i}" for i in range(NK)] + [f"sTm{i}" for i in range(NS_)]
                 + [f"hn{i}" for i in range(NH)] + [f"C_bf{i}" for i in range(NCB)]
                 + ["C_st", "zq", "ktok_s", "Wm"] + [f"Cf{i}" for i in range(NCF)] + [f"Csb{i}" for i in range(NCS)]
                 + ["nn_tok"])
        phase_bufs(names)
        for i in range(NV):
            fw.op("pool", lambda e, i=i: e.memset(vaug[i][:, 256:258], 1.0), writes=[bf(f"vaug{i}")])
        fw.op("pool", lambda e: e.memset(zq[:], 0.0), writes=[bf("zq")])
        zq_diag = bass.AP(zq.tensor, zq.offset, [list(zq.ap[0]), [1024, 2], [136, 8], [1, 8]])

        wout = xnT[:].rearrange("p k t -> p (k t)")[:, 0:16 * D].rearrange("p (k c) -> p k c", c=D)
        woB = bf("xnT")

        woQ = [Buf(f"woq{i}") for i in range(4)]

        def load_wout():
            for q4 in range(4):
                fw.dma("pool", lambda e, q4=q4: e.dma_start(out=wout[:, 4 * q4:4 * q4 + 4, :],
                                                            in_=w_out[512 * q4:512 * (q4 + 1), :].rearrange("(k p) c -> p k c", p=128)),
                       f"wout{q4}", writes=[woB, woQ[q4]])

        cnt = {"st": 0, "cf": 0, "co": 0, "kj": 0}
        SLAST = NCH - 1

        def head_program(h, drain_prev):
            wq, wqb = load_w(2048 + 256 * h)
            wk, wkb = load_w(3072 + 256 * h)
            wo, wob = load_w(5120 + 256 * h)
            wz, wzb = load_w(6144 + 256 * h)
            def qk_bank():
                b_ = (0, 1, 3, 4, 5, 6, 7)[rotqk["i"] % 7]
                rotqk["i"] += 1
                return b_
            for dc in range(2):
                for tb in TBLK:
                    c0, n = tb
                    bank = qk_bank()
                    proj_fm(wq, wqb, dc * 128, 128, tb, bank)
                    fw.op("act", lambda e: e.activation(out=qT[:, dc, c0:c0 + n], in_=ps[:, bank, 0:n], func=AF.Copy),
                          reads=[PB[bank]], writes=[bf("qT")])
                    bank = qk_bank()
                    proj_fm(wk, wkb, dc * 128, 128, tb, bank)
                    fw.op("act", lambda e: e.activation(out=kT[:, dc, c0:c0 + n], in_=ps[:, bank, 0:n], func=AF.Copy, scale=1.0 / 16),
                          reads=[PB[bank]], writes=[bf("kT")])
                    if drain_prev:
                        drain_prev.pop(0)()
            while drain_prev:
                drain_prev.pop(0)()
            wv, wvb = load_w(4096 + 256 * h)
            for dc in range(2):
                yb = 8 + 2 * h + dc
                for ti, tb in enumerate(TBLK):
                    c0, n = tb
                    i2 = ti % 2
                    bo = rotqk["g"] % 8
                    bz = (rotqk["g"] + 1) % 8
                    rotqk["g"] += 2
                    proj_fm(wo, wob, dc * 128, 128, tb, bo)
                    proj_fm(wz, wzb, dc * 128, 128, tb, bz)
                    fw.op("act", lambda e: e.activation(out=tho[i2][:, 0:n], in_=ps[:, bo, 0:n], func=AF.Tanh, scale=0.5),
                          reads=[PB[bo]], writes=[bf(f"tho{i2}")])
                    fw.op("act", lambda e: e.activation(out=thz[i2][:, 0:n], in_=ps[:, bz, 0:n], func=AF.Tanh, scale=0.5),
                          reads=[PB[bz]], writes=[bf(f"thz{i2}")])
                    fw.op("act", lambda e: e.activation(out=zsb[i2][:, 0:n], in_=ps[:, bz, 0:n], func=AF.Copy, scale=mh4[:, 2 * h + dc:2 * h + dc + 1]),
                          reads=[PB[bz], bf("mh4")], writes=[bf(f"zsb{i2}")])
                    fw.op("dve", lambda e: e.scalar_tensor_tensor(out=t1b[i2][:, 0:n], in0=thz[i2][:, 0:n], scalar=1.0,
                                                                  in1=zsb[i2][:, 0:n], op0=ALU.add, op1=ALU.mult),
                          reads=[bf(f"thz{i2}"), bf(f"zsb{i2}")], writes=[bf(f"t1b{i2}")])
                    fw.op("dve", lambda e: e.tensor_tensor(out=tho[i2][:, 0:n], in0=tho[i2][:, 0:n], in1=t1b[i2][:, 0:n], op=ALU.mult),
                          reads=[bf(f"tho{i2}"), bf(f"t1b{i2}")], writes=[bf(f"tho{i2}")])
                    fw.op("pool", lambda e: e.tensor_tensor(out=yT[:, yb, c0:c0 + n], in0=tho[i2][:, 0:n], in1=t1b[i2][:, 0:n], op=ALU.add),
                          reads=[bf(f"tho{i2}"), bf(f"t1b{i2}")], writes=[bf("yT")])
            fw.op("pool", lambda e: e.memset(C_st[:], 0.0), writes=[bf("C_st")])
            fw.op("dve", lambda e: e.tensor_scalar(out=Wm[:], in0=cst[:, C_BDS:C_BDS + 16], scalar1=tokQ[:, SLAST, 8 + h:9 + h], scalar2=None,
                                                   op0=ALU.mult), reads=[bf("cst"), bf("tokQ")], writes=[bf("Wm")])
            P0a = PB[0]
            P2a = PB[0]
            P0b = PB[2]
            P2b = PB[2]
            P2c = PB[2]
            pk = psb(0)[:, 512:768]
            ph_p = psb(2)[:, 256:512].rearrange("p (d t) -> p d t", t=128)
            ph_s = psb(2)[:, 520:776].rearrange("p (d t) -> p d t", t=128)
            NP = SLAST

            pre_v = (h == 3)
            if pre_v:
                vall = arena[:, gate_tmp_off:gate_tmp_off + NCH * 129].bitcast(BF16).rearrange("p (c e) -> p c e", e=258)
                vallB = Buf("vall")
                for nm_ in ("tho0", "tho1", "thz0", "thz1", "t1b0", "t1b1", "zsb0", "zsb1"):
                    vallB.w.extend(B[nm_].w + B[nm_].r)
                fw.op("pool", lambda e: e.memset(vall[:, :, 256:258], 1.0), writes=[vallB])
                for ct_ in range(NCH):
                    c0_, L_ = CHUNKS[ct_]
                    bk_ = nbank()
                    for k in range(8):
                        fw.op("pe", lambda e, k=k: e.matmul(out=ps[0:L_, bk_, 0:256], lhsT=xnT[:, k, c0_:c0_ + L_], rhs=wv[:, k, :],
                                                            start=(k == 0), stop=(k == 7)),
                              reads=[bf("xnT"), wvb], writes=[PB[bk_]], inc=(k == 7))
                    fw.op("act", lambda e: e.activation(out=vall[0:L_, ct_, 0:256], in_=ps[0:L_, bk_, 0:256], func=AF.Copy),
                          reads=[PB[bk_]], writes=[vallB])
                load_wout()

            def slot_v(ct):
                if pre_v:
                    return vall[:, ct, :], vallB
                i = NV - 1 if ct == SLAST else ct % (NV - 1)
                return vaug[i], bf(f"vaug{i}")

            def slot_s(ct):
                i = NS_ - 1 if ct == SLAST else ct % (NS_ - 1)
                return sTm[i], bf(f"sTm{i}")

            def nbank_of(ct):
                return 1 if ct == SLAST else 4 + (ct % 2)

            def PE_A(ct):
                c0, L = CHUNKS[ct]
                for k in range(8):
                    if pre_v:
                        break
                    fw.op("pe", lambda e, k=k: e.matmul(out=ps[0:L, 0, 0:256], lhsT=xnT[:, k, c0:c0 + L], rhs=wv[:, k, :],
                                                        start=(k == 0), stop=(k == 7)),
                          reads=[bf("xnT"), wvb], writes=[P0a], inc=(k == 7))
                for dc in range(2):
                    fw.op("pe", lambda e, dc=dc: e.transpose(out=pk[0:L, dc * 128:(dc + 1) * 128], in_=kT[:, dc, c0:c0 + L], identity=ident[:, :]),
                          reads=[bf("kT"), bf("ident")], writes=[P2a], inc=(dc == 1))
                for dc in range(2):
                    fw.op("pe", lambda e, dc=dc: e.matmul(out=ps[0:L, 2, 0:L], lhsT=kT[:, dc, c0:c0 + L], rhs=qT[:, dc, c0:c0 + L],
                                                          start=(dc == 0), stop=(dc == 1)),
                          reads=[bf("kT"), bf("qT")], writes=[P0b], inc=(dc == 1))

            def EV_A(ct):
                c0, L = CHUNKS[ct]
                is_s = (ct == SLAST)
                va, vaB = slot_v(ct)
                if not pre_v:
                    fw.op("act", lambda e: e.activation(out=va[0:L, 0:256], in_=ps[0:L, 0, 0:256], func=AF.Copy), reads=[P0a], writes=[vaB])
                if not is_s:
                    kwt, kwB = kw[ct % 2], bf(f"kw{ct % 2}")
                    fw.op("act", lambda e: e.activation(out=kwt[0:L, :], in_=pk[0:L, 0:256], func=AF.Copy, scale=tokQ[0:L, ct, 8 + h:9 + h]),
                          reads=[P2a, bf("tokQ")], writes=[kwB])
                else:
                    fw.op("act", lambda e: e.activation(out=ktok_s[:, :], in_=pk[:, 0:256], func=AF.Copy), reads=[P2a], writes=[bf("ktok_s")])
                mcol = C_BD if is_s else C_CM
                sm_, smB = slot_s(ct)
                fw.op("dve", lambda e: e.scalar_tensor_tensor(out=sm_[0:L, 0:L], in0=ps[0:L, 2, 0:L], scalar=tokQ[0:L, ct, h:h + 1],
                                                              in1=cst[0:L, mcol:mcol + L], op0=ALU.mult, op1=ALU.mult),
                      reads=[P0b, bf("tokQ"), bf("cst")], writes=[smB])

            def PE_U(ct):
                c0, L = CHUNKS[ct]
                va, vaB = slot_v(ct)
                kwt, kwB = kw[ct % 2], bf(f"kw{ct % 2}")
                for dc in range(2):
                    fw.op("pe", lambda e, dc=dc: e.matmul(out=ps[:, 6 + dc, 0:257], lhsT=kwt[0:L, dc * 128:(dc + 1) * 128], rhs=va[0:L, 0:257],
                                                          start=True, stop=True),
                          reads=[kwB, vaB], writes=[PB[6 + dc]], inc=True)

            def ST(ct):
                fw.op("dve", lambda e: e.scalar_tensor_tensor(out=C_st[:], in0=C_st[:], scalar=dec_bc[:, h, ct:ct + 1], in1=ps[:, 6:8, 0:257],
                                                              op0=ALU.mult, op1=ALU.add),
                      reads=[bf("C_st"), bf("dec_bc"), PB[6], PB[7]], writes=[bf("C_st")])
                if ct < NP - 1:
                    cb_, cbB = C_bf[ct % NCB], bf(f"C_bf{ct % NCB}")
                    fw.op("dve", lambda e: e.tensor_copy(out=cb_[:, :, 0:257], in_=C_st[:]), reads=[bf("C_st")], writes=[cbB])
                else:
                    fw.dma("sp", lambda e: e.dma_start(out=C_prompt[h].rearrange("(dc p) e -> p dc e", p=128), in_=C_st[:, :, 0:256]),
                           "smallout", reads=[bf("C_st")])
                    fw.dma("sp", lambda e: e.dma_start(out=n_prompt[h].rearrange("(dc p o) -> p dc o", p=128, o=1), in_=C_st[:, :, 256:257], **NCD),
                           "smallout", reads=[bf("C_st")])

            def NR(ct):
                c0, L = CHUNKS[ct]
                nb_ = nbank_of(ct)
                va, vaB = slot_v(ct)
                sm_, smB = slot_s(ct)
                last_only = (ct == 0)
                stop_ = last_only
                fw.op("pe", lambda e: e.matmul(out=ps[0:L, nb_, 0:257], lhsT=sm_[0:L, 0:L], rhs=va[0:L, 0:257], start=True, stop=stop_),
                      reads=[smB, vaB], writes=[PB[nb_]], inc=True)
                if ct > 0 and ct != SLAST:
                    cb_, cbB = C_bf[(ct - 1) % NCB], bf(f"C_bf{(ct - 1) % NCB}")
                    for dc in range(2):
                        fw.op("pe", lambda e, dc=dc: e.matmul(out=ps[0:L, nb_, 0:257], lhsT=qT[:, dc, c0:c0 + L], rhs=cb_[:, dc, 0:257],
                                                              start=False, stop=(dc == 1)),
                              reads=[bf("qT"), cbB], writes=[PB[nb_]], inc=(dc == 1))

            def issue_loads(j):
                ci = j % NCF
                fw.dma("sp", lambda e: e.dma_start(out=Cf[ci][:], in_=sC[j, h].rearrange("(dc p) e -> p dc e", p=128)),
                       f"cf{ci}", writes=[bf(f"Cf{ci}")])

            def zq_refresh(half):
                zd_new = bass.AP(zq.tensor, zq.offset + 64 * half, [list(zq.ap[0]), [1024, 2], [136, 8], [1, 8]])
                zd_old = bass.AP(zq.tensor, zq.offset + 64 * (1 - half), [list(zq.ap[0]), [1024, 2], [136, 8], [1, 8]])
                fw.op("dve", lambda e: e.memset(zd_old, 0.0), writes=[bf("zq")])
                fw.op("dve", lambda e: e.tensor_copy(
                    out=zd_new, in_=qT[:, :, SOFF + 64 * half:SOFF + 64 * half + 64].rearrange("p d (b r) -> p d b r", r=8)),
                    reads=[bf("qT")], writes=[bf("zq")])

            def T0(j):
                ci, si_ = j % NCF, j % NCS
                fw.op("act", lambda e: e.activation(out=Csb[si_][:], in_=Cf[ci][:], func=AF.Copy), reads=[bf(f"Cf{ci}")], writes=[bf(f"Csb{si_}")])
                kj = 2 + (j % 2)
                fw.op("act", lambda e: e.activation(out=kw[kj][:, :], in_=ktok_s[:, :], func=AF.Copy, scale=Wm[:, j:j + 1]),
                      reads=[bf("ktok_s"), bf("Wm")], writes=[bf(f"kw{kj}")])

            def T1(j):
                va, vaB = slot_v(SLAST)
                si_ = j % NCS
                kj = 2 + (j % 2)
                jj = j % 8
                for dc in range(2):
                    fw.op("pe", lambda e, dc=dc: e.matmul(out=ps[:, 1, 0:256], lhsT=zq[:, dc, jj * 128:(jj + 1) * 128],
                                                          rhs=Csb[si_][:, dc, :], start=False, stop=False),
                          reads=[bf("zq"), bf(f"Csb{si_}")], writes=[PB[1]], inc=False)
                    lastmm = (j == NSEQ - 1 and dc == 1)
                    fw.op("pe", lambda e, dc=dc, lastmm=lastmm: e.matmul(
                        out=ps[:, 1, 256:257], lhsT=zq[:, dc, jj * 128:(jj + 1) * 128], rhs=nT_b[:, dc, 4 * j + h:4 * j + h + 1],
                        start=False, stop=lastmm), reads=[bf("zq"), bf("nT_b")], writes=[PB[1]], inc=True)
                for dc in range(2):
                    fw.op("pe", lambda e, dc=dc: e.matmul(out=ps[:, 3, dc * 256:(dc + 1) * 256], lhsT=kw[kj][:, dc * 128:(dc + 1) * 128],
                                                          rhs=va[:, 0:256], start=True, stop=True),
                          reads=[bf(f"kw{kj}"), vaB], writes=[PB[3]], inc=(dc == 1))
                for dc in range(2):
                    fw.op("pe", lambda e, dc=dc: e.matmul(out=ps[:, 2, 256 + dc:257 + dc], lhsT=kw[kj][:, dc * 128:(dc + 1) * 128],
                                                          rhs=va[:, 256:257], start=True, stop=True),
                          reads=[bf(f"kw{kj}"), vaB], writes=[P2c], inc=(dc == 1))

            def T2(j):
                ci = j % NCF
                fw.op("dve", lambda e: e.scalar_tensor_tensor(
                    out=nnT[:, :, 4 * j + h], in0=nT_f[:, :, 4 * j + h], scalar=dec_bc[:, h, 17 + j:18 + j],
                    in1=ps[:, 2, 256:258], op0=ALU.mult, op1=ALU.add),
                    reads=[bf("nT_f"), bf("dec_bc"), P2c], writes=[bf("nnT")])
                fw.op("dve", lambda e: e.scalar_tensor_tensor(
                    out=Cf[ci][:], in0=Cf[ci][:], scalar=dec_bc[:, h, 17 + j:18 + j], in1=ps[:, 3, :].rearrange("p (d e) -> p d e", e=256),
                    op0=ALU.mult, op1=ALU.add),
                    reads=[bf(f"Cf{ci}"), bf("dec_bc"), PB[3]], writes=[bf(f"Cf{ci}")])
                fw.dma("sp", lambda e: e.dma_start(out=C_sample[j, h].rearrange("(dc p) e -> p dc e", p=128), in_=Cf[ci][:]),
                       f"cf{ci}", reads=[bf(f"Cf{ci}")])

            hstate = {}

            def hr_slot(ct):
                i = NHR - 1 if ct == SLAST else ct % (NHR - 1)
                return hraw[i], bf(f"hraw{i}")

            def H1(ct):
                c0, L = CHUNKS[ct]
                bank = nbank_of(ct)
                si = cnt["st"] % 8
                cnt["st"] += 1
                stile, stb = stt[si], bf(f"stt{si}")
                hstate[ct] = (stile, stb)
                hr, hrB = hr_slot(ct)
                P_ = PB[bank]
                fw.op("pool", lambda e: e.memset(stile[:], 0.0), writes=[stb])
                fw.op("act", lambda e: e.activation(out=stile[0:L, 0:1], in_=ps[0:L, bank, 256:257], func=AF.Square), reads=[P_], writes=[stb])
                fw.op("act", lambda e: e.activation(out=hr[0:L, :], in_=ps[0:L, bank, 0:256], func=AF.Identity, scale=1.0 / 256,
                                                    accum_out=stile[0:L, 1:2]), reads=[P_, stb], writes=[stb, hrB])
                fw.op("act", lambda e: e.activation(out=junk[0:L, 0:256], in_=ps[0:L, bank, 0:256], func=AF.Square, scale=1.0 / 4096,
                                                    accum_out=stile[0:L, 2:3]), reads=[P_, stb], writes=[stb, bf("junk")])

            def H2(ct):
                c0, L = CHUNKS[ct]
                stile, stb = hstate[ct]
                fw.op("dve", lambda e: e.tensor_scalar(out=stile[0:L, 8:9], in0=stile[0:L, 1:2], scalar1=1.0 / 256, scalar2=None, op0=ALU.mult),
                      reads=[stb], writes=[stb])
                fw.op("dve", lambda e: e.tensor_tensor(out=stile[0:L, 3:4], in0=stile[0:L, 0:1], in1=tokQ[0:L, ct, 4 + h:5 + h], op=ALU.max),
                      reads=[stb, bf("tokQ")], writes=[stb])
                fw.op("dve", lambda e: e.scalar_tensor_tensor(out=stile[0:L, 4:5], in0=stile[0:L, 8:9], scalar=stile[0:L, 8:9], in1=stile[0:L, 2:3],
                                                              op0=ALU.mult, op1=ALU.subtract), reads=[stb], writes=[stb])
                fw.op("dve", lambda e: e.scalar_tensor_tensor(out=stile[0:L, 5:6], in0=stile[0:L, 3:4], scalar=EPS / 65536.0, in1=stile[0:L, 4:5],
                                                              op0=ALU.mult, op1=ALU.subtract), reads=[stb], writes=[stb])

            def H3(ct):
                c0, L = CHUNKS[ct]
                stile, stb = hstate[ct]
                fw.op("act", lambda e: e.activation(out=stile[0:L, 6:7], in_=stile[0:L, 5:6], func=AF.Ln), reads=[stb], writes=[stb])
                fw.op("act", lambda e: e.activation(out=stile[0:L, 7:8], in_=stile[0:L, 6:7], func=AF.Exp, scale=-0.5), reads=[stb], writes=[stb])

            def H4(ct):
                c0, L = CHUNKS[ct]
                stile, stb = hstate[ct]
                hr, hrB = hr_slot(ct)
                hi_ = (NH - 1) if ct == SLAST else ct % (NH - 1)
                hslot, hB = hn[hi_], bf(f"hn{hi_}")
                fw.op("dve", lambda e: e.tensor_scalar(out=hslot[0:L, :], in0=hr[0:L, :], scalar1=stile[0:L, 8:9], scalar2=stile[0:L, 7:8],
                                                       op0=ALU.subtract, op1=ALU.mult), reads=[hrB, stb], writes=[hB])

            def H5pe(ct):
                c0, L = CHUNKS[ct]
                ph = ph_s if ct == SLAST else ph_p
                hi_ = (NH - 1) if ct == SLAST else ct % (NH - 1)
                hslot, hB = hn[hi_], bf(f"hn{hi_}")
                for dc in range(2):
                    fw.op("pe", lambda e, dc=dc: e.transpose(out=ph[:, dc, 0:L], in_=hslot[0:L, dc * 128:(dc + 1) * 128], identity=ident[0:L, 0:L]),
                          reads=[hB, bf("ident")], writes=[P2b], inc=(dc == 1))

            def H5ev(ct):
                c0, L = CHUNKS[ct]
                ph = ph_s if ct == SLAST else ph_p
                yv = yT[:, 8 + 2 * h:10 + 2 * h, c0:c0 + L]
                fw.op("dve", lambda e: e.tensor_tensor(out=yv, in0=ph[:, :, 0:L], in1=yv, op=ALU.mult),
                      reads=[P2b, bf("yT")], writes=[bf("yT")])

            for j in range(3):
                issue_loads(j)
            PE_A(SLAST)
            EV_A(SLAST)
            va_s, vaB_s = slot_v(SLAST)
            sm_s, smB_s = slot_s(SLAST)
            fw.op("pe", lambda e: e.matmul(out=ps[:, 1, 0:257], lhsT=sm_s[:, :], rhs=va_s[:, 0:257], start=True, stop=False),
                  reads=[smB_s, vaB_s], writes=[PB[1]], inc=True)
            zq_refresh(0)
            NSTEP = NP + 9

            def step(T):
                def ok(c):
                    return 0 <= c < NP
                if ok(T - 3):
                    ST(T - 3)
                if ok(T - 1):
                    EV_A(T - 1)
                if ok(T - 3):
                    NR(T - 3)
                if ok(T - 2):
                    PE_U(T - 2)
                if 0 <= T + 2 < NSEQ and T + 2 >= 3:
                    issue_loads(T + 2)
                if 0 <= T - 1 < NSEQ:
                    T0(T - 1)
                if 0 <= T - 3 < NSEQ:
                    T2(T - 3)
                if 0 <= T - 2 < NSEQ:
                    T1(T - 2)
                TS = NSEQ + 2
                if ok(T - 5):
                    H2(T - 5)
                if T == TS + 1:
                    H2(SLAST)
                if ok(T - 7):
                    H4(T - 7)
                if T == TS + 3:
                    H4(SLAST)
                if ok(T - 6):
                    H3(T - 6)
                if T == TS + 2:
                    H3(SLAST)
                if ok(T - 4):
                    H1(T - 4)
                if T == TS:
                    H1(SLAST)
                if ok(T - 9):
                    H5ev(T - 9)
                if T == TS + 5:
                    H5ev(SLAST)
                if ok(T - 8):
                    H5pe(T - 8)
                if T == TS + 4:
                    H5pe(SLAST)
                if ok(T):
                    PE_A(T)
                if T - 2 == 7:
                    zq_refresh(1)

            NMAIN = NP + 4
            for T in range(NMAIN):
                step(T)
            return [lambda T=T: step(T) for T in range(NMAIN, NSTEP + 1)]

        rotqk = {"i": 0, "g": 0}
        drain_ = []
        for h_ in range(4):
            drain_ = head_program(h_, drain_)
        for f_ in drain_:
            f_()
        for dc in range(2):
            bank = 4 + dc
            fw.op("pe", lambda e, dc=dc, bank=bank: e.transpose(out=ps[0:64, bank, 0:128], in_=nnT[:, dc, :], identity=identf),
                  reads=[bf("nnT"), bf("cst")], writes=[PB[bank]])
            fw.op("act", lambda e, dc=dc, bank=bank: e.activation(out=nn_tok[:, dc * 128:(dc + 1) * 128], in_=ps[0:64, bank, 0:128], func=AF.Copy),
                  reads=[PB[bank]], writes=[bf("nn_tok")])
        fw.dma("sp", lambda e: e.dma_start(out=n_sample[:, :], in_=nn_tok[:, :]), "smallout", reads=[bf("nn_tok")])

        cv = Carve()
        NX5 = 6
        xts = [cv.get([128, D]) for _ in range(NX5)]
        rts = [cv.get([128, D]) for _ in range(3)]
        ots = [cv.get([128, D]) for _ in range(3)]
        nfw = cv.get([128, D])
        phase_bufs(["nfw"])
        fw.dma("sp", lambda e: e.dma_start(out=nfw[:], in_=normf_w.partition_broadcast(128)), "const", writes=[bf("nfw")])
        xtB = phase_bufs([f"xt{i}" for i in range(NX5)])
        rtB = phase_bufs(["rt0", "rt1", "rt2"])
        otB = phase_bufs(["ot0", "ot1", "ot2"])
        def p5_mm(j, qs):
            c0, n = NTILES[j]
            b0 = (j % 4) * 2
            for q4 in qs:
                for half in range(2):
                    for ic in range(4 * q4, 4 * q4 + 4):
                        fw.op("pe", lambda e, ic=ic, half=half: e.matmul(out=ps[0:n, b0 + half, :], lhsT=yT[:, ic, c0:c0 + n],
                                                                         rhs=wout[:, ic, half * 512:(half + 1) * 512],
                                                                         start=(ic == 0), stop=(ic == 15)),
                              reads=[bf("yT"), woQ[q4]], writes=[PB[b0 + half]], inc=(ic == 15 or ic % 4 == 3))

        def p5_epi(j):
            c0, n = NTILES[j]
            xt, xtb = xts[j % NX5], xtB[j % NX5]
            rt, rtb = rts[j % 3], rtB[j % 3]
            ot, otb = ots[j % 3], otB[j % 3]
            stile, stb = stt[j % 4], bf(f"stt{j % 4}")
            b0 = (j % 4) * 2
            jn = j + NX5 - 1
            if jn < 18:
                load_x_tile(jn, xts[jn % NX5], xtB[jn % NX5], f"xt{jn % NX5}")
            fw.op("dve", lambda e: e.tensor_tensor(out=rt[0:n, :].rearrange("p (a b) -> p a b", b=512),
                                                   in0=ps[0:n, b0:b0 + 2, :], in1=xt[0:n, :].rearrange("p (a b) -> p a b", b=512), op=ALU.add),
                  reads=[PB[b0], PB[b0 + 1], xtb], writes=[rtb])
            fw.op("pool", lambda e: e.memset(stile[:], 0.0), writes=[stb])
            fw.op("act", lambda e: e.activation(out=junk[0:n, :], in_=rt[0:n, :], func=AF.Square, accum_out=stile[0:n, 0:1]),
                  reads=[rtb, stb], writes=[bf("junk"), stb])
            rstd_from_ss(stile, stb, 0, 1, 1.0 / D)
            fw.op("dve", lambda e: e.scalar_tensor_tensor(out=ot[0:n, :], in0=rt[0:n, :], scalar=stile[0:n, 1:2],
                                                          in1=nfw[0:n, :], op0=ALU.mult, op1=ALU.mult),
                  reads=[rtb, stb, bf("nfw")], writes=[otb])
            for (a_, b_, dst, base) in [(16, TP, y_prompt, 16), (SOFF, TT, y_sample, SOFF)]:
                lo = max(c0, a_)
                hi = min(c0 + n, b_)
                if lo < hi:
                    fw.dma("sp", lambda e, lo=lo, hi=hi, dst=dst, base=base: e.dma_start(out=dst[lo - base:hi - base, :],
                                                                                        in_=ot[lo - c0:hi - c0, :]),
                           f"ot{j % 3}", reads=[otb])

        for j_ in range(NX5 - 1):
            load_x_tile(j_, xts[j_], xtB[j_], f"xt{j_}")
        for q4 in range(4):
            for j in range(4):
                p5_mm(j, [q4])
        for j in range(4):
            p5_epi(j)
        for j in range(4, 18):
            p5_mm(j, [0, 1, 2, 3])
            p5_epi(j)
        fw.finish("sp")
        fw.emit(block)
    return nc


_NC = None


def _prep(inputs, i):
    f = lambda a: np.ascontiguousarray(np.asarray(a, dtype=np.float32))
    sl = slice(NSEQ * i, NSEQ * (i + 1))
    return {
        "xp": f(inputs["x_prompt"][i]),
        "xs": f(inputs["x_sample"][sl]).reshape(NS, D),
        "spool": f(inputs["state_pool"][0, sl]),
        "sC": f(inputs["state_C"][0, sl]),
        "sn": f(inputs["state_n"][0, sl]).reshape(NSEQ * 4, 256),
        "sm": f(inputs["state_m"][0, sl]),
        "meta": f(inputs["meta_tokens"]),
        "norm1_w": f(inputs["norm1_w"][0]),
        "w_in": f(inputs["w_in"][0]),
        "b_if": f(inputs["b_if"][0]),
        "w_pool": f(inputs["w_pool"][0]),
        "pool_scale": f(inputs["pool_scale"][0]),
        "mhln_w": f(inputs["mhln_w"][0]).reshape(D),
        "w_out": f(inputs["w_out"][0]),
        "normf_w": f(inputs["normf_w"]),
        "consts": make_consts(),
    }


def _assemble(results):
    n = len(results)
    st = lambda k: np.stack([np.asarray(results[i][k], dtype=np.float32) for i in range(n)])
    y_prompt = st("y_prompt")
    y_sample = st("y_sample").reshape(n * NSEQ, 8, D)
    pool_prompt = st("pool_prompt")[None]
    C_prompt = st("C_prompt")[None]
    n_prompt = st("n_prompt")[None]
    m_prompt = st("m_prompt").reshape(n, 4)[None]
    pool_sample = st("pool_sample").reshape(n * NSEQ, 15, D)[None]
    C_sample = st("C_sample").reshape(n * NSEQ, 4, 256, 256)[None]
    n_sample = st("n_sample").reshape(n * NSEQ, 4, 256)[None]
    m_sample = st("m_sample").reshape(n * NSEQ, 4)[None]
    return (y_prompt, y_sample, pool_prompt, C_prompt, n_prompt, m_prompt, pool_sample, C_sample, n_sample, m_sample)


def kernel(**inputs):
    global _NC
    if _NC is None:
        _NC = build_nc()
    in_maps = [_prep(inputs, i) for i in range(8)]
    res = run_bass_kernel_spmd(_NC, in_maps, core_ids=list(range(8)))
    return _assemble(res.results)
```

```python
import contextlib
import numpy as np
import concourse.bass as bass
import concourse.mybir as mybir
from concourse.bass_utils import run_bass_kernel_spmd

F32 = mybir.dt.float32
BF16 = mybir.dt.bfloat16
AF = mybir.ActivationFunctionType
ALU = mybir.AluOpType

D = 1024
TP = 2064
NS = 128
TT = TP + NS
SOFF = TP
NSEQ = 16
EPS = 1e-6
DPROJ = 7176
TBLK = [(0, 512), (512, 512), (1024, 512), (1536, 512), (2048, 144)]
CHUNKS = [(0, 16)] + [(16 + 128 * c, 128) for c in range(16)] + [(SOFF, 128)]
NCH = len(CHUNKS)
NTILES = [(128 * j, min(128, TT - 128 * j)) for j in range(18)]
UW = 2096 + 23 * NSEQ
UP0 = 32
US0 = 2096

C_ID = 0
C_CM = 128
C_BD = 256
C_SEL = 384
C_BDS = 896
C_RC = 912
C_N = 976


class Buf:
    __slots__ = ("name", "w", "r", "psum")

    def __init__(self, name):
        self.name = name
        self.w = []
        self.r = []
        self.psum = name.startswith("ps") and name[2:].isdigit()


class _Rec:
    def __getattr__(self, name):
        return lambda *a, **k: (name, a, k)


_REC = _Rec()


class FW:
    ENG = ("pe", "act", "dve", "pool", "sp")

    def __init__(self, nc, sems):
        self.nc = nc
        self.free_sems = list(sems)
        self.sem = {}
        self.cnt = {}
        for e in self.ENG:
            self.sem[e] = self.free_sems.pop()
            self.cnt[e] = 0
        self.known = {e: {} for e in self.ENG}
        self.prog = {e: [] for e in self.ENG}

    def _needs(self, e, reads, writes):
        need = {}

        def add(ev):
            k, v = ev
            if k == "pe" and e == "pe":
                return
            if isinstance(k, tuple):
                v = self.cnt[k]
            if need.get(k, 0) < v:
                need[k] = v

        for b in reads:
            for ev in b.w:
                add(ev)
            if b.psum:
                for ev in b.r:
                    if ev[0] != e:
                        add(ev)
        for b in writes:
            for ev in b.w:
                add(ev)
            for ev in b.r:
                add(ev)
        out = []
        kn = self.known[e]
        for k, v in need.items():
            if kn.get(k, 0) < v:
                kn[k] = v
                out.append((k, v))
        return out

    def _commit(self, ev, reads, writes):
        for b in reads:
            b.r.append(ev)
            if len(b.r) > 16:
                d = {}
                for k, v in b.r:
                    if d.get(k, 0) < v:
                        d[k] = v
                b.r = list(d.items())
        for b in writes:
            b.w = [ev]
            b.r = []

    deferred = None

    def op(self, e, fn, reads=(), writes=(), inc=True):
        rec = fn(_REC) if callable(fn) else fn
        if self.deferred is not None:
            self.deferred.append(("op", e, rec, tuple(reads), tuple(writes), inc))
            return
        waits = self._needs(e, reads, writes)
        if inc:
            self.cnt[e] += 1
            ev = (e, self.cnt[e])
        else:
            ev = (e, self.cnt[e] + 1)
        self.prog[e].append((waits, rec, (e, 1) if inc else None))
        self._commit(ev, reads, writes)

    def replay(self, item):
        if item[0] == "op":
            _, e, rec, reads, writes, inc = item
            self.op(e, rec, reads, writes, inc)
        else:
            _, q, rec, key, reads, writes = item
            self.dma(q, rec, key, reads, writes)

    def dma(self, q, fn, key, reads=(), writes=()):
        rec = fn(_REC) if callable(fn) else fn
        if self.deferred is not None:
            self.deferred.append(("dma", q, rec, key, tuple(reads), tuple(writes)))
            return
        key = ("d", key)
        if key not in self.sem:
            self.sem[key] = self.free_sems.pop()
            self.cnt[key] = 0
        waits = self._needs(q, reads, writes)
        self.cnt[key] += 16
        ev = (key, self.cnt[key])
        self.prog[q].append((waits, rec, (key, 16)))
        self._commit(ev, reads, writes)

    def all_events(self):
        evs = []
        for k, v in self.cnt.items():
            if v > 0 and k != "sp":
                evs.append((k, v))
        return evs

    def finish(self, q="sp"):
        waits = []
        for k, v in self.cnt.items():
            if v > 0 and k != q and self.known[q].get(k, 0) < v:
                waits.append((k, v))
        self.prog[q].append((waits, None, None))

    def emit(self, block):
        def run(e):
            def body(engine):
                for waits, fn, inc in self.prog[e]:
                    for k, v in waits:
                        engine.wait_ge(self.sem[k], v)
                    if fn is None:
                        continue
                    ins = getattr(engine, fn[0])(*fn[1], **fn[2])
                    if inc is not None:
                        ins.then_inc(self.sem[inc[0]], inc[1])
            return body
        block.tensor(run("pe"))
        block.scalar(run("act"))
        block.vector(run("dve"))
        block.gpsimd(run("pool"))
        block.sync(run("sp"))


def make_consts():
    c = np.zeros((128, C_N), np.float32)
    p = np.arange(128)
    c[:, C_ID:C_ID + 128] = np.eye(128, dtype=np.float32)
    c[:, C_CM:C_CM + 128] = (p[:, None] <= p[None, :]).astype(np.float32)
    c[:, C_BD:C_BD + 128] = ((p[:, None] <= p[None, :]) & ((p[:, None] // 8) == (p[None, :] // 8))).astype(np.float32)
    for h in range(4):
        c[h, C_SEL + 128 * h:C_SEL + 128 * (h + 1)] = 1.0
    c[:, C_BDS:C_BDS + 16] = ((p[:, None] // 8) == np.arange(16)[None, :]).astype(np.float32)
    for g, w in enumerate((2, 4, 8, 16)):
        pos = np.arange(16)
        c[:, C_RC + 16 * g:C_RC + 16 * (g + 1)] = (1.0 / np.minimum(w, pos + 1)).astype(np.float32)[None, :]
    return c


def build_nc():
    nc = bass.Bass("TRN2", target_bir_lowering=False)

    def din(name, shape):
        return nc.dram_tensor(name, list(shape), F32, kind="ExternalInput").ap()

    def dout(name, shape):
        return nc.dram_tensor(name, list(shape), F32, kind="ExternalOutput").ap()

    xp = din("xp", [2048, D])
    xs = din("xs", [NS, D])
    spool = din("spool", [NSEQ, 15, D])
    sC = din("sC", [NSEQ, 4, 256, 256])
    sn = din("sn", [NSEQ * 4, 256])
    sm = din("sm", [NSEQ, 4])
    meta = din("meta", [16, D])
    norm1_w = din("norm1_w", [D])
    w_in = din("w_in", [D, DPROJ])
    b_if = din("b_if", [8])
    w_pool = din("w_pool", [4, 256, 256])
    pool_scale = din("pool_scale", [D])
    mhln_w = din("mhln_w", [D])
    w_out = din("w_out", [2048, D])
    normf_w = din("normf_w", [D])
    consts = din("consts", [128, C_N])

    y_prompt = dout("y_prompt", [2048, D])
    y_sample = dout("y_sample", [NS, D])
    pool_prompt = dout("pool_prompt", [15, D])
    C_prompt = dout("C_prompt", [4, 256, 256])
    n_prompt = dout("n_prompt", [4, 256])
    m_prompt = dout("m_prompt", [4, 1])
    pool_sample = dout("pool_sample", [NSEQ, 15, D])
    C_sample = dout("C_sample", [NSEQ, 4, 256, 256])
    n_sample = dout("n_sample", [NSEQ * 4, 256])
    m_sample = dout("m_sample", [NSEQ, 4])

    with contextlib.ExitStack() as st:
        E = st.enter_context

        def sb(name, shape, dt=F32):
            return E(nc.sbuf_tensor(name, list(shape), dt))

        xnT = sb("xnT", [128, 8, TT], BF16)
        yT = sb("yT", [128, 16, TT], BF16)
        NW = 4
        wslot = [sb(f"wslot{i}", [128, 8, 256], BF16) for i in range(NW)]
        cst = sb("cst", [128, C_N])
        ident = sb("ident", [128, 128], BF16)
        tokQ = sb("tokQ", [128, NCH, 12])
        dec_bc = sb("dec_bc", [128, 4, 33])
        n1w = sb("n1w", [128, 8])
        ps5 = sb("ps5", [128, 8])
        mh4 = sb("mh4", [128, 8])
        epsT = sb("epsT", [128, 1])
        wg = sb("wg", [128, 8, 8], BF16)
        wp = sb("wp", [128, 4, 2, 256], BF16)
        nT_f = sb("nT_f", [128, 2, 64])
        nT_b = sb("nT_b", [128, 2, 64], BF16)
        nnT = sb("nnT", [128, 2, 64])
        sn_tok = sb("sn_tok", [64, 256])
        stt = [sb(f"stt{i}", [128, 16]) for i in range(8)]
        junk = sb("junk", [128, 1024], BF16)
        Dall = sb("Dall", [4, 33])
        m0T = sb("m0T", [4, 16])
        bif = sb("bif", [4, 2])
        nbf = sb("nbf", [4, 1])
        msm = sb("msm", [4, 17])

        ARENA = 18600
        arena = sb("arena", [128, ARENA])

        class Carve:
            def __init__(self):
                self.off = 0

            def get(self, shape, dt=F32):
                n = int(np.prod(shape[1:]))
                words = n if dt == F32 else (n + 1) // 2
                a = arena[0:shape[0], self.off:self.off + words]
                self.off += words
                assert self.off <= ARENA, self.off
                if dt != F32:
                    a = a.bitcast(dt)
                if len(shape) == 3:
                    a = a.rearrange("p (a b) -> p a b", b=shape[2])
                elif len(shape) == 4:
                    a = a.rearrange("p (a b c) -> p a b c", b=shape[2], c=shape[3])
                return a

        ps = E(nc.psum_tensor("ps", [128, 8, 512], F32))
        sems = [E(nc.semaphore(f"s{i}")) for i in range(80)]
        block = E(nc.Block())
        fw = FW(nc, sems)

        B = {}

        def bf(name):
            if name not in B:
                B[name] = Buf(name)
            return B[name]

        PB = [bf(f"ps{i}") for i in range(8)]

        def phase_bufs(names):
            evs = fw.all_events()
            out = []
            for n in names:
                b = Buf(n)
                b.w = list(evs)
                B[n] = b
                out.append(b)
            return out

        def psb(bank):
            return ps[:, bank, :].bitcast(BF16)

        fw.dma("sp", lambda e: e.dma_start(out=cst[:], in_=consts[:, :]), "const", writes=[bf("cst")])
        NCD = dict(allow_slow_non_contiguous=True)
        fw.dma("sp", lambda e: e.dma_start(out=n1w[:], in_=norm1_w.rearrange("(k p) -> p k", p=128), **NCD), "const", writes=[bf("n1w")])
        fw.dma("sp", lambda e: e.dma_start(out=ps5[:], in_=pool_scale.rearrange("(k p) -> p k", p=128), **NCD), "const", writes=[bf("ps5")])
        fw.dma("sp", lambda e: e.dma_start(out=mh4[:], in_=mhln_w.rearrange("(k p) -> p k", p=128), **NCD), "const", writes=[bf("mh4")])
        fw.dma("sp", lambda e: e.dma_start(out=m0T[:], in_=sm.rearrange("j h -> h j"), **NCD), "const", writes=[bf("m0T")])
        fw.dma("sp", lambda e: e.dma_start(out=bif[:], in_=b_if.rearrange("(t h) -> h t", h=4), **NCD), "const", writes=[bf("bif")])
        fw.dma("sp", lambda e: e.dma_start(out=sn_tok[:], in_=sn[:, :]), "const", writes=[bf("sn_tok")])
        fw.dma("pool", lambda e: e.dma_start(out=wg[:], in_=w_in[:, 7168:7176].rearrange("(k p) c -> p k c", p=128)),
               "wg", writes=[bf("wg")])
        fw.dma("pool", lambda e: e.dma_start(out=wp[:], in_=w_pool.rearrange("g (i p) d -> p g i d", p=128)),
               "wp", writes=[bf("wp")])
        fw.op("dve", lambda e: e.tensor_copy(out=ident[:], in_=cst[:, C_ID:C_ID + 128]), reads=[bf("cst")], writes=[bf("ident")])
        fw.op("dve", lambda e: e.memset(epsT[:], EPS), writes=[bf("epsT")])
        fw.op("dve", lambda e: e.tensor_scalar(out=ps5[:], in0=ps5[:], scalar1=0.5, scalar2=None, op0=ALU.mult),
              reads=[bf("ps5")], writes=[bf("ps5")])
        fw.op("dve", lambda e: e.tensor_scalar(out=mh4[:], in0=mh4[:], scalar1=0.25, scalar2=None, op0=ALU.mult),
              reads=[bf("mh4")], writes=[bf("mh4")])
        fw.op("pool", lambda e: e.memset(nnT[:], 0.0), writes=[bf("nnT")])

        identf = cst[:, C_ID:C_ID + 128]

        wstate = {"i": 0}

        def load_w(col0):
            i = wstate["i"] % NW
            wstate["i"] += 1
            t = wslot[i]
            b = bf(f"wslot{i}")
            fw.dma("pool", lambda e: e.dma_start(out=t[:], in_=w_in[:, col0:col0 + 256].rearrange("(k p) c -> p k c", p=128)),
                   f"w{i}", writes=[b])
            return t, b

        def load_x_tile(j, xt, xtb, key):
            c0, n = NTILES[j]
            segs = [(0, 16, meta, 0), (16, TP, xp, 16), (SOFF, TT, xs, SOFF)]
            for (a, b_, src, base) in segs:
                lo = max(c0, a)
                hi = min(c0 + n, b_)
                if lo < hi:
                    fw.dma("sp", lambda e, lo=lo, hi=hi, src=src, base=base: e.dma_start(
                        out=xt[lo - c0:hi - c0, :], in_=src[lo - base:hi - base, :]), key, writes=[xtb])

        def rstd_from_ss(stile, stb, col_ss, col_out, scale):
            fw.op("act", lambda e: e.activation(out=stile[:, col_out:col_out + 1], in_=stile[:, col_ss:col_ss + 1],
                                                func=AF.Ln, scale=scale, bias=epsT[:, 0:1]),
                  reads=[stb, bf("epsT")], writes=[stb])
            fw.op("act", lambda e: e.activation(out=stile[:, col_out:col_out + 1], in_=stile[:, col_out:col_out + 1],
                                                func=AF.Exp, scale=-0.5), reads=[stb], writes=[stb])

        cv = Carve()
        NX1 = 8
        xts = [cv.get([128, D]) for _ in range(NX1)]
        xbs = [cv.get([128, D], BF16) for _ in range(3)]
        xtB = phase_bufs([f"xt{i}" for i in range(NX1)])
        xbB = phase_bufs(["xb0", "xb1", "xb2"])
        def p1_front(j):
            c0, n = NTILES[j]
            xt, xtb = xts[j % NX1], xtB[j % NX1]
            xb, xbb = xbs[j % 3], xbB[j % 3]
            stile, stb = stt[j % 4], bf(f"stt{j % 4}")
            load_x_tile(j, xt, xtb, f"xt{j % NX1}")
            fw.op("pool", lambda e: e.memset(stile[:], 0.0), writes=[stb])
            fw.op("act", lambda e: e.activation(out=junk[0:n, :], in_=xt[0:n, :], func=AF.Square, accum_out=stile[0:n, 0:1]),
                  reads=[xtb, stb], writes=[bf("junk"), stb])
            rstd_from_ss(stile, stb, 0, 1, 1.0 / D)
            fw.op("dve", lambda e: e.tensor_scalar(out=xb[0:n, :], in0=xt[0:n, :], scalar1=stile[0:n, 1:2], scalar2=None, op0=ALU.mult),
                  reads=[xtb, stb], writes=[xbb])
            bank = 4 + (j % 2)
            pv = psb(bank).rearrange("p (k t) -> p k t", t=128)
            for k in range(8):
                fw.op("pe", lambda e, k=k: e.transpose(out=pv[:, k, 0:n], in_=xb[0:n, k * 128:(k + 1) * 128], identity=ident[0:n, 0:n]),
                      reads=[xbb, bf("ident")], writes=[PB[bank]], inc=(k == 7))

        def p1_back(j):
            c0, n = NTILES[j]
            bank = 4 + (j % 2)
            pv = psb(bank).rearrange("p (k t) -> p k t", t=128)
            fw.op("dve", lambda e: e.tensor_tensor(
                out=xnT[:, :, c0:c0 + n], in0=pv[:, :, 0:n], in1=n1w[:, :].unsqueeze(2).to_broadcast([128, 8, n]), op=ALU.mult),
                reads=[PB[bank], bf("n1w")], writes=[bf("xnT")])

        p1_front(0)
        for j in range(18):
            if j + 1 < 18:
                p1_front(j + 1)
            p1_back(j)

        def proj_fm(wt, wb, f0, nf, tb, bank, src=None):
            c0, n = tb
            for k in range(8):
                fw.op("pe", lambda e, k=k: e.matmul(out=ps[0:nf, bank, 0:n], lhsT=wt[:, k, f0:f0 + nf], rhs=xnT[:, k, c0:c0 + n],
                                                    start=(k == 0), stop=(k == 7)),
                      reads=[wb, bf("xnT")], writes=[PB[bank]], inc=(k == 7))

        def ytile(np_, blk):
            return yT[0:np_, blk:blk + 2, :].rearrange("p a t -> p (a t)").bitcast(F32)

        G_ig = ytile(4, 8)
        G_sp = ytile(4, 10)
        G_P = ytile(4, 12)
        G_gg = ytile(4, 14)
        G_Mx = ytile(4, 0)
        Qt = ytile(96, 10)
        G_t = G_ig
        for nm_ in ("G_ig", "G_sp", "G_P", "G_gg", "G_Mx"):
            B[nm_] = Buf(nm_)
        B["Qt"] = B["G_sp"]
        B["G_t"] = B["G_ig"]
        fw.deferred = []
        fw.op("dve", lambda e: e.tensor_scalar(out=nbf[:], in0=bif[:, 1:2], scalar1=-1.0, scalar2=None, op0=ALU.mult),
              reads=[bf("bif")], writes=[bf("nbf")])
        for ti, tb in enumerate(TBLK):
            c0, n = tb
            b0, b1 = (ti % 2) * 2, (ti % 2) * 2 + 1
            proj_fm(wg, bf("wg"), 0, 4, tb, b0)
            proj_fm(wg, bf("wg"), 4, 4, tb, b1)
            fw.op("dve", lambda e, b0=b0, c0=c0, n=n: e.tensor_scalar(out=G_ig[:, c0:c0 + n], in0=ps[0:4, b0, 0:n], scalar1=bif[:, 0:1],
                                                                      scalar2=None, op0=ALU.add),
                  reads=[PB[b0], bf("bif")], writes=[bf("G_ig")])
            fw.op("act", lambda e, b1=b1, c0=c0, n=n: e.activation(out=G_sp[:, c0:c0 + n], in_=ps[0:4, b1, 0:n], func=AF.Exp, scale=-1.0,
                                                                  bias=nbf[:, 0:1]),
                  reads=[PB[b1], bf("nbf")], writes=[bf("G_sp")])
        fw.op("act", lambda e: e.activation(out=G_sp[:], in_=G_sp[:], func=AF.Ln, bias=1.0), reads=[bf("G_sp")], writes=[bf("G_sp")])
        fw.op("dve", lambda e: e.tensor_tensor_scan(out=G_P[:, 0:TP], data0=G_sp[:, 0:TP], data1=G_sp[:, 0:TP], initial=0.0,
                                                    op0=ALU.add, op1=ALU.max),
              reads=[bf("G_sp")], writes=[bf("G_P")])
        for j in range(NSEQ):
            a = SOFF + 8 * j
            fw.op("dve", lambda e, a=a: e.tensor_tensor_scan(out=G_P[:, a:a + 8], data0=G_sp[:, a:a + 8], data1=G_sp[:, a:a + 8],
                                                             initial=0.0, op0=ALU.add, op1=ALU.max),
                  reads=[bf("G_sp")], writes=[bf("G_P")])
        fw.op("dve", lambda e: e.tensor_tensor(out=G_gg[:], in0=G_ig[:], in1=G_P[:], op=ALU.add),
              reads=[bf("G_ig"), bf("G_P")], writes=[bf("G_gg")])
        fw.op("dve", lambda e: e.tensor_tensor_scan(out=G_Mx[:, 0:TP], data0=G_gg[:, 0:TP], data1=G_gg[:, 0:TP], initial=0.0,
                                                    op0=ALU.max, op1=ALU.max),
              reads=[bf("G_gg")], writes=[bf("G_Mx")])
        for j in range(NSEQ):
            a = SOFF + 8 * j
            fw.op("dve", lambda e, a=a, j=j: e.tensor_tensor_scan(out=G_Mx[:, a:a + 8], data0=G_gg[:, a:a + 8], data1=G_gg[:, a:a + 8],
                                                                  initial=m0T[:, j:j + 1], op0=ALU.max, op1=ALU.max),
                  reads=[bf("G_gg"), bf("m0T")], writes=[bf("G_Mx")])

        def real3(t):
            return t[:, 16:TP].rearrange("p (c l) -> p c l", l=128)

        def samp3(t):
            return t[:, SOFF:TT].rearrange("p (c l) -> p c l", l=8)

        Rprev_real = G_Mx[:, 15:1936:128].unsqueeze(2).to_broadcast([4, 16, 128])
        Rend_real = G_Mx[:, 143:TP:128].unsqueeze(2).to_broadcast([4, 16, 128])
        Rprev_s = m0T[:, :].unsqueeze(2).to_broadcast([4, 16, 8])
        Rend_s = G_Mx[:, SOFF + 7:TT:8].unsqueeze(2).to_broadcast([4, 16, 8])
        Rend_meta = G_Mx[:, 15:16].to_broadcast([4, 16])

        def qrow(src, kind, row0, escale=1.0):
            rd = [bf("G_gg"), bf("G_P"), bf("G_Mx"), bf("m0T")]
            if kind == "prev":
                fw.op("dve", lambda e: e.tensor_copy(out=G_t[:, 0:16], in_=src[:, 0:16]), reads=rd, writes=[bf("G_t")])
                fw.op("dve", lambda e: e.tensor_tensor(out=real3(G_t), in0=real3(src), in1=Rprev_real, op=ALU.subtract), reads=rd, writes=[bf("G_t")])
                fw.op("dve", lambda e: e.tensor_tensor(out=samp3(G_t), in0=samp3(src), in1=Rprev_s, op=ALU.subtract), reads=rd, writes=[bf("G_t")])
            else:
                fw.op("dve", lambda e: e.tensor_tensor(out=G_t[:, 0:16], in0=src[:, 0:16], in1=Rend_meta, op=ALU.subtract), reads=rd, writes=[bf("G_t")])
                fw.op("dve", lambda e: e.tensor_tensor(out=real3(G_t), in0=real3(src), in1=Rend_real, op=ALU.subtract), reads=rd, writes=[bf("G_t")])
                fw.op("dve", lambda e: e.tensor_tensor(out=samp3(G_t), in0=samp3(src), in1=Rend_s, op=ALU.subtract), reads=rd, writes=[bf("G_t")])
            fw.op("act", lambda e: e.activation(out=Qt[row0:row0 + 4, :], in_=G_t[:], func=AF.Exp, scale=escale), reads=[bf("G_t")], writes=[bf("Qt")])

        fw.op("pool", lambda e: e.memset(Qt[:], 0.0), reads=[bf("G_P")], writes=[bf("Qt")])
        qrow(G_gg, "prev", 0)
        qrow(G_P, "prev", 32, 2.0)
        qrow(G_gg, "end", 64)
        rdm = [bf("G_Mx"), bf("m0T")]
        fw.op("dve", lambda e: e.tensor_scalar(out=Dall[:, 0:1], in0=G_Mx[:, 15:16], scalar1=-1.0, scalar2=None, op0=ALU.mult),
              reads=rdm, writes=[bf("Dall")])
        fw.op("dve", lambda e: e.tensor_tensor(out=Dall[:, 1:17], in0=G_Mx[:, 15:1936:128], in1=G_Mx[:, 143:TP:128], op=ALU.subtract),
              reads=rdm, writes=[bf("Dall")])
        fw.op("dve", lambda e: e.tensor_tensor(out=Dall[:, 17:33], in0=m0T[:, :], in1=G_Mx[:, SOFF + 7:TT:8], op=ALU.subtract),
              reads=rdm, writes=[bf("Dall")])
        fw.op("act", lambda e: e.activation(out=Dall[:], in_=Dall[:], func=AF.Exp), reads=[bf("Dall")], writes=[bf("Dall")])
        fw.op("dve", lambda e: e.tensor_tensor(out=msm[:, 0:1], in0=G_Mx[:, TP - 1:TP], in1=G_P[:, TP - 1:TP], op=ALU.subtract),
              reads=[bf("G_Mx"), bf("G_P")], writes=[bf("msm")])
        fw.op("dve", lambda e: e.tensor_tensor(out=msm[:, 1:17], in0=G_Mx[:, SOFF + 7:TT:8], in1=G_P[:, SOFF + 7:TT:8], op=ALU.subtract),
              reads=[bf("G_Mx"), bf("G_P")], writes=[bf("msm")])
        fw.dma("sp", lambda e: e.dma_start(out=m_prompt[:, :], in_=msm[:, 0:1]), "smallout", reads=[bf("msm")])
        fw.dma("sp", lambda e: e.dma_start(out=m_sample.rearrange("j h -> h j"), in_=msm[:, 1:17], **NCD), "smallout", reads=[bf("msm")])
        for ct, (c0, L) in enumerate(CHUNKS):
            bank = 4 + (ct % 2)
            fw.op("pe", lambda e, c0=c0, L=L, bank=bank: e.transpose(out=ps[0:L, bank, 0:96], in_=Qt[:, c0:c0 + L], identity=identf[0:96, 0:96]),
                  reads=[bf("Qt"), bf("cst")], writes=[PB[bank]])
            fw.op("act", lambda e, ct=ct, L=L, bank=bank: e.activation(
                out=tokQ[0:L, ct, :].rearrange("p (a b) -> p a b", b=4),
                in_=ps[0:L, bank, 0:96].rearrange("p (a b) -> p a b", b=32)[:, :, 0:4], func=AF.Copy),
                  reads=[PB[bank]], writes=[bf("tokQ")])
        for h in range(4):
            bank = 6 + (h % 2)
            fw.op("pe", lambda e, h=h, bank=bank: e.matmul(out=ps[:, bank, 0:33], lhsT=cst[0:4, C_SEL + 128 * h:C_SEL + 128 * (h + 1)],
                                                           rhs=Dall[:, :], start=True, stop=True),
                  reads=[bf("Dall"), bf("cst")], writes=[PB[bank]])
            fw.op("act", lambda e, h=h, bank=bank: e.activation(out=dec_bc[:, h, :], in_=ps[:, bank, 0:33], func=AF.Copy),
                  reads=[PB[bank]], writes=[bf("dec_bc")])
        for dc in range(2):
            bank = 6 + dc
            fw.op("pe", lambda e, dc=dc, bank=bank: e.transpose(out=ps[:, bank, 0:64], in_=sn_tok[:, dc * 128:(dc + 1) * 128],
                                                                identity=identf[0:64, 0:64]),
                  reads=[bf("sn_tok"), bf("cst")], writes=[PB[bank]])
            fw.op("act", lambda e, dc=dc, bank=bank: e.activation(out=nT_f[:, dc, :], in_=ps[:, bank, 0:64], func=AF.Copy),
                  reads=[PB[bank]], writes=[bf("nT_f")])
        fw.op("dve", lambda e: e.tensor_copy(out=nT_b[:], in_=nT_f[:]), reads=[bf("nT_f")], writes=[bf("nT_b")])
        p2_items = fw.deferred
        fw.deferred = None
        p2_pending = set()
        last_pe_open = [False]

        def p2_release(n=1):
            k_ = 0
            while p2_items:
                if k_ >= n and not p2_pending and not last_pe_open[0]:
                    break
                it_ = p2_items.pop(0)
                fw.replay(it_)
                if it_[0] == "op":
                    _, e_, _rec, rd_, wr_, inc_ = it_
                    for b_ in rd_:
                        if b_.psum:
                            p2_pending.discard(b_.name)
                    for b_ in wr_:
                        if b_.psum:
                            p2_pending.add(b_.name)
                    last_pe_open[0] = (e_ == "pe" and not inc_)
                if not p2_pending and not last_pe_open[0]:
                    k_ += 1

        def p2_drain():
            p2_release(10 ** 9)
            for nm_ in ("G_ig", "G_sp", "G_P", "G_gg", "G_Mx"):
                bf("yT").r.extend(B[nm_].w + B[nm_].r)

        cv = Carve()
        u_ = [cv.get([128, UW]) for _ in range(2)]
        Aa = cv.get([128, UW])
        Ab = cv.get([128, UW])
        pooled_ = [cv.get([128, 2, TT], BF16) for _ in range(2)]
        sp_tok = cv.get([120, 2, D])
        th = cv.get([128, 512])
        szt = cv.get([128, 512], BF16)
        pp_stage = cv.get([16, 256])
        ps_stage = cv.get([128, D])
        snc = cv.get([128, 128])
        (uB0, uB1, AaB, AbB, pooledB0, pooledB1, sptB, thB, sztB, ppB, pssB, sncB) = phase_bufs(
            ["u0", "u1", "Aa", "Ab", "pooledT0", "pooledT1", "sp_tok", "th", "szt", "pp_stage", "ps_stage", "snc"])
        pooledB_ = [pooledB0, pooledB1]
        uB_ = [uB0, uB1]
        for t in range(2):
            fw.dma("sp", lambda e, t=t: e.dma_start(out=sp_tok[:, t, :], in_=spool[8 * t:8 * t + 8].rearrange("b r c -> (b r) c")),
                   "sptok", writes=[sptB])
        for i_ in range(2):
            fw.op("pool", lambda e, i_=i_: e.memset(u_[i_][:, 0:UP0], 0.0), writes=[uB_[i_]])
        fw.op("pool", lambda e: e.memset(Aa[:, 0:16], 0.0), writes=[AaB])
        fw.op("pool", lambda e: e.memset(Ab[:, 0:16], 0.0), writes=[AbB])
        fw.dma("sp", lambda e: e.dma_start(out=pool_sample[:, 0:7, :], in_=spool[:, 8:15, :]), "smallout")

        def snew(t):
            return bass.AP(t.tensor, t.offset + US0 + 15, [list(t.ap[0]), [23, 16], [1, 8]])

        def sprev(t, half):
            return bass.AP(t.tensor, t.offset + US0 + 23 * 8 * half, [list(t.ap[0]), [23, 8], [1, 15]])

        rot = {"b": 0}

        def nbank():
            b = rot["b"] % 4
            rot["b"] += 1
            return b

        tmp16 = cv.get([128, 16])
        (t16B,) = phase_bufs(["tmp16"])
        wts = {}

        def stage_A(g, ib):
            cb = 2 * g + ib
            u, uB = u_[ib], uB_[ib]
            su, sub = wts[g][0], wts[g][1]
            for t in range(2):
                bank = 4 + t
                fw.op("pe", lambda e, t=t: e.transpose(out=ps[:, bank, 0:120], in_=sp_tok[:, t, cb * 128:(cb + 1) * 128],
                                                       identity=identf[0:120, 0:120]),
                      reads=[sptB, bf("cst")], writes=[PB[bank]])
                fw.op("act", lambda e, t=t: e.activation(out=sprev(u, t), in_=ps[:, bank, 0:120].rearrange("p (b r) -> p b r", r=15), func=AF.Copy),
                      reads=[PB[bank]], writes=[uB])
            for tb in TBLK:
                c0, n = tb
                bank = nbank()
                proj_fm(su, sub, ib * 128, 128, tb, bank)
                npr = min(c0 + n, TP) - c0
                fw.op("act", lambda e: e.activation(out=u[:, UP0 + c0:UP0 + c0 + npr], in_=ps[:, bank, 0:npr], func=AF.Copy),
                      reads=[PB[bank]], writes=[uB])
                if c0 + n > TP:
                    fw.op("act", lambda e: e.activation(out=snew(u), in_=ps[:, bank, npr:npr + NS].rearrange("p (b r) -> p b r", r=8), func=AF.Copy),
                          reads=[PB[bank]], writes=[uB])
                    fw.op("act", lambda e: e.activation(out=snc[:, :], in_=ps[:, bank, npr:npr + NS], func=AF.Copy),
                          reads=[PB[bank]], writes=[sncB])
                p2_release(P2N)
            fw.op("pe", lambda e: e.transpose(out=ps[0:15, 6, 0:128], in_=u[:, UP0 + TP - 15:UP0 + TP], identity=identf),
                  reads=[uB, bf("cst")], writes=[PB[6]])
            pcol = (cb % 2) * 128
            fw.op("act", lambda e: e.activation(out=pp_stage[0:15, pcol:pcol + 128], in_=ps[0:15, 6, 0:128], func=AF.Copy),
                  reads=[PB[6]], writes=[ppB])
            fw.dma("sp", lambda e: e.dma_start(out=pool_prompt[:, cb * 128:(cb + 1) * 128], in_=pp_stage[0:15, pcol:pcol + 128]),
                   "ppout", reads=[ppB])
            fw.op("pe", lambda e: e.transpose(out=ps[:, 7, 0:128], in_=snc[:, :], identity=identf),
                  reads=[sncB, bf("cst")], writes=[PB[7]])
            fw.op("act", lambda e: e.activation(out=ps_stage[:, cb * 128:(cb + 1) * 128], in_=ps[:, 7, 0:128], func=AF.Copy),
                  reads=[PB[7]], writes=[pssB])

        def stage_B(g, ib):
            ops = []
            w = 2 ** (g + 1)
            pooledT, pooledB = pooled_[gpar[g]], pooledB_[gpar[g]]
            u, uB = u_[ib], uB_[ib]
            src, srcB = u, uB
            dsts = [(Aa, AaB), (Ab, AbB)]
            for lvl in range(g + 1):
                sh = 2 ** lvl
                dst, dstB = dsts[lvl % 2]
                ops.append(lambda src=src, dst=dst, sh=sh, srcB=srcB, dstB=dstB: fw.op(
                    "dve", lambda e: e.tensor_tensor(out=dst[:, 16:UW], in0=src[:, 16:UW], in1=src[:, 16 - sh:UW - sh], op=ALU.add),
                    reads=[srcB], writes=[dstB]))
                src, srcB = dst, dstB
            A, AB = src, srcB

            def tail():
                fw.op("dve", lambda e: e.scalar_tensor_tensor(
                    out=pooledT[:, ib, 0:TP], in0=A[:, UP0:UP0 + TP], scalar=1.0 / w, in1=u[:, UP0:UP0 + TP], op0=ALU.mult, op1=ALU.subtract),
                    reads=[AB, uB], writes=[pooledB])
                fw.op("dve", lambda e: e.scalar_tensor_tensor(
                    out=pooledT[:, ib, SOFF:TT].rearrange("p (b r) -> p b r", r=8), in0=snew(A), scalar=1.0 / w, in1=snew(u),
                    op0=ALU.mult, op1=ALU.subtract), reads=[AB, uB], writes=[pooledB])
                fw.op("dve", lambda e: e.tensor_tensor(out=tmp16[:, 0:16], in0=A[:, UP0:UP0 + 16],
                                                       in1=cst[:, C_RC + 16 * g:C_RC + 16 * (g + 1)], op=ALU.mult),
                      reads=[AB, bf("cst")], writes=[t16B])
                fw.op("dve", lambda e: e.tensor_tensor(out=pooledT[:, ib, 0:16], in0=tmp16[:, 0:16], in1=u[:, UP0:UP0 + 16], op=ALU.subtract),
                      reads=[t16B, uB], writes=[pooledB])
            ops.append(tail)
            return ops

        rot6 = {"i": 0}

        def nbank6():
            b_ = (0, 1, 2, 3, 6, 7)[rot6["i"] % 6]
            rot6["i"] += 1
            return b_

        def stage_C(g, filler=()):
            filler = list(filler)
            nunits = 10
            per = [len(filler) * (i + 1) // nunits - len(filler) * i // nunits for i in range(nunits)]
            unit_i = 0
            sz, szb = wts[g][2], wts[g][3]
            pooledT, pooledB = pooled_[gpar[g]], pooledB_[gpar[g]]
            for ob in range(2):
                cb = 2 * g + ob
                for tb in TBLK:
                    c0, n = tb
                    bm = nbank6()
                    for ib in range(2):
                        fw.op("pe", lambda e, ib=ib: e.matmul(
                            out=ps[:, bm, 0:n], lhsT=wp[:, g, ib, ob * 128:(ob + 1) * 128], rhs=pooledT[:, ib, c0:c0 + n],
                            start=(ib == 0), stop=(ib == 1)), reads=[bf("wp"), pooledB], writes=[PB[bm]], inc=(ib == 1))
                    bz = nbank6()
                    proj_fm(sz, szb, ob * 128, 128, tb, bz)
                    fw.op("act", lambda e: e.activation(out=th[:, 0:n], in_=ps[:, bz, 0:n], func=AF.Tanh, scale=0.5),
                          reads=[PB[bz]], writes=[thB])
                    fw.op("dve", lambda e: e.scalar_tensor_tensor(
                        out=szt[:, 0:n], in0=th[:, 0:n], scalar=1.0, in1=ps[:, bz, 0:n], op0=ALU.add, op1=ALU.mult),
                        reads=[thB, PB[bz]], writes=[sztB])
                    fw.op("dve", lambda e: e.scalar_tensor_tensor(
                        out=yT[:, cb, c0:c0 + n], in0=ps[:, bm, 0:n], scalar=ps5[:, cb:cb + 1], in1=szt[:, 0:n],
                        op0=ALU.mult, op1=ALU.mult), reads=[PB[bm], bf("ps5"), sztB], writes=[bf("yT")])
                    for _ in range(per[unit_i]):
                        filler.pop(0)()
                    unit_i += 1
                    p2_release(P2N)
            assert not filler

        GORDER = [1, 3, 2, 0]
        P2N = 2
        gpar = {g: i % 2 for i, g in enumerate(GORDER)}
        prev_g = None
        for g in GORDER:
            su, sub = load_w(256 * g)
            sz, szb = load_w(1024 + 256 * g)
            wts[g] = (su, sub, sz, szb)
            stage_A(g, 0)
            for o_ in stage_B(g, 0):
                o_()
            stage_A(g, 1)
            bops = stage_B(g, 1)
            if prev_g is not None:
                stage_C(prev_g, bops)
            else:
                for o_ in bops:
                    o_()
            prev_g = g
        p2_drain()
        stage_C(prev_g)
        for j in range(NSEQ):
            fw.dma("sp", lambda e, j=j: e.dma_start(out=pool_sample[j, 7:15, :], in_=ps_stage[8 * j:8 * j + 8, :]), "smallout", reads=[pssB])

        cv = Carve()
        qT = cv.get([128, 2, TT], BF16)
        kT = cv.get([128, 2, TT], BF16)
        gate_tmp_off = cv.off
        tho = [cv.get([128, 512]) for _ in range(2)]
        thz = [cv.get([128, 512]) for _ in range(2)]
        t1b = [cv.get([128, 512]) for _ in range(2)]
        zsb = [cv.get([128, 512]) for _ in range(2)]
        NV, NK, NS_, NCB, NH = 4, 4, 4, 3, 3
        vaug = [cv.get([128, 258], BF16) for _ in range(NV)]
        kw = [cv.get([128, 256], BF16) for _ in range(NK)]
        sTm = [cv.get([128, 128], BF16) for _ in range(NS_)]
        hn = [cv.get([128, 256], BF16) for _ in range(NH)]
        C_st = cv.get([128, 2, 257])
        C_bf = [cv.get([128, 2, 258], BF16) for _ in range(NCB)]
        zq = cv.get([128, 2, 1024], BF16)
        ktok_s = cv.get([128, 256], BF16)
        Wm = cv.get([128, 16])
        NCF, NCS = 6, 2
        Cf = [cv.get([128, 2, 256]) for _ in range(NCF)]
        Csb = [cv.get([128, 2, 256], BF16) for _ in range(NCS)]
        nn_tok = cv.get([64, 256])
        NHR = 6
        hraw = [cv.get([128, 256]) for _ in range(NHR)]
        names = ([f"hraw{i}" for i in range(NHR)] + ["qT", "kT", "zsb0", "zsb1"] + [f"tho{i}" for i in range(2)] + [f"thz{i}" for i in range(2)] + [f"t1b{i}" for i in range(2)]
                 + [f"vaug{i}" for i in range(NV)] + [f"kw{i}" for i in range(NK)] + [f"sTm{i}" for i in range(NS_)]
                 + [f"hn{i}" for i in range(NH)] + [f"C_bf{i}" for i in range(NCB)]
                 + ["C_st", "zq", "ktok_s", "Wm"] + [f"Cf{i}" for i in range(NCF)] + [f"Csb{i}" for i in range(NCS)]
                 + ["nn_tok"])
        phase_bufs(names)
        for i in range(NV):
            fw.op("pool", lambda e, i=i: e.memset(vaug[i][:, 256:258], 1.0), writes=[bf(f"vaug{i}")])
        fw.op("pool", lambda e: e.memset(zq[:], 0.0), writes=[bf("zq")])
        zq_diag = bass.AP(zq.tensor, zq.offset, [list(zq.ap[0]), [1024, 2], [136, 8], [1, 8]])

        wout = xnT[:].rearrange("p k t -> p (k t)")[:, 0:16 * D].rearrange("p (k c) -> p k c", c=D)
        woB = bf("xnT")

        woQ = [Buf(f"woq{i}") for i in range(4)]

        def load_wout():
            for q4 in range(4):
                fw.dma("pool", lambda e, q4=q4: e.dma_start(out=wout[:, 4 * q4:4 * q4 + 4, :],
                                                            in_=w_out[512 * q4:512 * (q4 + 1), :].rearrange("(k p) c -> p k c", p=128)),
                       f"wout{q4}", writes=[woB, woQ[q4]])

        cnt = {"st": 0, "cf": 0, "co": 0, "kj": 0}
        SLAST = NCH - 1

        prefetched_w = {}

        def head_program(h, drain_prev):
            if h in prefetched_w:
                (wq, wqb), (wk, wkb), (wo, wob) = prefetched_w[h]
            else:
                wq, wqb = load_w(2048 + 256 * h)
                wk, wkb = load_w(3072 + 256 * h)
                wo, wob = load_w(5120 + 256 * h)
            wz, wzb = load_w(6144 + 256 * h)
            def qk_bank():
                b_ = (0, 1, 3, 4, 5, 6, 7)[rotqk["i"] % 7]
                rotqk["i"] += 1
                return b_
            for dc in range(2):
                for tb in TBLK:
                    c0, n = tb
                    bank = qk_bank()
                    proj_fm(wq, wqb, dc * 128, 128, tb, bank)
                    fw.op("act", lambda e: e.activation(out=qT[:, dc, c0:c0 + n], in_=ps[:, bank, 0:n], func=AF.Copy),
                          reads=[PB[bank]], writes=[bf("qT")])
                    bank = qk_bank()
                    proj_fm(wk, wkb, dc * 128, 128, tb, bank)
                    fw.op("act", lambda e: e.activation(out=kT[:, dc, c0:c0 + n], in_=ps[:, bank, 0:n], func=AF.Copy, scale=1.0 / 16),
                          reads=[PB[bank]], writes=[bf("kT")])
                    if drain_prev:
                        drain_prev.pop(0)()
            while drain_prev:
                drain_prev.pop(0)()
            wv, wvb = load_w(4096 + 256 * h)
            for dc in range(2):
                yb = 8 + 2 * h + dc
                for ti, tb in enumerate(TBLK):
                    c0, n = tb
                    i2 = ti % 2
                    bo = rotqk["g"] % 8
                    bz = (rotqk["g"] + 1) % 8
                    rotqk["g"] += 2
                    proj_fm(wo, wob, dc * 128, 128, tb, bo)
                    proj_fm(wz, wzb, dc * 128, 128, tb, bz)
                    fw.op("act", lambda e: e.activation(out=tho[i2][:, 0:n], in_=ps[:, bo, 0:n], func=AF.Tanh, scale=0.5),
                          reads=[PB[bo]], writes=[bf(f"tho{i2}")])
                    fw.op("act", lambda e: e.activation(out=thz[i2][:, 0:n], in_=ps[:, bz, 0:n], func=AF.Tanh, scale=0.5),
                          reads=[PB[bz]], writes=[bf(f"thz{i2}")])
                    fw.op("act", lambda e: e.activation(out=zsb[i2][:, 0:n], in_=ps[:, bz, 0:n], func=AF.Copy, scale=mh4[:, 2 * h + dc:2 * h + dc + 1]),
                          reads=[PB[bz], bf("mh4")], writes=[bf(f"zsb{i2}")])
                    fw.op("dve", lambda e: e.scalar_tensor_tensor(out=t1b[i2][:, 0:n], in0=thz[i2][:, 0:n], scalar=1.0,
                                                                  in1=zsb[i2][:, 0:n], op0=ALU.add, op1=ALU.mult),
                          reads=[bf(f"thz{i2}"), bf(f"zsb{i2}")], writes=[bf(f"t1b{i2}")])
                    fw.op("dve", lambda e: e.tensor_tensor(out=tho[i2][:, 0:n], in0=tho[i2][:, 0:n], in1=t1b[i2][:, 0:n], op=ALU.mult),
                          reads=[bf(f"tho{i2}"), bf(f"t1b{i2}")], writes=[bf(f"tho{i2}")])
                    fw.op("pool", lambda e: e.tensor_tensor(out=yT[:, yb, c0:c0 + n], in0=tho[i2][:, 0:n], in1=t1b[i2][:, 0:n], op=ALU.add),
                          reads=[bf(f"tho{i2}"), bf(f"t1b{i2}")], writes=[bf("yT")])
            if h < 3:
                prefetched_w[h + 1] = (load_w(2048 + 256 * (h + 1)), load_w(3072 + 256 * (h + 1)), load_w(5120 + 256 * (h + 1)))
            fw.op("pool", lambda e: e.memset(C_st[:], 0.0), writes=[bf("C_st")])
            fw.op("dve", lambda e: e.tensor_scalar(out=Wm[:], in0=cst[:, C_BDS:C_BDS + 16], scalar1=tokQ[:, SLAST, 8 + h:9 + h], scalar2=None,
                                                   op0=ALU.mult), reads=[bf("cst"), bf("tokQ")], writes=[bf("Wm")])
            P0a = PB[0]
            P2a = PB[0]
            P0b = PB[2]
            P2b = PB[2]
            P2c = PB[2]
            pk = psb(0)[:, 512:768]
            ph_p = psb(2)[:, 256:512].rearrange("p (d t) -> p d t", t=128)
            ph_s = psb(2)[:, 520:776].rearrange("p (d t) -> p d t", t=128)
            NP = SLAST

            pre_v = (h == 3)
            if pre_v:
                vall = arena[:, gate_tmp_off:gate_tmp_off + NCH * 129].bitcast(BF16).rearrange("p (c e) -> p c e", e=258)
                vallB = Buf("vall")
                for nm_ in ("tho0", "tho1", "thz0", "thz1", "t1b0", "t1b1", "zsb0", "zsb1"):
                    vallB.w.extend(B[nm_].w + B[nm_].r)
                fw.op("pool", lambda e: e.memset(vall[:, :, 256:258], 1.0), writes=[vallB])
                for ct_ in range(NCH):
                    c0_, L_ = CHUNKS[ct_]
                    bk_ = nbank()
                    for k in range(8):
                        fw.op("pe", lambda e, k=k: e.matmul(out=ps[0:L_, bk_, 0:256], lhsT=xnT[:, k, c0_:c0_ + L_], rhs=wv[:, k, :],
                                                            start=(k == 0), stop=(k == 7)),
                              reads=[bf("xnT"), wvb], writes=[PB[bk_]], inc=(k == 7))
                    fw.op("act", lambda e: e.activation(out=vall[0:L_, ct_, 0:256], in_=ps[0:L_, bk_, 0:256], func=AF.Copy),
                          reads=[PB[bk_]], writes=[vallB])
                load_wout()

            def slot_v(ct):
                if pre_v:
                    return vall[:, ct, :], vallB
                i = NV - 1 if ct == SLAST else ct % (NV - 1)
                return vaug[i], bf(f"vaug{i}")

            def slot_s(ct):
                i = NS_ - 1 if ct == SLAST else ct % (NS_ - 1)
                return sTm[i], bf(f"sTm{i}")

            def nbank_of(ct):
                return 1 if ct == SLAST else 4 + (ct % 2)

            def PE_A(ct):
                c0, L = CHUNKS[ct]
                for k in range(8):
                    if pre_v:
                        break
                    fw.op("pe", lambda e, k=k: e.matmul(out=ps[0:L, 0, 0:256], lhsT=xnT[:, k, c0:c0 + L], rhs=wv[:, k, :],
                                                        start=(k == 0), stop=(k == 7)),
                          reads=[bf("xnT"), wvb], writes=[P0a], inc=(k == 7))
                for dc in range(2):
                    fw.op("pe", lambda e, dc=dc: e.transpose(out=pk[0:L, dc * 128:(dc + 1) * 128], in_=kT[:, dc, c0:c0 + L], identity=ident[:, :]),
                          reads=[bf("kT"), bf("ident")], writes=[P2a], inc=(dc == 1))
                for dc in range(2):
                    fw.op("pe", lambda e, dc=dc: e.matmul(out=ps[0:L, 2, 0:L], lhsT=kT[:, dc, c0:c0 + L], rhs=qT[:, dc, c0:c0 + L],
                                                          start=(dc == 0), stop=(dc == 1)),
                          reads=[bf("kT"), bf("qT")], writes=[P0b], inc=(dc == 1))

            def EV_A(ct):
                c0, L = CHUNKS[ct]
                is_s = (ct == SLAST)
                va, vaB = slot_v(ct)
                if not pre_v:
                    fw.op("act", lambda e: e.activation(out=va[0:L, 0:256], in_=ps[0:L, 0, 0:256], func=AF.Copy), reads=[P0a], writes=[vaB])
                if not is_s:
                    kwt, kwB = kw[ct % 2], bf(f"kw{ct % 2}")
                    fw.op("act", lambda e: e.activation(out=kwt[0:L, :], in_=pk[0:L, 0:256], func=AF.Copy, scale=tokQ[0:L, ct, 8 + h:9 + h]),
                          reads=[P2a, bf("tokQ")], writes=[kwB])
                else:
                    fw.op("act", lambda e: e.activation(out=ktok_s[:, :], in_=pk[:, 0:256], func=AF.Copy), reads=[P2a], writes=[bf("ktok_s")])
                mcol = C_BD if is_s else C_CM
                sm_, smB = slot_s(ct)
                fw.op("dve", lambda e: e.scalar_tensor_tensor(out=sm_[0:L, 0:L], in0=ps[0:L, 2, 0:L], scalar=tokQ[0:L, ct, h:h + 1],
                                                              in1=cst[0:L, mcol:mcol + L], op0=ALU.mult, op1=ALU.mult),
                      reads=[P0b, bf("tokQ"), bf("cst")], writes=[smB])

            def PE_U(ct):
                c0, L = CHUNKS[ct]
                va, vaB = slot_v(ct)
                kwt, kwB = kw[ct % 2], bf(f"kw{ct % 2}")
                for dc in range(2):
                    fw.op("pe", lambda e, dc=dc: e.matmul(out=ps[:, 6 + dc, 0:257], lhsT=kwt[0:L, dc * 128:(dc + 1) * 128], rhs=va[0:L, 0:257],
                                                          start=True, stop=True),
                          reads=[kwB, vaB], writes=[PB[6 + dc]], inc=True)

            def ST(ct):
                fw.op("dve", lambda e: e.scalar_tensor_tensor(out=C_st[:], in0=C_st[:], scalar=dec_bc[:, h, ct:ct + 1], in1=ps[:, 6:8, 0:257],
                                                              op0=ALU.mult, op1=ALU.add),
                      reads=[bf("C_st"), bf("dec_bc"), PB[6], PB[7]], writes=[bf("C_st")])
                if ct < NP - 1:
                    cb_, cbB = C_bf[ct % NCB], bf(f"C_bf{ct % NCB}")
                    fw.op("dve", lambda e: e.tensor_copy(out=cb_[:, :, 0:257], in_=C_st[:]), reads=[bf("C_st")], writes=[cbB])
                else:
                    fw.dma("sp", lambda e: e.dma_start(out=C_prompt[h].rearrange("(dc p) e -> p dc e", p=128), in_=C_st[:, :, 0:256]),
                           "smallout", reads=[bf("C_st")])
                    fw.dma("sp", lambda e: e.dma_start(out=n_prompt[h].rearrange("(dc p o) -> p dc o", p=128, o=1), in_=C_st[:, :, 256:257], **NCD),
                           "smallout", reads=[bf("C_st")])

            def NR(ct):
                c0, L = CHUNKS[ct]
                nb_ = nbank_of(ct)
                va, vaB = slot_v(ct)
                sm_, smB = slot_s(ct)
                last_only = (ct == 0)
                stop_ = last_only
                fw.op("pe", lambda e: e.matmul(out=ps[0:L, nb_, 0:257], lhsT=sm_[0:L, 0:L], rhs=va[0:L, 0:257], start=True, stop=stop_),
                      reads=[smB, vaB], writes=[PB[nb_]], inc=True)
                if ct > 0 and ct != SLAST:
                    cb_, cbB = C_bf[(ct - 1) % NCB], bf(f"C_bf{(ct - 1) % NCB}")
                    for dc in range(2):
                        fw.op("pe", lambda e, dc=dc: e.matmul(out=ps[0:L, nb_, 0:257], lhsT=qT[:, dc, c0:c0 + L], rhs=cb_[:, dc, 0:257],
                                                              start=False, stop=(dc == 1)),
                              reads=[bf("qT"), cbB], writes=[PB[nb_]], inc=(dc == 1))

            def issue_loads(j):
                ci = j % NCF
                fw.dma("sp", lambda e: e.dma_start(out=Cf[ci][:], in_=sC[j, h].rearrange("(dc p) e -> p dc e", p=128)),
                       f"cf{ci}", writes=[bf(f"Cf{ci}")])

            def zq_refresh(half):
                zd_new = bass.AP(zq.tensor, zq.offset + 64 * half, [list(zq.ap[0]), [1024, 2], [136, 8], [1, 8]])
                zd_old = bass.AP(zq.tensor, zq.offset + 64 * (1 - half), [list(zq.ap[0]), [1024, 2], [136, 8], [1, 8]])
                fw.op("dve", lambda e: e.memset(zd_old, 0.0), writes=[bf("zq")])
                fw.op("dve", lambda e: e.tensor_copy(
                    out=zd_new, in_=qT[:, :, SOFF + 64 * half:SOFF + 64 * half + 64].rearrange("p d (b r) -> p d b r", r=8)),
                    reads=[bf("qT")], writes=[bf("zq")])

            def T0(j):
                ci, si_ = j % NCF, j % NCS
                fw.op("act", lambda e: e.activation(out=Csb[si_][:], in_=Cf[ci][:], func=AF.Copy), reads=[bf(f"Cf{ci}")], writes=[bf(f"Csb{si_}")])
                kj = 2 + (j % 2)
                fw.op("act", lambda e: e.activation(out=kw[kj][:, :], in_=ktok_s[:, :], func=AF.Copy, scale=Wm[:, j:j + 1]),
                      reads=[bf("ktok_s"), bf("Wm")], writes=[bf(f"kw{kj}")])

            def T1(j):
                va, vaB = slot_v(SLAST)
                si_ = j % NCS
                kj = 2 + (j % 2)
                jj = j % 8
                for dc in range(2):
                    fw.op("pe", lambda e, dc=dc: e.matmul(out=ps[:, 1, 0:256], lhsT=zq[:, dc, jj * 128:(jj + 1) * 128],
                                                          rhs=Csb[si_][:, dc, :], start=False, stop=False),
                          reads=[bf("zq"), bf(f"Csb{si_}")], writes=[PB[1]], inc=False)
                    lastmm = (j == NSEQ - 1 and dc == 1)
                    fw.op("pe", lambda e, dc=dc, lastmm=lastmm: e.matmul(
                        out=ps[:, 1, 256:257], lhsT=zq[:, dc, jj * 128:(jj + 1) * 128], rhs=nT_b[:, dc, 4 * j + h:4 * j + h + 1],
                        start=False, stop=lastmm), reads=[bf("zq"), bf("nT_b")], writes=[PB[1]], inc=True)
                for dc in range(2):
                    fw.op("pe", lambda e, dc=dc: e.matmul(out=ps[:, 3, dc * 256:(dc + 1) * 256], lhsT=kw[kj][:, dc * 128:(dc + 1) * 128],
                                                          rhs=va[:, 0:256], start=True, stop=True),
                          reads=[bf(f"kw{kj}"), vaB], writes=[PB[3]], inc=(dc == 1))
                for dc in range(2):
                    fw.op("pe", lambda e, dc=dc: e.matmul(out=ps[:, 2, 256 + dc:257 + dc], lhsT=kw[kj][:, dc * 128:(dc + 1) * 128],
                                                          rhs=va[:, 256:257], start=True, stop=True),
                          reads=[bf(f"kw{kj}"), vaB], writes=[P2c], inc=(dc == 1))

            def T2(j):
                ci = j % NCF
                fw.op("dve", lambda e: e.scalar_tensor_tensor(
                    out=nnT[:, :, 4 * j + h], in0=nT_f[:, :, 4 * j + h], scalar=dec_bc[:, h, 17 + j:18 + j],
                    in1=ps[:, 2, 256:258], op0=ALU.mult, op1=ALU.add),
                    reads=[bf("nT_f"), bf("dec_bc"), P2c], writes=[bf("nnT")])
                fw.op("dve", lambda e: e.scalar_tensor_tensor(
                    out=Cf[ci][:], in0=Cf[ci][:], scalar=dec_bc[:, h, 17 + j:18 + j], in1=ps[:, 3, :].rearrange("p (d e) -> p d e", e=256),
                    op0=ALU.mult, op1=ALU.add),
                    reads=[bf(f"Cf{ci}"), bf("dec_bc"), PB[3]], writes=[bf(f"Cf{ci}")])
                fw.dma("sp", lambda e: e.dma_start(out=C_sample[j, h].rearrange("(dc p) e -> p dc e", p=128), in_=Cf[ci][:]),
                       f"cf{ci}", reads=[bf(f"Cf{ci}")])

            hstate = {}

            def hr_slot(ct):
                i = NHR - 1 if ct == SLAST else ct % (NHR - 1)
                return hraw[i], bf(f"hraw{i}")

            def H1(ct):
                c0, L = CHUNKS[ct]
                bank = nbank_of(ct)
                si = cnt["st"] % 8
                cnt["st"] += 1
                stile, stb = stt[si], bf(f"stt{si}")
                hstate[ct] = (stile, stb)
                hr, hrB = hr_slot(ct)
                P_ = PB[bank]
                fw.op("pool", lambda e: e.memset(stile[:], 0.0), writes=[stb])
                fw.op("act", lambda e: e.activation(out=stile[0:L, 0:1], in_=ps[0:L, bank, 256:257], func=AF.Square), reads=[P_], writes=[stb])
                fw.op("act", lambda e: e.activation(out=hr[0:L, :], in_=ps[0:L, bank, 0:256], func=AF.Identity, scale=1.0 / 256,
                                                    accum_out=stile[0:L, 1:2]), reads=[P_, stb], writes=[stb, hrB])
                fw.op("act", lambda e: e.activation(out=junk[0:L, 0:256], in_=ps[0:L, bank, 0:256], func=AF.Square, scale=1.0 / 4096,
                                                    accum_out=stile[0:L, 2:3]), reads=[P_, stb], writes=[stb, bf("junk")])

            def H2(ct):
                c0, L = CHUNKS[ct]
                stile, stb = hstate[ct]
                fw.op("dve", lambda e: e.tensor_scalar(out=stile[0:L, 8:9], in0=stile[0:L, 1:2], scalar1=1.0 / 256, scalar2=None, op0=ALU.mult),
                      reads=[stb], writes=[stb])
                fw.op("dve", lambda e: e.tensor_tensor(out=stile[0:L, 3:4], in0=stile[0:L, 0:1], in1=tokQ[0:L, ct, 4 + h:5 + h], op=ALU.max),
                      reads=[stb, bf("tokQ")], writes=[stb])
                fw.op("dve", lambda e: e.scalar_tensor_tensor(out=stile[0:L, 4:5], in0=stile[0:L, 8:9], scalar=stile[0:L, 8:9], in1=stile[0:L, 2:3],
                                                              op0=ALU.mult, op1=ALU.subtract), reads=[stb], writes=[stb])
                fw.op("dve", lambda e: e.scalar_tensor_tensor(out=stile[0:L, 5:6], in0=stile[0:L, 3:4], scalar=EPS / 65536.0, in1=stile[0:L, 4:5],
                                                              op0=ALU.mult, op1=ALU.subtract), reads=[stb], writes=[stb])

            def H3(ct):
                c0, L = CHUNKS[ct]
                stile, stb = hstate[ct]
                fw.op("act", lambda e: e.activation(out=stile[0:L, 6:7], in_=stile[0:L, 5:6], func=AF.Ln), reads=[stb], writes=[stb])
                fw.op("act", lambda e: e.activation(out=stile[0:L, 7:8], in_=stile[0:L, 6:7], func=AF.Exp, scale=-0.5), reads=[stb], writes=[stb])

            def H4(ct):
                c0, L = CHUNKS[ct]
                stile, stb = hstate[ct]
                hr, hrB = hr_slot(ct)
                hi_ = (NH - 1) if ct == SLAST else ct % (NH - 1)
                hslot, hB = hn[hi_], bf(f"hn{hi_}")
                fw.op("dve", lambda e: e.tensor_scalar(out=hslot[0:L, :], in0=hr[0:L, :], scalar1=stile[0:L, 8:9], scalar2=stile[0:L, 7:8],
                                                       op0=ALU.subtract, op1=ALU.mult), reads=[hrB, stb], writes=[hB])

            def H5pe(ct):
                c0, L = CHUNKS[ct]
                ph = ph_s if ct == SLAST else ph_p
                hi_ = (NH - 1) if ct == SLAST else ct % (NH - 1)
                hslot, hB = hn[hi_], bf(f"hn{hi_}")
                for dc in range(2):
                    fw.op("pe", lambda e, dc=dc: e.transpose(out=ph[:, dc, 0:L], in_=hslot[0:L, dc * 128:(dc + 1) * 128], identity=ident[0:L, 0:L]),
                          reads=[hB, bf("ident")], writes=[P2b], inc=(dc == 1))

            def H5ev(ct):
                c0, L = CHUNKS[ct]
                ph = ph_s if ct == SLAST else ph_p
                yv = yT[:, 8 + 2 * h:10 + 2 * h, c0:c0 + L]
                fw.op("dve", lambda e: e.tensor_tensor(out=yv, in0=ph[:, :, 0:L], in1=yv, op=ALU.mult),
                      reads=[P2b, bf("yT")], writes=[bf("yT")])

            for j in range(3):
                issue_loads(j)
            PE_A(SLAST)
            EV_A(SLAST)
            va_s, vaB_s = slot_v(SLAST)
            sm_s, smB_s = slot_s(SLAST)
            fw.op("pe", lambda e: e.matmul(out=ps[:, 1, 0:257], lhsT=sm_s[:, :], rhs=va_s[:, 0:257], start=True, stop=False),
                  reads=[smB_s, vaB_s], writes=[PB[1]], inc=True)
            zq_refresh(0)
            NSTEP = NP + 9

            def step(T):
                def ok(c):
                    return 0 <= c < NP
                if ok(T - 3):
                    ST(T - 3)
                if ok(T - 1):
                    EV_A(T - 1)
                if ok(T - 3):
                    NR(T - 3)
                if ok(T - 2):
                    PE_U(T - 2)
                if 0 <= T + 2 < NSEQ and T + 2 >= 3:
                    issue_loads(T + 2)
                if 0 <= T - 1 < NSEQ:
                    T0(T - 1)
                if 0 <= T - 3 < NSEQ:
                    T2(T - 3)
                if 0 <= T - 2 < NSEQ:
                    T1(T - 2)
                TS = NSEQ + 2
                if ok(T - 5):
                    H2(T - 5)
                if T == TS + 1:
                    H2(SLAST)
                if ok(T - 7):
                    H4(T - 7)
                if T == TS + 3:
                    H4(SLAST)
                if ok(T - 6):
                    H3(T - 6)
                if T == TS + 2:
                    H3(SLAST)
                if ok(T - 4):
                    H1(T - 4)
                if T == TS:
                    H1(SLAST)
                if ok(T - 9):
                    H5ev(T - 9)
                if T == TS + 5:
                    H5ev(SLAST)
                if ok(T - 8):
                    H5pe(T - 8)
                if T == TS + 4:
                    H5pe(SLAST)
                if ok(T):
                    PE_A(T)
                if T - 2 == 7:
                    zq_refresh(1)

            NMAIN = NP + 4
            for T in range(NMAIN):
                step(T)
            return [lambda T=T: step(T) for T in range(NMAIN, NSTEP + 1)]

        rotqk = {"i": 0, "g": 0}
        drain_ = []
        for h_ in range(4):
            drain_ = head_program(h_, drain_)
        for f_ in drain_:
            f_()
        for dc in range(2):
            bank = 4 + dc
            fw.op("pe", lambda e, dc=dc, bank=bank: e.transpose(out=ps[0:64, bank, 0:128], in_=nnT[:, dc, :], identity=identf),
                  reads=[bf("nnT"), bf("cst")], writes=[PB[bank]])
            fw.op("act", lambda e, dc=dc, bank=bank: e.activation(out=nn_tok[:, dc * 128:(dc + 1) * 128], in_=ps[0:64, bank, 0:128], func=AF.Copy),
                  reads=[PB[bank]], writes=[bf("nn_tok")])
        fw.dma("sp", lambda e: e.dma_start(out=n_sample[:, :], in_=nn_tok[:, :]), "smallout", reads=[bf("nn_tok")])

        cv = Carve()
        NX5 = 6
        xts = [cv.get([128, D]) for _ in range(NX5)]
        rts = [cv.get([128, D]) for _ in range(3)]
        ots = [cv.get([128, D]) for _ in range(3)]
        nfw = cv.get([128, D])
        phase_bufs(["nfw"])
        fw.dma("sp", lambda e: e.dma_start(out=nfw[:], in_=normf_w.partition_broadcast(128)), "const", writes=[bf("nfw")])
        xtB = phase_bufs([f"xt{i}" for i in range(NX5)])
        rtB = phase_bufs(["rt0", "rt1", "rt2"])
        otB = phase_bufs(["ot0", "ot1", "ot2"])
        def p5_mm(j, qs):
            c0, n = NTILES[j]
            b0 = (j % 4) * 2
            for q4 in qs:
                for half in range(2):
                    for ic in range(4 * q4, 4 * q4 + 4):
                        fw.op("pe", lambda e, ic=ic, half=half: e.matmul(out=ps[0:n, b0 + half, :], lhsT=yT[:, ic, c0:c0 + n],
                                                                         rhs=wout[:, ic, half * 512:(half + 1) * 512],
                                                                         start=(ic == 0), stop=(ic == 15)),
                              reads=[bf("yT"), woQ[q4]], writes=[PB[b0 + half]], inc=(ic == 15 or ic % 4 == 3))

        def p5_epi(j):
            c0, n = NTILES[j]
            xt, xtb = xts[j % NX5], xtB[j % NX5]
            rt, rtb = rts[j % 3], rtB[j % 3]
            ot, otb = ots[j % 3], otB[j % 3]
            stile, stb = stt[j % 4], bf(f"stt{j % 4}")
            b0 = (j % 4) * 2
            jn = j + NX5 - 1
            if jn < 18:
                load_x_tile(jn, xts[jn % NX5], xtB[jn % NX5], f"xt{jn % NX5}")
            fw.op("dve", lambda e: e.tensor_tensor(out=rt[0:n, :].rearrange("p (a b) -> p a b", b=512),
                                                   in0=ps[0:n, b0:b0 + 2, :], in1=xt[0:n, :].rearrange("p (a b) -> p a b", b=512), op=ALU.add),
                  reads=[PB[b0], PB[b0 + 1], xtb], writes=[rtb])
            fw.op("pool", lambda e: e.memset(stile[:], 0.0), writes=[stb])
            fw.op("act", lambda e: e.activation(out=junk[0:n, :], in_=rt[0:n, :], func=AF.Square, accum_out=stile[0:n, 0:1]),
                  reads=[rtb, stb], writes=[bf("junk"), stb])
            rstd_from_ss(stile, stb, 0, 1, 1.0 / D)
            fw.op("dve", lambda e: e.scalar_tensor_tensor(out=ot[0:n, :], in0=rt[0:n, :], scalar=stile[0:n, 1:2],
                                                          in1=nfw[0:n, :], op0=ALU.mult, op1=ALU.mult),
                  reads=[rtb, stb, bf("nfw")], writes=[otb])
            for (a_, b_, dst, base) in [(16, TP, y_prompt, 16), (SOFF, TT, y_sample, SOFF)]:
                lo = max(c0, a_)
                hi = min(c0 + n, b_)
                if lo < hi:
                    fw.dma("sp", lambda e, lo=lo, hi=hi, dst=dst, base=base: e.dma_start(out=dst[lo - base:hi - base, :],
                                                                                        in_=ot[lo - c0:hi - c0, :]),
                           f"ot{j % 3}", reads=[otb])

        for j_ in range(NX5 - 1):
            load_x_tile(j_, xts[j_], xtB[j_], f"xt{j_}")
        for q4 in range(4):
            for j in range(4):
                p5_mm(j, [q4])
        for j in range(4):
            p5_epi(j)
        for j in range(4, 18):
            p5_mm(j, [0, 1, 2, 3])
            p5_epi(j)
        fw.finish("sp")
        fw.emit(block)
    return nc


_NC = None


def _prep(inputs, i):
    f = lambda a: np.ascontiguousarray(np.asarray(a, dtype=np.float32))
    sl = slice(NSEQ * i, NSEQ * (i + 1))
    return {
        "xp": f(inputs["x_prompt"][i]),
        "xs": f(inputs["x_sample"][sl]).reshape(NS, D),
        "spool": f(inputs["state_pool"][0, sl]),
        "sC": f(inputs["state_C"][0, sl]),
        "sn": f(inputs["state_n"][0, sl]).reshape(NSEQ * 4, 256),
        "sm": f(inputs["state_m"][0, sl]),
        "meta": f(inputs["meta_tokens"]),
        "norm1_w": f(inputs["norm1_w"][0]),
        "w_in": f(inputs["w_in"][0]),
        "b_if": f(inputs["b_if"][0]),
        "w_pool": f(inputs["w_pool"][0]),
        "pool_scale": f(inputs["pool_scale"][0]),
        "mhln_w": f(inputs["mhln_w"][0]).reshape(D),
        "w_out": f(inputs["w_out"][0]),
        "normf_w": f(inputs["normf_w"]),
        "consts": make_consts(),
    }


def _assemble(results):
    n = len(results)
    st = lambda k: np.stack([np.asarray(results[i][k], dtype=np.float32) for i in range(n)])
    y_prompt = st("y_prompt")
    y_sample = st("y_sample").reshape(n * NSEQ, 8, D)
    pool_prompt = st("pool_prompt")[None]
    C_prompt = st("C_prompt")[None]
    n_prompt = st("n_prompt")[None]
    m_prompt = st("m_prompt").reshape(n, 4)[None]
    pool_sample = st("pool_sample").reshape(n * NSEQ, 15, D)[None]
    C_sample = st("C_sample").reshape(n * NSEQ, 4, 256, 256)[None]
    n_sample = st("n_sample").reshape(n * NSEQ, 4, 256)[None]
    m_sample = st("m_sample").reshape(n * NSEQ, 4)[None]
    return (y_prompt, y_sample, pool_prompt, C_prompt, n_prompt, m_prompt, pool_sample, C_sample, n_sample, m_sample)


def kernel(**inputs):
    global _NC
    if _NC is None:
        _NC = build_nc()
    in_maps = [_prep(inputs, i) for i in range(8)]
    res = run_bass_kernel_spmd(_NC, in_maps, core_ids=list(range(8)))
    return _assemble(res.results)
```
